# Optimizing a Trainium2 kernel written in Bass

```python
import jax, jax.numpy as jnp
from jax import lax
import numpy as np

D_MODEL = 1024
BATCH = 4
SEQ = 8192
DEPTH = 2

GRID_W = 64
CTX_LEN = 256
MLA_HEADS = 8
MLA_NOPE = 64
MLA_ROPE = 32
MLA_V = 64
MLA_QK = MLA_NOPE + MLA_ROPE
MLA_WIDTH = MLA_HEADS * MLA_V
Q_RANK = 256
KV_RANK = 128
SC_WIDTH = 256
SC_K = 3
CF_WIDTH = 256
CF_K = 31
D_MIX = MLA_WIDTH + SC_WIDTH + CF_WIDTH
IN_SPLITS = (Q_RANK, KV_RANK, MLA_ROPE, MLA_WIDTH, SC_WIDTH, SC_WIDTH, SC_WIDTH, SC_WIDTH, 2 * CF_WIDTH, CF_WIDTH)
D_IN = Q_RANK + KV_RANK + MLA_ROPE + MLA_WIDTH + 4 * SC_WIDTH + 3 * CF_WIDTH
ROPE_THETA = 10000.0
Q_BLOCK = 128
EPS = 1e-6
LN_EPS = 1e-5

kernel_name = 'hybrid_mla_shortconv_conformer_dit'


def rms_norm(x, w):
    xf = x.astype(jnp.float32)
    y = xf * lax.rsqrt(jnp.mean(xf * xf, axis=-1, keepdims=True) + EPS)
    return (y * w.astype(jnp.float32)).astype(x.dtype)


def layer_norm(x, w, b):
    xf = x.astype(jnp.float32)
    mu = jnp.mean(xf, axis=-1, keepdims=True)
    var = jnp.mean(jnp.square(xf - mu), axis=-1, keepdims=True)
    y = (xf - mu) * lax.rsqrt(var + LN_EPS)
    return (y * w.astype(jnp.float32) + b.astype(jnp.float32)).astype(x.dtype)


def axial_rope_tables(T):
    rows = T // GRID_W
    pos_r = jnp.repeat(jnp.arange(rows, dtype=jnp.float32), GRID_W)
    pos_c = jnp.tile(jnp.arange(GRID_W, dtype=jnp.float32), rows)
    n_freq = MLA_ROPE // 4
    inv = ROPE_THETA ** (-jnp.arange(n_freq, dtype=jnp.float32) / n_freq)
    ang = jnp.concatenate([pos_r[:, None] * inv, pos_c[:, None] * inv], axis=-1)
    return jnp.cos(ang), jnp.sin(ang)


def apply_rope(x, cos, sin):
    half = x.shape[-1] // 2
    x1, x2 = x[..., :half], x[..., half:]
    cs = cos[None, :, None, :].astype(x.dtype)
    sn = sin[None, :, None, :].astype(x.dtype)
    return jnp.concatenate([x1 * cs - x2 * sn, x2 * cs + x1 * sn], axis=-1)


def dwconv(x, w):
    return lax.conv_general_dilated(x, w[:, None, :], window_strides=(1,), padding='SAME',
                                    dimension_numbers=('NWC', 'WIO', 'NWC'),
                                    feature_group_count=x.shape[-1])


def split_in(u):
    offsets = np.cumsum(IN_SPLITS)[:-1].tolist()
    return jnp.split(u, offsets, axis=-1)


def mla_q(cq, q_norm_w, w_uq, q_head_norm_w, cos, sin):
    B, T, _ = cq.shape
    q = (rms_norm(cq, q_norm_w) @ w_uq).reshape(B, T, MLA_HEADS, MLA_QK)
    q = rms_norm(q, q_head_norm_w)
    if cos is not None:
        q = jnp.concatenate([q[..., :MLA_NOPE], apply_rope(q[..., MLA_NOPE:], cos, sin)], axis=-1)
    return q


def mla_kv(ckv, kr, kv_norm_w, w_ukv, k_head_norm_w, cos, sin):
    B, T, _ = ckv.shape
    kv = (rms_norm(ckv, kv_norm_w) @ w_ukv).reshape(B, T, MLA_HEADS, MLA_NOPE + MLA_V)
    k_nope, v = kv[..., :MLA_NOPE], kv[..., MLA_NOPE:]
    k_rope = jnp.broadcast_to(kr[:, :, None, :], (B, T, MLA_HEADS, MLA_ROPE))
    k = rms_norm(jnp.concatenate([k_nope, k_rope], axis=-1), k_head_norm_w)
    if cos is not None:
        k = jnp.concatenate([k[..., :MLA_NOPE], apply_rope(k[..., MLA_NOPE:], cos, sin)], axis=-1)
    return k, v


def attend(q, k, v):
    B, T, H, Dq = q.shape
    nb = T // Q_BLOCK
    qb = q.reshape(B, nb, Q_BLOCK, H, Dq).transpose(1, 0, 2, 3, 4)
    sm_scale = MLA_QK ** -0.5

    def one_block(qi):
        s = jnp.einsum('bqhd,bkhd->bhqk', qi, k).astype(jnp.float32) * sm_scale
        p = jax.nn.softmax(s, axis=-1).astype(v.dtype)
        return jnp.einsum('bhqk,bkhd->bqhd', p, v)

    o = lax.map(one_block, qb)
    return o.transpose(1, 0, 2, 3, 4).reshape(B, T, H * MLA_V)


def mixer_branches(parts, attn, sc_conv_w, cf_conv_w, cf_conv_b, cf_ln_w, cf_ln_b, cf_pw_w, cf_pw_b):
    _, _, _, g_a, sc_in, sc_b, sc_c, g_b, cf_glu, g_c = parts
    y_a = attn * jax.nn.silu(g_a)
    y_b = sc_b * dwconv(sc_c * sc_in, sc_conv_w) * jax.nn.silu(g_b)
    a, g = jnp.split(cf_glu, 2, axis=-1)
    z = dwconv(a * jax.nn.sigmoid(g), cf_conv_w) + cf_conv_b
    z = jax.nn.silu(layer_norm(z, cf_ln_w, cf_ln_b))
    y_c = (z @ cf_pw_w + cf_pw_b) * jax.nn.silu(g_c)
    return jnp.concatenate([y_a, y_b, y_c], axis=-1)


def setup_inputs(seed: int = 0) -> dict:
    key = jax.random.key(seed)
    ks = jax.random.split(key, 24)
    f32 = jnp.float32
    nrm = lambda k, shape, s: jax.random.normal(k, shape, f32) * s
    return {
        'x': nrm(ks[0], (BATCH, SEQ, D_MODEL), 1.0),
        'c': nrm(ks[1], (BATCH, D_MODEL), 1.0),
        'ctx': nrm(ks[2], (BATCH, CTX_LEN, D_MODEL), 1.0),
        'c_ctx': nrm(ks[3], (D_MODEL,), 1.0),
        'norm_w': 1.0 + nrm(ks[4], (DEPTH, D_MODEL), 0.02),
        'w_mod': nrm(ks[5], (DEPTH, D_MODEL, 3 * D_MODEL), D_MODEL ** -0.5),
        'b_mod': nrm(ks[6], (DEPTH, 3 * D_MODEL), 0.02),
        'w_in': nrm(ks[7], (DEPTH, D_MODEL, D_IN), D_MODEL ** -0.5),
        'q_norm_w': 1.0 + nrm(ks[8], (DEPTH, Q_RANK), 0.02),
        'w_uq': nrm(ks[9], (DEPTH, Q_RANK, MLA_HEADS * MLA_QK), Q_RANK ** -0.5),
        'kv_norm_w': 1.0 + nrm(ks[10], (DEPTH, KV_RANK), 0.02),
        'w_ukv': nrm(ks[11], (DEPTH, KV_RANK, MLA_HEADS * (MLA_NOPE + MLA_V)), KV_RANK ** -0.5),
        'q_head_norm_w': 1.0 + nrm(ks[12], (DEPTH, MLA_QK), 0.02),
        'k_head_norm_w': 1.0 + nrm(ks[13], (DEPTH, MLA_QK), 0.02),
        'sc_conv_w': nrm(ks[14], (DEPTH, SC_K, SC_WIDTH), SC_K ** -0.5),
        'cf_conv_w': nrm(ks[15], (DEPTH, CF_K, CF_WIDTH), CF_K ** -0.5),
        'cf_conv_b': nrm(ks[16], (DEPTH, CF_WIDTH), 0.02),
        'cf_ln_w': 1.0 + nrm(ks[17], (DEPTH, CF_WIDTH), 0.02),
        'cf_ln_b': nrm(ks[18], (DEPTH, CF_WIDTH), 0.02),
        'cf_pw_w': nrm(ks[19], (DEPTH, CF_WIDTH, CF_WIDTH), CF_WIDTH ** -0.5),
        'cf_pw_b': nrm(ks[20], (DEPTH, CF_WIDTH), 0.02),
        'w_out': nrm(ks[21], (DEPTH, D_MIX, D_MODEL), D_MIX ** -0.5),
    }


def reference(x, c, ctx, c_ctx, norm_w, w_mod, b_mod, w_in, q_norm_w, w_uq, kv_norm_w, w_ukv,
              q_head_norm_w, k_head_norm_w, sc_conv_w, cf_conv_w, cf_conv_b, cf_ln_w, cf_ln_b,
              cf_pw_w, cf_pw_b, w_out):
    T = x.shape[1]
    cos, sin = axial_rope_tables(T)
    xc = ctx
    for l in range(DEPTH):
        last = l == DEPTH - 1
        shift, scale, gate = jnp.split(jax.nn.silu(c) @ w_mod[l] + b_mod[l], 3, axis=-1)
        shift_c, scale_c, gate_c = jnp.split(jax.nn.silu(c_ctx) @ w_mod[l] + b_mod[l], 3, axis=-1)
        h = rms_norm(x, norm_w[l]) * (1.0 + scale[:, None, :]) + shift[:, None, :]
        hc = rms_norm(xc, norm_w[l]) * (1.0 + scale_c) + shift_c

        parts = split_in(h @ w_in[l])
        q = mla_q(parts[0], q_norm_w[l], w_uq[l], q_head_norm_w[l], cos, sin)
        k, v = mla_kv(parts[1], parts[2], kv_norm_w[l], w_ukv[l], k_head_norm_w[l], cos, sin)

        if last:
            ckv_c, kr_c = jnp.split(hc @ w_in[l][:, Q_RANK:Q_RANK + KV_RANK + MLA_ROPE], [KV_RANK], axis=-1)
            parts_c = None
        else:
            parts_c = split_in(hc @ w_in[l])
            ckv_c, kr_c = parts_c[1], parts_c[2]
        kc, vc = mla_kv(ckv_c, kr_c, kv_norm_w[l], w_ukv[l], k_head_norm_w[l], None, None)

        attn = attend(q, jnp.concatenate([kc, k], axis=1), jnp.concatenate([vc, v], axis=1))
        y = mixer_branches(parts, attn, sc_conv_w[l], cf_conv_w[l], cf_conv_b[l], cf_ln_w[l],
                           cf_ln_b[l], cf_pw_w[l], cf_pw_b[l])
        x_new = x + gate[:, None, :] * (y @ w_out[l])

        if not last:
            qc = mla_q(parts_c[0], q_norm_w[l], w_uq[l], q_head_norm_w[l], None, None)
            attn_c = attend(qc, kc, vc)
            yc = mixer_branches(parts_c, attn_c, sc_conv_w[l], cf_conv_w[l], cf_conv_b[l], cf_ln_w[l],
                                cf_ln_b[l], cf_pw_w[l], cf_pw_b[l])
            xc = xc + gate_c * (yc @ w_out[l])
        x = x_new
    return x
```

```python
import numpy as np
from contextlib import ExitStack
import concourse.bass as bass
import concourse.mybir as mybir
from concourse.bass_utils import run_bass_kernel_spmd

F32 = mybir.dt.float32
BF16 = mybir.dt.bfloat16
AF = mybir.ActivationFunctionType
ALU = mybir.AluOpType
AX = mybir.AxisListType


class Ref:
    __slots__ = ("lane", "v")

    def __init__(self, lane, v=None):
        self.lane = lane
        self.v = v


class Buf:
    def __init__(self, t, name=""):
        self.t = t
        self.name = name
        self.last_w = None
        self.readers = {}


class Fw:
    NDMA = 8

    def __init__(self, nc, es):
        self.nc = nc
        self.es = es
        self.eng = {"pe": nc.tensor, "act": nc.scalar, "dve": nc.vector, "pool": nc.gpsimd, "sp": nc.sync}
        self.sem = {}
        self.cnt = {}
        self.inc = {}
        for n in ["pe", "act", "dve", "pool"]:
            self.sem[n] = es.enter_context(nc.semaphore("s_" + n))
            self.cnt[n] = 0
            self.inc[n] = 1
        for i in range(self.NDMA):
            n = "d%d" % i
            self.sem[n] = es.enter_context(nc.semaphore("s_" + n))
            self.cnt[n] = 0
            self.inc[n] = 16
        self.waited = {e: {} for e in self.eng}
        self.pending = {e: [] for e in self.eng}
        self.dma_rr = 0
        self.n_inst = 0

    def sb(self, name, shape, dt, es=None):
        self.uid = getattr(self, "uid", 0) + 1
        t = (es or self.es).enter_context(self.nc.sbuf_tensor("sb%d_%s" % (self.uid, name), shape, dt))
        return Buf(t, name)

    def ps(self, name, shape, dt, es=None):
        self.uid = getattr(self, "uid", 0) + 1
        t = (es or self.es).enter_context(self.nc.psum_tensor("ps%d_%s" % (self.uid, name), shape, dt))
        return Buf(t, name)

    def _wait(self, e, ref):
        if e == "pe" and ref.lane == "pe":
            return
        if ref.v is None:
            raise RuntimeError("dependency on unsignalled instruction (lane %s)" % ref.lane)
        if self.waited[e].get(ref.lane, 0) < ref.v:
            self.eng[e].wait_ge(self.sem[ref.lane], ref.v * self.inc[ref.lane])
            self.waited[e][ref.lane] = ref.v

    def _deps(self, e, reads, writes):
        need = {}
        for b in reads:
            r = b.last_w
            if r is not None:
                if r.lane not in need or (need[r.lane].v or 1 << 60) < (r.v or 1 << 60):
                    need[r.lane] = r
        for b in writes:
            cands = list(b.readers.values())
            if b.last_w is not None:
                cands.append(b.last_w)
            for r in cands:
                if r.lane not in need or (need[r.lane].v or 1 << 60) < (r.v or 1 << 60):
                    need[r.lane] = r
        for r in need.values():
            self._wait(e, r)

    def _record(self, ref, reads, writes):
        for b in reads:
            b.readers[ref.lane] = ref
        for b in writes:
            b.last_w = ref
            b.readers = {}

    def op(self, e, fn, reads, writes, signal=True):
        self._deps(e, reads, writes)
        inst = fn(self.eng[e])
        self.n_inst += 1
        if signal:
            self.cnt[e] += 1
            inst.then_inc(self.sem[e], 1)
            ref = Ref(e, self.cnt[e])
            for p in self.pending[e]:
                p.v = ref.v
            self.pending[e] = []
        else:
            ref = Ref(e, None)
            self.pending[e].append(ref)
        self._record(ref, reads, writes)
        return ref

    def act(self, out, in_, func, reads, writes, **kw):
        return self.op("act", lambda e: e.activation(out=out, in_=in_, func=func, **kw), reads, writes)

    def dma(self, q, out, in_, reads, writes, **kw):
        lane = "d%d" % self.dma_rr
        self.dma_rr = (self.dma_rr + 1) % self.NDMA
        if self.cnt[lane] > 0:
            self._wait(q, Ref(lane, self.cnt[lane]))
        self._deps(q, reads, writes)
        inst = self.eng[q].dma_start(out=out, in_=in_, **kw)
        self.n_inst += 1
        self.cnt[lane] += 1
        inst.then_inc(self.sem[lane], 16)
        ref = Ref(lane, self.cnt[lane])
        self._record(ref, reads, writes)
        return ref

    def barrier(self):
        for e in self.eng:
            if self.pending[e]:
                raise RuntimeError("barrier with pending unsignalled instr on " + e)
        for e in self.eng:
            for lane in self.sem:
                if self.cnt[lane] > 0:
                    self._wait(e, Ref(lane, self.cnt[lane]))


D = 1024
DIN = 2720
H = 8
CTX = 256
TM = 512
EPS = 1e-6
LN_EPS = 1e-5
SM_SCALE = 96 ** -0.5
OFF = dict(cq=0, ckv=256, kr=384, gA=416, scin=928, scB=1184, scC=1440, gB=1696, cfa=1952, cfg=2208, gC=2464)

MUL, ADD, SUB = ALU.mult, ALU.add, ALU.subtract


class Prog:
    def __init__(self, S, layers, n_layers_total=2, fused=False, stop=None):
        self.stop = stop
        self.S = S
        self.OWN = S // 2
        self.NKB = (CTX + S) // 128
        self.layers = layers
        self.fused = fused
        self.L = n_layers_total
        nc = self.nc = bass.Bass("TRN2", target_bir_lowering=False)
        self.es = ExitStack()
        self.fw = Fw(nc, self.es)
        self.din = {}
        self._declare_io()
        with self.es:
            self._alloc_persist()
            self._emit()

    def dram_in(self, name, shape, dt=F32):
        t = self.nc.dram_tensor(name, list(shape), dt, kind="ExternalInput").ap()
        self.din[name] = Buf(t, name)
        return self.din[name]

    def _declare_io(self):
        S, OWN, L = self.S, self.OWN, self.L
        nc = self.nc
        self.x_full = self.dram_in("x_full", [S, D])
        self.x_own = self.dram_in("x_own", [OWN, D])
        self.ctx_in = self.dram_in("ctx_in", [CTX, D])
        self.csil = self.dram_in("csil", [128, 8, 2])
        self.rope_k = self.dram_in("rope_k", [CTX + S, 32])
        self.rope_q = self.dram_in("rope_q", [CTX + OWN, 32])
        self.hmask = self.dram_in("hmask", [128, 2])
        self.ident_d = self.dram_in("ident", [128, 128])
        self.sel_d = self.dram_in("sel", [2, 256])
        self.norm_w = self.dram_in("norm_w", [L, D])
        self.w_mod = self.dram_in("w_mod", [L, D, 3 * D])
        self.b_mod = self.dram_in("b_mod", [L, 3 * D])
        self.w_in = self.dram_in("w_in", [L, D, DIN])
        self.qnw_col = self.dram_in("qnw_col", [L, 128, 2])
        self.w_uq = self.dram_in("w_uq", [L, 256, 768])
        self.kvnw_col = self.dram_in("kvnw_col", [L, 128, 1])
        self.w_ukv = self.dram_in("w_ukv", [L, 128, 1024])
        self.qhw = self.dram_in("qhw", [L, 96])
        self.khw = self.dram_in("khw", [L, 96])
        self.scw_col = self.dram_in("scw_col", [L, 128, 2, 3])
        self.cfw_col = self.dram_in("cfw_col", [L, 128, 2, 31])
        self.cfv_col = self.dram_in("cfv_col", [L, 128, 2, 4])
        self.cf_pw = self.dram_in("cf_pw", [L, 256, 256])
        self.w_out = self.dram_in("w_out", [L, D, D])
        self.y = Buf(nc.dram_tensor("y", [OWN, D], F32, kind="ExternalOutput").ap(), "y")
        self.yc = None
        if len(self.layers) == 1:
            self.yc = Buf(nc.dram_tensor("yc", [CTX, D], F32, kind="ExternalOutput").ap(), "yc")
        self.NCH = OWN // 512
        self.x1o_t = [nc.dram_tensor("x1o%d" % i, [512, D], F32) for i in range(self.NCH)]
        self.x1g_t = [nc.dram_tensor("x1g%d" % i, [1024, D], F32) for i in range(self.NCH)]
        self.x1o = [Buf(t.ap(), "x1o") for t in self.x1o_t]
        self.x1g = [Buf(t.ap(), "x1g") for t in self.x1g_t]
        self.xc1 = Buf(nc.dram_tensor("xc1", [CTX, D], F32).ap(), "xc1")
        self.w_in_bf = Buf(nc.dram_tensor("w_in_bf", [D, DIN], BF16, kind="Internal").ap(), "w_in_bf")
        self.w_out_bf = Buf(nc.dram_tensor("w_out_bf", [D, D], BF16, kind="Internal").ap(), "w_out_bf")

    def _alloc_persist(self):
        fw = self.fw
        sb, ps = fw.sb, fw.ps
        NKB = self.NKB
        NK = NKB * 128
        self.ident_f = sb("ident_f", [128, 128], F32)
        self.ident_b = sb("ident_b", [128, 128], BF16)
        self.ones_f = sb("ones_f", [128, 128], F32)
        self.sel = sb("sel", [2, 256], F32)
        self.hm = sb("hm", [128, 2], F32)
        self.G = [sb("G%d" % i, [128, D], F32) for i in range(2)]
        self.Sh = [sb("Sh%d" % i, [128, D], F32) for i in range(2)]
        self.GATE = [sb("GATE%d" % i, [128, D], F32) for i in range(2)]
        self.w_kv = sb("w_kv", [128, 8, 160], BF16)
        self.w_cq = sb("w_cq", [128, 8, 256], BF16)
        self.w_uqb = sb("w_uqb", [128, 2, 768], BF16)
        self.Wkn1 = sb("Wkn1", [128, 512], BF16)
        self.Wkn2 = sb("Wkn2", [128, 512], BF16)
        self.Wv = sb("Wv", [128, 512], BF16)
        self.pwb = sb("pwb", [128, 2, 256], BF16)
        self.wq_b = sb("wq_b", [128, 96], F32)
        self.wk_b = sb("wk_b", [128, 96], F32)
        self.scw = sb("scw", [128, 2, 3], F32)
        self.cfw = sb("cfw", [128, 2, 31], F32)
        self.cfv = sb("cfv", [128, 2, 4], F32)
        self.wch = [sb("wch%d" % i, [128, 8, 128], BF16) for i in range(3)]
        self.woch = [sb("woch%d" % i, [128, D], BF16) for i in range(3)]
        self.ckvT_t = sb("ckvT", [128, NK], BF16)
        self.ckvT = [Buf(None, "ckvT%d" % j) for j in range(NKB)]
        self.KB_t = sb("KB", [96, NK], BF16)
        self.KBr = [Buf(None, "KBr%d" % j) for j in range(NKB)]
        self.KBn = [Buf(None, "KBn%d" % j) for j in range((NK + 511) // 512)]
        self.ALPHA = sb("ALPHA", [128, NKB, 8], F32)
        self.ALb = [Buf(None, "AL%d" % j) for j in range(NKB)]
        self.RCKV = sb("RCKV", [128, NKB], F32)
        self.RCb = [Buf(None, "RC%d" % j) for j in range(NKB)]
        self.V2 = sb("V2", [128, NKB, 2, 65], BF16)
        self.pT2 = ps("pT2", [128, 1024], BF16)
        self.pT = self.pT2
        self.pA = ps("pA", [128, 512], F32)
        self.pQ = [ps("pQ%d" % i, [128, 512], F32) for i in range(2)]
        self.pS = [ps("pS%d" % i, [128, 512], F32) for i in range(2)]
        self.pO = [ps("pO%d" % i, [128, 512], F32) for i in range(2)]

    def _alloc_prep(self, es):
        sb = lambda n, sh, dt: self.fw.sb(n, sh, dt, es)
        self.csl = sb("csl", [128, 8, 2], F32)
        self.wmc = [sb("wmc%d" % i, [128, 3 * D], F32) for i in range(2)]
        self.modrow = sb("modrow", [2, 3 * D], F32)
        self.bmod2 = sb("bmod2", [2, 3 * D], F32)
        self.nw2 = sb("nw2", [2, D], F32)
        self.wstage = [sb("wstage%d" % i, [128, DIN], F32) for i in range(2)]
        self.wcast = [sb("wcast%d" % i, [128, DIN], BF16) for i in range(2)]
        self.qnw = sb("qnw", [128, 2], F32)
        self.kvnw = sb("kvnw", [128, 1], F32)
        self.tmpw = sb("tmpw", [128, 1024], F32)

    def _alloc_main(self, es):
        sb = lambda n, sh, dt: self.fw.sb(n, sh, dt, es)
        EXT = TM + 256
        self.xt = [sb("xt%d" % i, [128, D], F32) for i in range(2)]
        self.hb = [sb("hb%d" % i, [128, D], BF16) for i in range(2)]
        self.sm = [sb("sm%d" % i, [128, 16], F32) for i in range(2)]
        self.ckv_tok = sb("ckv_tok", [128, 128], BF16)
        self.k_tok = sb("k_tok", [128, 96], BF16)
        self.krw = sb("krw", [128, 32], F32)
        self.rtmp = sb("rtmp", [128, 4, 16], F32)
        self.cs = [sb("cs%d" % i, [128, 32], F32) for i in range(2)]
        self.sqn = sb("sqn", [128, 768], F32)
        self.ssn = sb("ssn", [128, 16], F32)
        self.hTm_t = sb("hTm", [128, 8, EXT], BF16)
        self.hTm = [Buf(None, "hTm%d" % i) for i in range(EXT // 128)]
        self.QT = sb("QT", [96, 8, TM], BF16)
        self.GA = sb("GA", [64, 8, TM], BF16)
        self.GB = sb("GB", [128, 2, TM], BF16)
        self.GC = sb("GC", [128, 2, TM], BF16)
        self.SCB = sb("SCB", [128, 2, TM], BF16)
        self.YB = sb("YB", [128, 2, TM], BF16)
        self.YC = sb("YC", [128, 2, TM], BF16)
        self.ZA = sb("ZA", [128, 2, TM], BF16)
        self.F1 = sb("F1", [128, 2, EXT], F32)
        self.F2 = sb("F2", [128, 2, EXT], F32)
        self.F3 = sb("F3", [128, 2, TM], F32)
        self.MU = sb("MU", [128, TM], F32)
        self.RS = sb("RS", [128, TM], F32)
        self.cqn_tok = sb("cqn_tok", [128, 256], BF16)
        self.cqnT = sb("cqnT", [128, 2, 128], BF16)
        self.qf = sb("qf", [128, 8, 96], F32)
        self.q_tok = sb("q_tok", [128, 8, 96], BF16)
        self.qtmp = sb("qtmp", [128, 4, 8, 16], F32)
        self.PT = [sb("PT%d" % i, [128, 512], BF16) for i in range(4)]
        self.ONR = sb("ONR", [128, 512], F32)
        self.rb = sb("rb", [64, 512], F32)

    def mm(self, out, lhsT, rhs, start, stop, reads, writes, signal=True):
        return self.fw.op("pe", lambda e: e.matmul(out, lhsT=lhsT, rhs=rhs, start=start, stop=stop), reads, writes, signal)

    def tp(self, out, in_, ident, reads, writes, signal=True):
        return self.fw.op("pe", lambda e: e.transpose(out=out, in_=in_, identity=ident), reads, writes, signal)

    def tt(self, eng, out, in0, in1, op, reads, writes):
        return self.fw.op(eng, lambda e: e.tensor_tensor(out=out, in0=in0, in1=in1, op=op), reads, writes)

    def ts(self, eng, out, in0, s1, op0, reads, writes, s2=None, op1=None):
        if op1 is None:
            return self.fw.op(eng, lambda e: e.tensor_scalar(out=out, in0=in0, scalar1=s1, scalar2=None, op0=op0), reads, writes)
        return self.fw.op(eng, lambda e: e.tensor_scalar(out=out, in0=in0, scalar1=s1, scalar2=s2, op0=op0, op1=op1), reads, writes)

    def stt(self, eng, out, in0, scalar, in1, op0, op1, reads, writes):
        return self.fw.op(eng, lambda e: e.scalar_tensor_tensor(out=out, in0=in0, scalar=scalar, in1=in1, op0=op0, op1=op1), reads, writes)

    def cp(self, eng, out, in_, reads, writes):
        if eng == "act":
            return self.fw.act(out, in_, AF.Copy, reads, writes)
        return self.fw.op(eng, lambda e: e.tensor_copy(out=out, in_=in_), reads, writes)

    def rsqrt(self, out, in_, scale, eps, buf_in, buf_out, power=-0.5):
        self.fw.act(out, in_, AF.Ln, [buf_in], [buf_out], scale=scale, bias=eps)
        self.fw.act(out, out, AF.Exp, [buf_out], [buf_out], scale=power)

    def _emit(self):
        fw = self.fw
        fw.dma("sp", self.ident_f.t[:], self.ident_d.t[:, :], [], [self.ident_f])
        fw.dma("sp", self.sel.t[:], self.sel_d.t[:, :], [], [self.sel])
        fw.dma("sp", self.hm.t[:], self.hmask.t[:, :], [], [self.hm])
        self.cp("dve", self.ident_b.t[:], self.ident_f.t[:], [self.ident_f], [self.ident_b])
        fw.op("pool", lambda e: e.memset(self.ones_f.t[:], 1.0), [], [self.ones_f])
        self.V2b = Buf(None, "V2b")
        fw.op("pool", lambda e: e.memset(self.V2.t[:, :, :, 64:65], 1.0), [], [self.V2b])
        self.xi = 0
        for li, l in enumerate(self.layers):
            last = (l == self.L - 1)
            with ExitStack() as es:
                self._alloc_prep(es)
                self._emit_prep(l)
                fw.barrier()
            if self.stop == "prep":
                return
            first = (li == 0)
            OWN = self.OWN
            if first:
                xfull = lambda t0: (self.x_full.t[t0:t0 + 128, :], self.x_full)
                xown = lambda t0: (self.x_own.t[t0:t0 + 128, :], self.x_own)
                cx = lambda t0: (self.ctx_in.t[t0:t0 + 128, :], self.ctx_in)
            else:
                def xfull(t0):
                    half, o = t0 // OWN, t0 % OWN
                    ch, r = o // 512, half * 512 + o % 512
                    return (self.x1g[ch].t[r:r + 128, :], self.x1g[ch])
                xown = lambda t0: (self.x1o[t0 // 512].t[t0 % 512:t0 % 512 + 128, :], self.x1o[t0 // 512])
                cx = lambda t0: (self.xc1.t[t0:t0 + 128, :], self.xc1)
            if last or len(self.layers) == 1:
                yd = lambda t0: (self.y.t[t0:t0 + 128, :], self.y)
                ycd = lambda t0: (self.yc.t[t0:t0 + 128, :], self.yc)
            else:
                yd = lambda t0: (self.x1o[t0 // 512].t[t0 % 512:t0 % 512 + 128, :], self.x1o[t0 // 512])
                ycd = lambda t0: (self.xc1.t[t0:t0 + 128, :], self.xc1)
            with ExitStack() as es:
                self._alloc_main(es)
                self._emit_main(l, last, xfull, xown, cx, yd, ycd)
                fw.barrier()
            if len(self.layers) > 1 and not last:
                fw.sem["cc"] = self.es.enter_context(self.nc.semaphore("s_cc"))
                fw.inc["cc"] = 1
                for i in range(self.NCH):
                    inst = self.nc.gpsimd.collective_compute(
                        "AllGather", ALU.bypass, replica_groups=[[0, 1], [2, 3], [4, 5], [6, 7]],
                        ins=[self.x1o_t[i].ap().opt()], outs=[self.x1g_t[i].ap().opt()])
                    inst.then_inc(fw.sem["cc"])
                    self.x1g[i].last_w = Ref("cc", i + 1)
                    self.x1g[i].readers = {}
                fw.cnt["cc"] = self.NCH
                fw.barrier()

    def _emit_prep(self, l):
        fw = self.fw
        banks = [self.pA, self.pQ[0], self.pQ[1], self.pS[0], self.pS[1], self.pO[0]]
        csl = self.csl
        fw.dma("sp", csl.t[:], self.csil.t[:, :, :], [], [csl])
        fw.act(csl.t[:], csl.t[:], AF.Silu, [csl], [csl])
        for r in range(2):
            fw.dma("sp", self.bmod2.t[r:r + 1, :], self.b_mod.t[l:l + 1, :], [], [self.bmod2])
            fw.dma("sp", self.nw2.t[r:r + 1, :], self.norm_w.t[l:l + 1, :], [], [self.nw2])
        for k in range(8):
            wm = self.wmc[k % 2]
            fw.dma("sp", wm.t[:], self.w_mod.t[l, k * 128:(k + 1) * 128, :], [], [wm])
            for j in range(6):
                self.mm(banks[j].t[0:2, :], csl.t[:, k, :], wm.t[:, j * 512:(j + 1) * 512], k == 0, k == 7,
                        [csl, wm], [banks[j]], signal=(j == 5 or k == 7))
        mr = self.modrow
        for j in range(6):
            self.tt("dve", mr.t[:, j * 512:(j + 1) * 512], banks[j].t[0:2, :], self.bmod2.t[:, j * 512:(j + 1) * 512], ADD,
                    [banks[j], self.bmod2], [mr])
        self.stt("dve", mr.t[:, D:2 * D], mr.t[:, D:2 * D], 1.0, self.nw2.t[:, :], ADD, MUL, [mr, self.nw2], [mr])
        bi = 0
        for v in range(2):
            for dst, c0 in ((self.Sh[v], 0), (self.G[v], D), (self.GATE[v], 2 * D)):
                for hf in range(2):
                    bk = banks[bi % 6]
                    bi += 1
                    self.mm(bk.t[:, :], self.sel.t[0:2, v * 128:(v + 1) * 128], mr.t[0:2, c0 + hf * 512:c0 + (hf + 1) * 512],
                            True, True, [self.sel, mr], [bk])
                    self.cp("act", dst.t[:, hf * 512:(hf + 1) * 512], bk.t[:, :], [bk], [dst])
        for k in range(8):
            st, wc = self.wstage[k % 2], self.wcast[k % 2]
            fw.dma("sp", st.t[:], self.w_in.t[l, k * 128:(k + 1) * 128, :], [], [st])
            self.cp("dve", wc.t[:, 0:1360], st.t[:, 0:1360], [st], [wc])
            self.cp("pool", wc.t[:, 1360:DIN], st.t[:, 1360:DIN], [st], [wc])
            fw.dma("sp", self.w_in_bf.t[k * 128:(k + 1) * 128, :], wc.t[:], [wc], [self.w_in_bf])
            self.cp("pool", self.w_kv.t[:, k, :], wc.t[:, 256:416], [wc], [self.w_kv])
            self.cp("pool", self.w_cq.t[:, k, :], wc.t[:, 0:256], [wc], [self.w_cq])
        for k in range(8):
            st, wc = self.wstage[k % 2], self.wcast[k % 2]
            fw.dma("sp", st.t[:, 0:D], self.w_out.t[l, k * 128:(k + 1) * 128, :], [], [st])
            self.cp("dve", wc.t[:, 0:D], st.t[:, 0:D], [st], [wc])
            fw.dma("sp", self.w_out_bf.t[k * 128:(k + 1) * 128, :], wc.t[:, 0:D], [wc], [self.w_out_bf])
        fw.dma("sp", self.qnw.t[:], self.qnw_col.t[l, :, :], [], [self.qnw])
        fw.dma("sp", self.kvnw.t[:], self.kvnw_col.t[l, :, :], [], [self.kvnw])
        for c in range(2):
            st = self.wstage[c]
            fw.dma("sp", st.t[:, 0:768], self.w_uq.t[l, c * 128:(c + 1) * 128, :], [], [st])
            self.ts("dve", self.w_uqb.t[:, c, :], st.t[:, 0:768], self.qnw.t[:, c:c + 1], MUL, [st, self.qnw], [self.w_uqb])
        fw.dma("sp", self.wk_b.t[:], self.khw.t[l, :].partition_broadcast(128), [], [self.wk_b])
        fw.dma("sp", self.wq_b.t[:], self.qhw.t[l, :].partition_broadcast(128), [], [self.wq_b])
        tw = self.tmpw
        fw.dma("sp", tw.t[:], self.w_ukv.t[l, :, :], [], [tw])
        twv = tw.t[:].rearrange("p (h t d) -> p h t d", h=8, t=2)
        v3 = lambda b: b.t[:].rearrange("p (h d) -> p h d", h=8)
        self.ts("dve", v3(self.Wkn1), twv[:, :, 0, :], self.kvnw.t[:, 0:1], MUL, [tw, self.kvnw], [self.Wkn1])
        self.ts("dve", v3(self.Wv), twv[:, :, 1, :], self.kvnw.t[:, 0:1], MUL, [tw, self.kvnw], [self.Wv])
        self.stt("dve", v3(self.Wkn2), twv[:, :, 0, :], self.kvnw.t[:, 0:1],
                 self.wk_b.t[:, 0:64].unsqueeze(1).to_broadcast([128, 8, 64]), MUL, MUL, [tw, self.kvnw, self.wk_b], [self.Wkn2])
        for c in range(2):
            st = self.wstage[c]
            fw.dma("sp", st.t[:, 0:256], self.cf_pw.t[l, c * 128:(c + 1) * 128, :], [], [st])
            self.cp("dve", self.pwb.t[:, c, :], st.t[:, 0:256], [st], [self.pwb])
        fw.dma("sp", self.scw.t[:], self.scw_col.t[l, :, :, :], [], [self.scw])
        fw.dma("sp", self.cfw.t[:], self.cfw_col.t[l, :, :, :], [], [self.cfw])
        fw.dma("sp", self.cfv.t[:], self.cfv_col.t[l, :, :, :], [], [self.cfv])

    def x_to_hT(self, src_ap, src_buf, v, mask_ap, dst_buf, dst_view):
        fw = self.fw
        i = self.xi % 2
        self.xi += 1
        xt, hb, sm = self.xt[i], self.hb[i], self.sm[i]
        fw.dma("sp", xt.t[:], src_ap, [src_buf], [xt])
        fw.act(hb.t[:], xt.t[:], AF.Square, [xt], [hb, sm], accum_out=sm.t[:, 0:1])
        self.rsqrt(sm.t[:, 1:2], sm.t[:, 0:1], 1.0 / D, EPS, sm, sm)
        self.stt("dve", xt.t[:], xt.t[:], sm.t[:, 1:2], self.G[v].t[:], MUL, MUL, [xt, sm, self.G[v]], [xt])
        self.tt("dve", hb.t[:], xt.t[:], self.Sh[v].t[:], ADD, [xt, self.Sh[v]], [hb])
        if mask_ap is not None:
            self.ts("dve", hb.t[:], hb.t[:], mask_ap, MUL, [hb, self.hm], [hb])
        for k in range(8):
            self.tp(self.pT.t[:, k * 128:(k + 1) * 128], hb.t[:, k * 128:(k + 1) * 128], self.ident_b.t[:],
                    [hb, self.ident_b], [self.pT], signal=(k == 7))
        self.cp("act", dst_view, self.pT.t[:].rearrange("p (k t) -> p k t", k=8), [self.pT], [dst_buf])

    def phase_k_block(self, j, src_ap, src_buf, v):
        fw = self.fw
        slot = j % len(self.hTm)
        hTb = self.hTm[slot]
        hT = self.hTm_t.t[:, :, slot * 128:(slot + 1) * 128]
        self.x_to_hT(src_ap, src_buf, v, None, hTb, hT)
        if self.stop == "k_x2h":
            return
        cs = self.cs[j % 2]
        fw.dma("sp", cs.t[:], self.rope_k.t[j * 128:(j + 1) * 128, :], [], [cs])
        sm = self.sm[j % 2]
        pA = self.pA
        for k in range(8):
            self.mm(pA.t[:, 0:160], hT[:, k, :], self.w_kv.t[:, k, :], k == 0, k == 7, [hTb, self.w_kv], [pA], signal=(k == 7))
        if self.stop == "k_mm":
            return
        fw.act(self.sqn.t[:, 0:160], pA.t[:, 0:160], AF.Square, [pA], [self.sqn])
        fw.op("dve", lambda e: e.tensor_reduce(out=sm.t[:, 2:3], in_=self.sqn.t[:, 0:128], axis=AX.X, op=ADD), [self.sqn], [sm])
        fw.op("dve", lambda e: e.tensor_reduce(out=sm.t[:, 3:4], in_=self.sqn.t[:, 128:160], axis=AX.X, op=ADD), [self.sqn], [sm])
        self.cp("dve", self.ckv_tok.t[:], pA.t[:, 0:128], [pA], [self.ckv_tok])
        if self.stop == "k_sq":
            return
        krw, rt = self.krw, self.rtmp
        self.tt("dve", krw.t[:], pA.t[:, 128:160], self.wk_b.t[:, 64:96], MUL, [pA, self.wk_b], [krw])
        x1, x2, cos, sin = krw.t[:, 0:16], krw.t[:, 16:32], cs.t[:, 0:16], cs.t[:, 16:32]
        self.tt("dve", rt.t[:, 0, :], x1, cos, MUL, [krw, cs], [rt])
        self.tt("dve", rt.t[:, 1, :], x2, sin, MUL, [krw, cs], [rt])
        self.tt("dve", rt.t[:, 2, :], x2, cos, MUL, [krw, cs], [rt])
        self.tt("dve", rt.t[:, 3, :], x1, sin, MUL, [krw, cs], [rt])
        self.tt("dve", krw.t[:, 0:16], rt.t[:, 0, :], rt.t[:, 1, :], SUB, [rt], [krw])
        self.tt("dve", krw.t[:, 16:32], rt.t[:, 2, :], rt.t[:, 3, :], ADD, [rt], [krw])
        if self.stop == "k_rope":
            return
        fw.act(sm.t[:, 4:5], sm.t[:, 2:3], AF.Ln, [sm], [sm], scale=1.0 / 128, bias=EPS)
        fw.act(sm.t[:, 5:6], sm.t[:, 4:5], AF.Exp, [sm], [sm], scale=0.5)
        fw.act(self.RCKV.t[:, j:j + 1], sm.t[:, 4:5], AF.Exp, [sm], [self.RCb[j]], scale=-0.5)
        self.ts("dve", self.k_tok.t[:, 64:96], krw.t[:], sm.t[:, 5:6], MUL, [krw, sm], [self.k_tok])
        self.tp(self.pT2.t[:, 0:128], self.ckv_tok.t[:], self.ident_b.t[:], [self.ckv_tok, self.ident_b], [self.pT2], signal=False)
        self.tp(self.pT2.t[0:96, 128:256], self.k_tok.t[:], self.ident_b.t[:], [self.k_tok, self.ident_b], [self.pT2])
        self.cp("dve", self.ckvT_t.t[:, j * 128:(j + 1) * 128], self.pT2.t[:, 0:128], [self.pT2], [self.ckvT[j]])
        self.cp("act", self.KB_t.t[64:96, j * 128:(j + 1) * 128], self.pT2.t[64:96, 128:256], [self.pT2], [self.KBr[j]])
        if self.stop == "k_tp":
            return
        pq = self.pQ[j % 2]
        self.mm(pq.t[:, :], self.ckvT_t.t[:, j * 128:(j + 1) * 128], self.Wkn1.t[:, :], True, True, [self.ckvT[j], self.Wkn1], [pq])
        fw.act(self.sqn.t[:, 0:512], pq.t[:, :], AF.Square, [pq], [self.sqn])
        fw.op("dve", lambda e: e.tensor_reduce(out=self.ssn.t[:, 0:8], in_=self.sqn.t[:, 0:512].rearrange("p (h d) -> p h d", h=8),
                                               axis=AX.X, op=ADD), [self.sqn], [self.ssn])
        self.tt("dve", sm.t[:, 6:7], self.RCKV.t[:, j:j + 1], self.RCKV.t[:, j:j + 1], MUL, [self.RCb[j]], [sm])
        self.ts("dve", self.ssn.t[:, 0:8], self.ssn.t[:, 0:8], sm.t[:, 6:7], MUL, [self.ssn, sm], [self.ssn], s2=sm.t[:, 3:4], op1=ADD)
        self.rsqrt(self.ssn.t[:, 8:16], self.ssn.t[:, 0:8], 1.0 / 96, EPS, self.ssn, self.ssn)
        self.ts("dve", self.ALPHA.t[:, j, :], self.ssn.t[:, 8:16], self.RCKV.t[:, j:j + 1], MUL, [self.ssn, self.RCb[j]], [self.ALb[j]],
                s2=SM_SCALE, op1=MUL)

    def macro(self, ext_srcs, q_rope_row0, key_blocks, v, x_res, y_dst):
        fw = self.fw
        nblk = len(ext_srcs) - 2
        ntok = nblk * 128
        ext = ntok + 256
        hTt = self.hTm_t.t
        for e, s in enumerate(ext_srcs):
            view = hTt[:, :, e * 128:(e + 1) * 128]
            if s is None:
                fw.op("pool", lambda en: en.memset(view, 0.0), [], [self.hTm[e]])
            else:
                self.x_to_hT(s[0], s[1], v, s[2], self.hTm[e], view)
        hT_all = self.hTm[:len(ext_srcs)]
        hT_c = self.hTm[1:1 + nblk]
        QTb = Buf(None, "QTb")
        self.QTb = QTb
        for b in range(nblk):
            e = b + 1
            hT = hTt[:, :, e * 128:(e + 1) * 128]
            sm = self.sm[b % 2]
            cs = self.cs[b % 2]
            fw.dma("sp", cs.t[:], self.rope_q.t[q_rope_row0 + b * 128:q_rope_row0 + (b + 1) * 128, :], [], [cs])
            pA = self.pA
            for k in range(8):
                self.mm(pA.t[:, 0:256], hT[:, k, :], self.w_cq.t[:, k, :], k == 0, k == 7, [self.hTm[e], self.w_cq], [pA], signal=(k == 7))
            fw.act(self.sqn.t[:, 0:256], pA.t[:, 0:256], AF.Square, [pA], [self.sqn, sm], accum_out=sm.t[:, 8:9])
            self.rsqrt(sm.t[:, 9:10], sm.t[:, 8:9], 1.0 / 256, EPS, sm, sm)
            fw.act(self.cqn_tok.t[:], pA.t[:, 0:256], AF.Copy, [pA, sm], [self.cqn_tok], scale=sm.t[:, 9:10])
            for c in range(2):
                self.tp(self.pT2.t[:, c * 128:(c + 1) * 128], self.cqn_tok.t[:, c * 128:(c + 1) * 128], self.ident_b.t[:],
                        [self.cqn_tok, self.ident_b], [self.pT2], signal=(c == 1))
            self.cp("dve", self.cqnT.t[:].rearrange("p c t -> p (c t)"), self.pT2.t[:, 0:256], [self.pT2], [self.cqnT])
            for (bk, c0, c1) in ((self.pQ[0], 0, 480), (self.pQ[1], 480, 768)):
                for c in range(2):
                    self.mm(bk.t[:, 0:c1 - c0], self.cqnT.t[:, c, :], self.w_uqb.t[:, c, c0:c1], c == 0, c == 1,
                            [self.cqnT, self.w_uqb], [bk], signal=(c == 1))
                fw.act(self.sqn.t[:, c0:c1], bk.t[:, 0:c1 - c0], AF.Square, [bk], [self.sqn])
            fw.op("dve", lambda en: en.tensor_reduce(out=self.ssn.t[:, 0:8], in_=self.sqn.t[:, 0:768].rearrange("p (h d) -> p h d", h=8),
                                                     axis=AX.X, op=ADD), [self.sqn], [self.ssn])
            self.rsqrt(self.ssn.t[:, 8:16], self.ssn.t[:, 0:8], 1.0 / 96, EPS, self.ssn, self.ssn)
            qf = self.qf
            self.tt("dve", qf.t[:, 0:5, :], self.pQ[0].t[:, 0:480].rearrange("p (h d) -> p h d", h=5),
                    self.ssn.t[:, 8:13].unsqueeze(2).to_broadcast([128, 5, 96]), MUL, [self.pQ[0], self.ssn], [qf])
            self.tt("dve", qf.t[:, 5:8, :], self.pQ[1].t[:, 0:288].rearrange("p (h d) -> p h d", h=3),
                    self.ssn.t[:, 13:16].unsqueeze(2).to_broadcast([128, 3, 96]), MUL, [self.pQ[1], self.ssn], [qf])
            self.tt("dve", qf.t[:, :, :], qf.t[:, :, :], self.wq_b.t[:, :].unsqueeze(1).to_broadcast([128, 8, 96]), MUL, [qf, self.wq_b], [qf])
            qt_, tmp = self.q_tok, self.qtmp
            self.cp("pool", qt_.t[:, :, 0:64], qf.t[:, :, 0:64], [qf], [qt_])
            x1, x2 = qf.t[:, :, 64:80], qf.t[:, :, 80:96]
            cos = cs.t[:, 0:16].unsqueeze(1).to_broadcast([128, 8, 16])
            sin = cs.t[:, 16:32].unsqueeze(1).to_broadcast([128, 8, 16])
            self.tt("dve", tmp.t[:, 0, :, :], x1, cos, MUL, [qf, cs], [tmp])
            self.tt("dve", tmp.t[:, 1, :, :], x2, sin, MUL, [qf, cs], [tmp])
            self.tt("dve", tmp.t[:, 2, :, :], x2, cos, MUL, [qf, cs], [tmp])
            self.tt("dve", tmp.t[:, 3, :, :], x1, sin, MUL, [qf, cs], [tmp])
            self.tt("dve", qt_.t[:, :, 64:80], tmp.t[:, 0, :, :], tmp.t[:, 1, :, :], SUB, [tmp], [qt_])
            self.tt("dve", qt_.t[:, :, 80:96], tmp.t[:, 2, :, :], tmp.t[:, 3, :, :], ADD, [tmp], [qt_])
            for h in range(H):
                self.tp(self.pT2.t[0:96, h * 128:(h + 1) * 128], qt_.t[:, h, :], self.ident_b.t[:], [qt_, self.ident_b], [self.pT2],
                        signal=(h == H - 1))
            self.cp("act", self.QT.t[:, :, b * 128:(b + 1) * 128], self.pT2.t[0:96, :].rearrange("p (h t) -> p h t", h=8), [self.pT2], [QTb])

        if self.stop == "macro_b":
            return
        banks = [self.pQ[0], self.pQ[1], self.pS[0], self.pS[1]]
        self._bk = getattr(self, "_bk", 0)
        self._wc = getattr(self, "_wc", 0)
        w_in_v = self.w_in_bf.t.rearrange("(k p) n -> p k n", p=128)

        def load_chunk(c0):
            wch = self.wch[self._wc % 3]
            self._wc += 1
            fw.dma("sp", wch.t[:], w_in_v[:, :, c0:c0 + 128], [self.w_in_bf], [wch])
            return wch

        def proj(wch, m0, m1, t0, tw, hbufs):
            bk = banks[self._bk % 4]
            self._bk += 1
            for k in range(8):
                self.mm(bk.t[0:m1 - m0, 0:tw], wch.t[:, k, m0:m1], hTt[:, k, t0:t0 + tw], k == 0, k == 7, [wch] + hbufs, [bk], signal=(k == 7))
            return bk

        ext_tiles = [(t0, min(512, ext - t0)) for t0 in range(0, ext, 512)]
        F1, F2, F3 = self.F1, self.F2, self.F3
        for name in ("scin", "scC", "cfg", "cfa"):
            for c in range(2):
                wch = load_chunk(OFF[name] + c * 128)
                for (t0, tw) in ext_tiles:
                    bk = proj(wch, 0, 128, t0, tw, hT_all)
                    if name == "scin":
                        self.cp("act", F1.t[:, c, t0:t0 + tw], bk.t[:, 0:tw], [bk], [F1])
                    elif name == "scC":
                        self.tt("dve", F1.t[:, c, t0:t0 + tw], bk.t[:, 0:tw], F1.t[:, c, t0:t0 + tw], MUL, [bk, F1], [F1])
                    elif name == "cfg":
                        fw.act(F2.t[:, c, t0:t0 + tw], bk.t[:, 0:tw], AF.Sigmoid, [bk], [F2])
                    else:
                        self.tt("dve", F2.t[:, c, t0:t0 + tw], bk.t[:, 0:tw], F2.t[:, c, t0:t0 + tw], MUL, [bk, F2], [F2])
        for name, dst, fn in (("scB", self.SCB, None), ("gB", self.GB, AF.Silu), ("gC", self.GC, AF.Silu)):
            for c in range(2):
                wch = load_chunk(OFF[name] + c * 128)
                bk = proj(wch, 0, 128, 128, ntok, hT_c)
                if fn is None:
                    self.cp("dve", dst.t[:, c, 0:ntok], bk.t[:, 0:ntok], [bk], [dst])
                else:
                    fw.act(dst.t[:, c, 0:ntok], bk.t[:, 0:ntok], fn, [bk], [dst])
        GAb = [Buf(None, "GA%d" % h) for h in range(H)]
        for c in range(4):
            wch = load_chunk(OFF["gA"] + c * 128)
            for hh in range(2):
                h = 2 * c + hh
                bk = proj(wch, hh * 64, hh * 64 + 64, 128, ntok, hT_c)
                fw.act(self.GA.t[0:64, h, 0:ntok], bk.t[0:64, 0:ntok], AF.Silu, [bk], [GAb[h]])

        accs = [self.MU, self.RS]
        ptmp = self.ONR

        def pool_mac(acc, a, in_view, w_ap, first, in_buf):
            if first:
                self.ts("pool", a, in_view, w_ap, MUL, [in_buf, self.scw, self.cfw], [acc])
            else:
                self.ts("pool", ptmp.t[:, 0:ntok], in_view, w_ap, MUL, [in_buf, self.scw, self.cfw], [ptmp])
                self.tt("pool", a, a, ptmp.t[:, 0:ntok], ADD, [acc, ptmp], [acc])

        for c in range(2):
            acc = accs[c]
            a = acc.t[:, 0:ntok]
            for j in range(3):
                pool_mac(acc, a, F1.t[:, c, 127 + j:127 + j + ntok], self.scw.t[:, c, j:j + 1], j == 0, F1)
            self.tt("pool", a, a, self.SCB.t[:, c, 0:ntok], MUL, [acc, self.SCB], [acc])
            self.tt("pool", self.YB.t[:, c, 0:ntok], a, self.GB.t[:, c, 0:ntok], MUL, [acc, self.GB], [self.YB])

        NDV = 21
        for c in range(2):
            za = F3.t[:, c, 0:ntok]
            self.ts("dve", za, F2.t[:, c, 113:113 + ntok], self.cfw.t[:, c, 0:1], MUL, [F2, self.cfw], [F3])
            for j in range(1, NDV):
                self.stt("dve", za, F2.t[:, c, 113 + j:113 + j + ntok], self.cfw.t[:, c, j:j + 1], za, MUL, ADD, [F2, self.cfw, F3], [F3])
        for c in range(2):
            zb = F1.t[:, c, 0:ntok]
            for j in range(NDV, 31):
                pool_mac(F1, zb, F2.t[:, c, 113 + j:113 + j + ntok], self.cfw.t[:, c, j:j + 1], j == NDV, F2)
        for c in range(2):
            self.stt("dve", F3.t[:, c, 0:ntok], F1.t[:, c, 0:ntok], self.cfv.t[:, c, 0:1], F3.t[:, c, 0:ntok], ADD, ADD, [F1, self.cfv, F3], [F3])
        for c in range(2):
            fw.act(F1.t[:, c, 0:ntok], F3.t[:, c, 0:ntok], AF.Square, [F3], [F1])
        for c in range(2):
            self.mm(self.pS[0].t[:, 0:ntok], self.ones_f.t[:, :], F3.t[:, c, 0:ntok], c == 0, c == 1, [self.ones_f, F3], [self.pS[0]], signal=(c == 1))
        for c in range(2):
            self.mm(self.pS[1].t[:, 0:ntok], self.ones_f.t[:, :], F1.t[:, c, 0:ntok], c == 0, c == 1, [self.ones_f, F1], [self.pS[1]], signal=(c == 1))
        MU, RS = self.MU, self.RS
        fw.act(MU.t[:, 0:ntok], self.pS[0].t[:, 0:ntok], AF.Copy, [self.pS[0]], [MU], scale=1.0 / 256)
        self.tt("dve", F2.t[:, 0, 0:ntok], MU.t[:, 0:ntok], MU.t[:, 0:ntok], MUL, [MU], [F2])
        self.stt("dve", RS.t[:, 0:ntok], self.pS[1].t[:, 0:ntok], 1.0 / 256, F2.t[:, 0, 0:ntok], MUL, SUB, [self.pS[1], F2], [RS])
        self.rsqrt(RS.t[:, 0:ntok], RS.t[:, 0:ntok], 1.0, LN_EPS, RS, RS)
        for c in range(2):
            z = F3.t[:, c, 0:ntok]
            self.tt("dve", z, z, MU.t[:, 0:ntok], SUB, [F3, MU], [F3])
            self.tt("dve", z, z, RS.t[:, 0:ntok], MUL, [F3, RS], [F3])
            fw.act(self.ZA.t[:, c, 0:ntok], z, AF.Silu, [F3, self.cfv], [self.ZA], scale=self.cfv.t[:, c, 1:2], bias=self.cfv.t[:, c, 2:3])
        for oc in range(2):
            bk = banks[self._bk % 4]
            self._bk += 1
            for c in range(2):
                self.mm(bk.t[:, 0:ntok], self.pwb.t[:, c, oc * 128:(oc + 1) * 128], self.ZA.t[:, c, 0:ntok], c == 0, c == 1,
                        [self.pwb, self.ZA], [bk], signal=(c == 1))
            self.stt("dve", self.YC.t[:, oc, 0:ntok], bk.t[:, 0:ntok], self.cfv.t[:, oc, 3:4], self.GC.t[:, oc, 0:ntok], ADD, MUL,
                     [bk, self.cfv, self.GC], [self.YC])

        if self.stop == "macro_g":
            return
        self.attention(ntok, key_blocks, GAb)
        if self.stop == "attn":
            return

        self.out_proj(nblk, v, GAb, x_res, y_dst)

    def attention(self, ntok, key_blocks, GAb):
        fw = self.fw
        nkb = len(key_blocks)
        assert key_blocks[0] == 0
        nkeys = nkb * 128
        self._pt = getattr(self, "_pt", 0)
        self._ps = getattr(self, "_ps", 0)
        self._po = getattr(self, "_po", 0)
        qtiles = [(q0, min(512, ntok - q0)) for q0 in range(0, ntok, 512)]
        for h in range(H):
            for kt, k0 in enumerate(range(0, nkeys, 512)):
                w = min(512, nkeys - k0)
                bk = self.pQ[kt % 2]
                blks = [self.ckvT[j] for j in range(k0 // 128, (k0 + w) // 128)]
                self.mm(bk.t[0:64, 0:w], self.Wkn2.t[:, h * 64:(h + 1) * 64], self.ckvT_t.t[:, k0:k0 + w], True, True, [self.Wkn2] + blks, [bk])
                self.cp("dve", self.KB_t.t[0:64, k0:k0 + w], bk.t[0:64, 0:w], [bk], [self.KBn[kt]])
            if h % 2 == 0:
                for g0 in range(0, nkb, 4):
                    ng = min(4, nkb - g0)
                    for g in range(ng):
                        j = g0 + g
                        self.mm(self.pA.t[:, g * 128:(g + 1) * 128], self.ckvT_t.t[:, j * 128:(j + 1) * 128], self.Wv.t[:, h * 64:h * 64 + 128],
                                True, True, [self.ckvT[j], self.Wv], [self.pA], signal=(g == ng - 1))
                    self.tt("dve", self.V2.t[:, g0:g0 + ng, :, 0:64],
                            self.pA.t[:, 0:ng * 128].rearrange("p (g t d) -> p g t d", g=ng, t=2),
                            self.RCKV.t[:, g0:g0 + ng].unsqueeze(2).unsqueeze(3).to_broadcast([128, ng, 2, 64]), MUL,
                            [self.pA] + [self.RCb[j] for j in range(g0, g0 + ng)], [self.V2b])
            for (q0, w) in qtiles:
                pO = self.pO[self._po % 2]
                self._po += 1

                def s_mm(i):
                    j = key_blocks[i]
                    ps_ = self.pS[(self._ps + i) % 2]
                    self.mm(ps_.t[:, 0:w], self.KB_t.t[0:96, j * 128:(j + 1) * 128], self.QT.t[0:96, h, q0:q0 + w], True, True,
                            [self.KBn[j // 4], self.KBr[j], self.QTb], [ps_])
                s_mm(0)
                for i in range(nkb):
                    j = key_blocks[i]
                    if i + 1 < nkb:
                        s_mm(i + 1)
                    ps_ = self.pS[(self._ps + i) % 2]
                    pt = self.PT[self._pt % 4]
                    self._pt += 1
                    fw.act(pt.t[:, 0:w], ps_.t[:, 0:w], AF.Exp, [ps_, self.ALb[j]], [pt], scale=self.ALPHA.t[:, j, h:h + 1])
                    self.mm(pO.t[0:65, 0:w], self.V2.t[:, j, h % 2, :], pt.t[:, 0:w], i == 0, i == nkb - 1, [self.V2b, pt], [pO],
                            signal=(i == nkb - 1))
                self._ps += nkb
                ONR = self.ONR
                fw.op("dve", lambda e: e.reciprocal(out=ONR.t[64:65, 0:w], in_=pO.t[64:65, 0:w]), [pO], [ONR])
                self.mm(self.pA.t[0:64, 0:w], self.ones_f.t[64:65, 0:64], ONR.t[64:65, 0:w], True, True, [self.ones_f, ONR], [self.pA])
                self.cp("act", self.rb.t[:, 0:w], self.pA.t[0:64, 0:w], [self.pA], [self.rb])
                self.tt("dve", ONR.t[0:64, 0:w], pO.t[0:64, 0:w], self.rb.t[:, 0:w], MUL, [pO, self.rb], [ONR])
                self.tt("dve", self.GA.t[0:64, h, q0:q0 + w], ONR.t[0:64, 0:w], self.GA.t[0:64, h, q0:q0 + w], MUL, [ONR, GAb[h]], [GAb[h]])

    def out_proj(self, nblk, v, GAb, x_res, y_dst):
        fw = self.fw
        self._wo = getattr(self, "_wo", 0)
        chunks = [("a", h) for h in range(H)] + [("b", 0), ("b", 1), ("c", 0), ("c", 1)]
        for p0 in range(0, nblk, 2):
            blks = list(range(p0, min(p0 + 2, nblk)))
            bankset = {blks[0]: (self.pQ[0], self.pQ[1])}
            if len(blks) > 1:
                bankset[blks[1]] = (self.pS[0], self.pS[1])
            for ci, (kind, idx) in enumerate(chunks):
                wo = self.woch[self._wo % 3]
                self._wo += 1
                if kind == "a":
                    K = 64
                    r0 = idx * 64
                    ybuf, yv = GAb[idx], (lambda b: self.GA.t[0:64, idx, b * 128:(b + 1) * 128])
                elif kind == "b":
                    K = 128
                    r0 = 512 + idx * 128
                    ybuf, yv = self.YB, (lambda b: self.YB.t[:, idx, b * 128:(b + 1) * 128])
                else:
                    K = 128
                    r0 = 768 + idx * 128
                    ybuf, yv = self.YC, (lambda b: self.YC.t[:, idx, b * 128:(b + 1) * 128])
                fw.dma("sp", wo.t[0:K, :], self.w_out_bf.t[r0:r0 + K, :], [self.w_out_bf], [wo])
                for b in blks:
                    for ct in range(2):
                        bk = bankset[b][ct]
                        self.mm(bk.t[:, :], yv(b), wo.t[0:K, ct * 512:(ct + 1) * 512], ci == 0, ci == len(chunks) - 1, [ybuf, wo], [bk],
                                signal=(b == blks[-1] and ct == 1))
            for b in blks:
                i = self.xi % 2
                self.xi += 1
                xt = self.xt[i]
                rap, rbuf = x_res(b)
                fw.dma("sp", xt.t[:], rap, [rbuf], [xt])
                for ct in range(2):
                    bk = bankset[b][ct]
                    tmp = self.sqn.t[:, 0:512]
                    self.tt("dve", tmp, bk.t[:, :], self.GATE[v].t[:, ct * 512:(ct + 1) * 512], MUL, [bk, self.GATE[v]], [self.sqn])
                    self.tt("pool", xt.t[:, ct * 512:(ct + 1) * 512], xt.t[:, ct * 512:(ct + 1) * 512], tmp, ADD, [xt, self.sqn], [xt])
                yap, ybuf_ = y_dst(b)
                fw.dma("sp", yap, xt.t[:], [xt], [ybuf_])

    def _emit_main(self, l, last, xfull, xown, cx, yd, ycd):
        fw = self.fw
        S, OWN, NKB = self.S, self.OWN, self.NKB
        fw.op("pool", lambda e: e.memset(self.k_tok.t[:, 0:64], 0.0), [], [self.k_tok])
        for j in range(NKB if self.stop not in ("k_x2h", "k_rope", "k_tp", "k_1", "k_mm", "k_sq") else 1):
            if j < 2:
                ap, bf = cx(j * 128)
                self.phase_k_block(j, ap, bf, 1)
            else:
                ap, bf = xfull((j - 2) * 128)
                self.phase_k_block(j, ap, bf, 0)
        if self.stop in ("phasek", "k_x2h", "k_rope", "k_tp", "k_1", "k_mm", "k_sq"):
            return
        if not last:
            srcs = [None] + [cx(b * 128) + (None,) for b in range(2)] + [None]
            self.macro(srcs, 0, [0, 1], 1, lambda b: cx(b * 128), lambda b: ycd(b * 128))
        nb = TM // 128
        for m in range(OWN // TM):
            srcs = []
            for e in range(nb + 2):
                ob = m * nb - 1 + e
                if ob < 0:
                    srcs.append(xfull(S // 2 - 128) + (self.hm.t[:, 0:1],))
                elif ob >= OWN // 128:
                    srcs.append(xfull(S // 2) + (self.hm.t[:, 1:2],))
                else:
                    srcs.append(xown(ob * 128) + (None,))
            self.macro(srcs, CTX + m * TM, list(range(NKB)), 0,
                       (lambda m_: (lambda b: xown(m_ * TM + b * 128)))(m), (lambda m_: (lambda b: yd(m_ * TM + b * 128)))(m))


_PROG_CACHE = {}


def _get_prog(S, layers, fused=False):
    key = (S, tuple(layers), fused)
    if key not in _PROG_CACHE:
        _PROG_CACHE[key] = Prog(S, list(layers), fused=fused)
    return _PROG_CACHE[key]


def _rope_tables(S):
    rows = S // 64
    pos_r = np.repeat(np.arange(rows, dtype=np.float32), 64)
    pos_c = np.tile(np.arange(64, dtype=np.float32), rows)
    inv = (np.float32(10000.0) ** (-np.arange(8, dtype=np.float32) / np.float32(8))).astype(np.float32)
    ang = np.concatenate([pos_r[:, None] * inv, pos_c[:, None] * inv], axis=-1).astype(np.float32)
    lat = np.concatenate([np.cos(ang), np.sin(ang)], axis=-1).astype(np.float32)
    ctxr = np.concatenate([np.ones((CTX, 16), np.float32), np.zeros((CTX, 16), np.float32)], axis=-1)
    return lat, ctxr


def _static_inputs(inp):
    f = lambda a: np.ascontiguousarray(np.asarray(a, dtype=np.float32))
    L = inp["norm_w"].shape[0]
    col = lambda a, c: np.ascontiguousarray(np.asarray(a, np.float32).reshape(L, c, 128).transpose(0, 2, 1))
    sel = np.zeros((2, 256), np.float32)
    sel[0, 0:128] = 1.0
    sel[1, 128:256] = 1.0
    scw = np.asarray(inp["sc_conv_w"], np.float32).reshape(L, 3, 2, 128).transpose(0, 3, 2, 1)
    cfw = np.asarray(inp["cf_conv_w"], np.float32).reshape(L, 31, 2, 128).transpose(0, 3, 2, 1)
    cfv = np.stack([col(inp["cf_conv_b"], 2), col(inp["cf_ln_w"], 2), col(inp["cf_ln_b"], 2), col(inp["cf_pw_b"], 2)], axis=-1)
    return dict(
        ident=np.eye(128, dtype=np.float32), sel=sel,
        norm_w=f(inp["norm_w"]), w_mod=f(inp["w_mod"]), b_mod=f(inp["b_mod"]), w_in=f(inp["w_in"]),
        qnw_col=col(inp["q_norm_w"], 2), w_uq=f(inp["w_uq"]), kvnw_col=col(inp["kv_norm_w"], 1), w_ukv=f(inp["w_ukv"]),
        qhw=f(inp["q_head_norm_w"]), khw=f(inp["k_head_norm_w"]),
        scw_col=np.ascontiguousarray(scw), cfw_col=np.ascontiguousarray(cfw), cfv_col=np.ascontiguousarray(cfv),
        cf_pw=f(inp["cf_pw_w"]), w_out=f(inp["w_out"]),
    )


def _core_inputs(static, x, ctx, c, c_ctx, S):
    lat, ctxr = _rope_tables(S)
    OWN = S // 2
    maps = []
    for core in range(8):
        b, half = core // 2, core % 2
        cv = np.stack([np.asarray(c[b], np.float32), np.asarray(c_ctx, np.float32)], axis=0)
        csil = np.ascontiguousarray(cv.reshape(2, 8, 128).transpose(2, 1, 0))
        hm = np.zeros((128, 2), np.float32)
        hm[:, 0] = 1.0 if half == 1 else 0.0
        hm[:, 1] = 1.0 if half == 0 else 0.0
        m = dict(static)
        m.update(
            x_full=np.ascontiguousarray(x[b]), x_own=np.ascontiguousarray(x[b, half * OWN:(half + 1) * OWN]),
            ctx_in=np.ascontiguousarray(ctx[b]), csil=csil,
            rope_k=np.ascontiguousarray(np.concatenate([ctxr, lat], axis=0)),
            rope_q=np.ascontiguousarray(np.concatenate([ctxr, lat[half * OWN:(half + 1) * OWN]], axis=0)),
            hmask=hm,
        )
        maps.append(m)
    return maps


def kernel(x, c, ctx, c_ctx, norm_w, w_mod, b_mod, w_in, q_norm_w, w_uq, kv_norm_w, w_ukv, q_head_norm_w, k_head_norm_w,
           sc_conv_w, cf_conv_w, cf_conv_b, cf_ln_w, cf_ln_b, cf_pw_w, cf_pw_b, w_out):
    inp = dict(norm_w=norm_w, w_mod=w_mod, b_mod=b_mod, w_in=w_in, q_norm_w=q_norm_w, w_uq=w_uq, kv_norm_w=kv_norm_w, w_ukv=w_ukv,
               q_head_norm_w=q_head_norm_w, k_head_norm_w=k_head_norm_w, sc_conv_w=sc_conv_w, cf_conv_w=cf_conv_w,
               cf_conv_b=cf_conv_b, cf_ln_w=cf_ln_w, cf_ln_b=cf_ln_b, cf_pw_w=cf_pw_w, cf_pw_b=cf_pw_b, w_out=w_out)
    x = np.asarray(x, np.float32)
    ctx = np.asarray(ctx, np.float32)
    c = np.asarray(c, np.float32)
    c_ctx = np.asarray(c_ctx, np.float32)
    B, S, _ = x.shape
    OWN = S // 2
    static = _static_inputs(inp)
    L = static["norm_w"].shape[0]
    prog = _get_prog(S, list(range(L)), fused=True)
    maps = _core_inputs(static, x, ctx, c, c_ctx, S)
    res = run_bass_kernel_spmd(prog.nc, maps, core_ids=list(range(8)))
    out = np.empty_like(x)
    for core in range(8):
        b, half = core // 2, core % 2
        out[b, half * OWN:(half + 1) * OWN] = res.results[core]["y"]
    return out
```

```python
import numpy as np
from contextlib import ExitStack
import concourse.bass as bass
import concourse.mybir as mybir
from concourse.bass_utils import run_bass_kernel_spmd

F32 = mybir.dt.float32
BF16 = mybir.dt.bfloat16
AF = mybir.ActivationFunctionType
ALU = mybir.AluOpType
AX = mybir.AxisListType


class Ref:
    __slots__ = ("lane", "v")

    def __init__(self, lane, v=None):
        self.lane = lane
        self.v = v


class Buf:
    def __init__(self, t, name=""):
        self.t = t
        self.name = name
        self.last_w = None
        self.readers = {}


class Fw:
    NDMA = 8

    def __init__(self, nc, es):
        self.nc = nc
        self.es = es
        self.eng = {"pe": nc.tensor, "act": nc.scalar, "dve": nc.vector, "pool": nc.gpsimd, "sp": nc.sync}
        self.sem = {}
        self.cnt = {}
        self.inc = {}
        for n in ["pe", "act", "dve", "pool"]:
            self.sem[n] = es.enter_context(nc.semaphore("s_" + n))
            self.cnt[n] = 0
            self.inc[n] = 1
        for i in range(self.NDMA):
            n = "d%d" % i
            self.sem[n] = es.enter_context(nc.semaphore("s_" + n))
            self.cnt[n] = 0
            self.inc[n] = 16
        self.waited = {e: {} for e in self.eng}
        self.pending = {e: [] for e in self.eng}
        self.dma_rr = 0
        self.n_inst = 0
        self.log = {e: [] for e in self.eng}

    def sb(self, name, shape, dt, es=None):
        self.uid = getattr(self, "uid", 0) + 1
        t = (es or self.es).enter_context(self.nc.sbuf_tensor("sb%d_%s" % (self.uid, name), shape, dt))
        return Buf(t, name)

    def ps(self, name, shape, dt, es=None):
        self.uid = getattr(self, "uid", 0) + 1
        t = (es or self.es).enter_context(self.nc.psum_tensor("ps%d_%s" % (self.uid, name), shape, dt))
        return Buf(t, name)

    def _wait(self, e, ref):
        if e == "pe" and ref.lane == "pe":
            return
        if ref.v is None:
            raise RuntimeError("dependency on unsignalled instruction (lane %s)" % ref.lane)
        if self.waited[e].get(ref.lane, 0) < ref.v:
            self.eng[e].wait_ge(self.sem[ref.lane], ref.v * self.inc[ref.lane])
            self.log[e].append(("w", ref.lane, ref.v))
            self.waited[e][ref.lane] = ref.v

    def _deps(self, e, reads, writes):
        need = {}
        for b in reads:
            r = b.last_w
            if r is not None:
                if r.lane not in need or (need[r.lane].v or 1 << 60) < (r.v or 1 << 60):
                    need[r.lane] = r
        for b in writes:
            cands = list(b.readers.values())
            if b.last_w is not None:
                cands.append(b.last_w)
            for r in cands:
                if r.lane not in need or (need[r.lane].v or 1 << 60) < (r.v or 1 << 60):
                    need[r.lane] = r
        for r in need.values():
            self._wait(e, r)

    def _record(self, ref, reads, writes):
        for b in reads:
            b.readers[ref.lane] = ref
        for b in writes:
            b.last_w = ref
            b.readers = {}

    def op(self, e, fn, reads, writes, signal=True):
        self._deps(e, reads, writes)
        inst = fn(self.eng[e])
        self.n_inst += 1
        if signal:
            self.cnt[e] += 1
            inst.then_inc(self.sem[e], 1)
            self.log[e].append(("s", e))
            ref = Ref(e, self.cnt[e])
            for p in self.pending[e]:
                p.v = ref.v
            self.pending[e] = []
        else:
            ref = Ref(e, None)
            self.pending[e].append(ref)
        self._record(ref, reads, writes)
        return ref

    def act(self, out, in_, func, reads, writes, **kw):
        return self.op("act", lambda e: e.activation(out=out, in_=in_, func=func, **kw), reads, writes)

    def dma(self, q, out, in_, reads, writes, **kw):
        lane = "d%d" % self.dma_rr
        self.dma_rr = (self.dma_rr + 1) % self.NDMA
        if self.cnt[lane] > 0:
            self._wait(q, Ref(lane, self.cnt[lane]))
        self._deps(q, reads, writes)
        inst = self.eng[q].dma_start(out=out, in_=in_, **kw)
        self.n_inst += 1
        self.cnt[lane] += 1
        inst.then_inc(self.sem[lane], 16)
        self.log[q].append(("s", lane))
        ref = Ref(lane, self.cnt[lane])
        self._record(ref, reads, writes)
        return ref

    def simulate(self):
        cnt = {l: 0 for l in self.sem}
        ptr = {e: 0 for e in self.eng}
        progress = True
        while progress:
            progress = False
            for e in self.eng:
                lg = self.log[e]
                while ptr[e] < len(lg):
                    it = lg[ptr[e]]
                    if it[0] == "w":
                        if cnt[it[1]] >= it[2]:
                            ptr[e] += 1
                            progress = True
                        else:
                            break
                    else:
                        cnt[it[1]] += 1
                        ptr[e] += 1
                        progress = True
        stuck = {e: (ptr[e], len(self.log[e]), self.log[e][ptr[e]] if ptr[e] < len(self.log[e]) else None) for e in self.eng}
        return all(ptr[e] == len(self.log[e]) for e in self.eng), stuck, cnt

    def barrier(self):
        for e in self.eng:
            if self.pending[e]:
                raise RuntimeError("barrier with pending unsignalled instr on " + e)
        for e in self.eng:
            for lane in self.sem:
                if self.cnt[lane] > 0:
                    self._wait(e, Ref(lane, self.cnt[lane]))


D = 1024
DIN = 2720
H = 8
CTX = 256
TM = 512
EPS = 1e-6
LN_EPS = 1e-5
SM_SCALE = 96 ** -0.5
OFF = dict(cq=0, ckv=256, kr=384, gA=416, scin=928, scB=1184, scC=1440, gB=1696, cfa=1952, cfg=2208, gC=2464)

MUL, ADD, SUB = ALU.mult, ALU.add, ALU.subtract


class Prog:
    def __init__(self, S, layers, n_layers_total=2, fused=False, stop=None):
        self.stop = stop
        self.S = S
        self.OWN = S // 2
        self.NKB = (CTX + S) // 128
        self.layers = layers
        self.fused = fused
        self.L = n_layers_total
        nc = self.nc = bass.Bass("TRN2", target_bir_lowering=False)
        self.es = ExitStack()
        self.fw = Fw(nc, self.es)
        self.din = {}
        self._declare_io()
        with self.es:
            self._alloc_persist()
            self._emit()

    def dram_in(self, name, shape, dt=F32):
        t = self.nc.dram_tensor(name, list(shape), dt, kind="ExternalInput").ap()
        self.din[name] = Buf(t, name)
        return self.din[name]

    def _declare_io(self):
        S, OWN, L = self.S, self.OWN, self.L
        nc = self.nc
        self.x_full = self.dram_in("x_full", [S, D])
        self.x_own = self.dram_in("x_own", [OWN, D])
        self.ctx_in = self.dram_in("ctx_in", [CTX, D])
        self.csil = self.dram_in("csil", [128, 8, 2])
        self.rope_k = self.dram_in("rope_k", [CTX + S, 32])
        self.rope_q = self.dram_in("rope_q", [CTX + OWN, 32])
        self.hmask = self.dram_in("hmask", [128, 2])
        self.ident_d = self.dram_in("ident", [128, 128])
        self.sel_d = self.dram_in("sel", [2, 256])
        self.norm_w = self.dram_in("norm_w", [L, D])
        self.w_mod = self.dram_in("w_mod", [L, D, 3 * D])
        self.b_mod = self.dram_in("b_mod", [L, 3 * D])
        self.w_in = self.dram_in("w_in", [L, D, DIN])
        self.qnw_col = self.dram_in("qnw_col", [L, 128, 2])
        self.w_uq = self.dram_in("w_uq", [L, 256, 768])
        self.kvnw_col = self.dram_in("kvnw_col", [L, 128, 1])
        self.w_ukv = self.dram_in("w_ukv", [L, 128, 1024])
        self.qhw = self.dram_in("qhw", [L, 96])
        self.khw = self.dram_in("khw", [L, 96])
        self.scw_col = self.dram_in("scw_col", [L, 128, 2, 3])
        self.cfw_col = self.dram_in("cfw_col", [L, 128, 2, 31])
        self.cfv_col = self.dram_in("cfv_col", [L, 128, 2, 4])
        self.cf_pw = self.dram_in("cf_pw", [L, 256, 256])
        self.w_out = self.dram_in("w_out", [L, D, D])
        self.y = Buf(nc.dram_tensor("y", [OWN, D], F32, kind="ExternalOutput").ap(), "y")
        self.yc = None
        if len(self.layers) == 1:
            self.yc = Buf(nc.dram_tensor("yc", [CTX, D], F32, kind="ExternalOutput").ap(), "yc")
        self.NCH = OWN // 512
        self.x1o_t = [nc.dram_tensor("x1o%d" % i, [512, D], F32) for i in range(self.NCH)]
        self.x1g_t = [nc.dram_tensor("x1g%d" % i, [1024, D], F32) for i in range(self.NCH)]
        self.x1o = [Buf(t.ap(), "x1o") for t in self.x1o_t]
        self.x1g = [Buf(t.ap(), "x1g") for t in self.x1g_t]
        self.xc1 = Buf(nc.dram_tensor("xc1", [CTX, D], F32).ap(), "xc1")
        self.w_in_bf = Buf(nc.dram_tensor("w_in_bf", [D, DIN], BF16, kind="Internal").ap(), "w_in_bf")
        self.w_out_bf = Buf(nc.dram_tensor("w_out_bf", [D, D], BF16, kind="Internal").ap(), "w_out_bf")

    def _alloc_persist(self):
        fw = self.fw
        sb, ps = fw.sb, fw.ps
        NKB = self.NKB
        NK = NKB * 128
        self.ident_f = sb("ident_f", [128, 128], F32)
        self.ident_b = sb("ident_b", [128, 128], BF16)
        self.ones_f = sb("ones_f", [128, 128], F32)
        self.sel = sb("sel", [2, 256], F32)
        self.hm = sb("hm", [128, 2], F32)
        self.G = [sb("G%d" % i, [128, D], F32) for i in range(2)]
        self.Sh = [sb("Sh%d" % i, [128, D], F32) for i in range(2)]
        self.GATE = [sb("GATE%d" % i, [128, D], F32) for i in range(2)]
        self.w_kv = sb("w_kv", [128, 8, 160], BF16)
        self.w_cq = sb("w_cq", [128, 8, 256], BF16)
        self.w_uqb = sb("w_uqb", [128, 2, 768], BF16)
        self.Wkn1 = sb("Wkn1", [128, 512], BF16)
        self.Wkn2 = sb("Wkn2", [128, 512], BF16)
        self.Wv = sb("Wv", [128, 512], BF16)
        self.pwb = sb("pwb", [128, 2, 256], BF16)
        self.wq_b = sb("wq_b", [128, 96], F32)
        self.wk_b = sb("wk_b", [128, 96], F32)
        self.scw = sb("scw", [128, 2, 3], F32)
        self.cfw = sb("cfw", [128, 2, 31], F32)
        self.cfv = sb("cfv", [128, 2, 4], F32)
        self.wch = [sb("wch%d" % i, [128, 8, 128], BF16) for i in range(3)]
        self.woch = [sb("woch%d" % i, [128, D], BF16) for i in range(3)]
        self.ckvT_t = sb("ckvT", [128, NK], BF16)
        self.ckvT = [Buf(None, "ckvT%d" % j) for j in range(NKB)]
        self.KB_t = sb("KB", [96, NK], BF16)
        self.KBr = [Buf(None, "KBr%d" % j) for j in range(NKB)]
        self.KBn = [Buf(None, "KBn%d" % j) for j in range((NK + 511) // 512)]
        self.ALPHA = sb("ALPHA", [128, NKB, 8], F32)
        self.ALb = [Buf(None, "AL%d" % j) for j in range(NKB)]
        self.RCKV = sb("RCKV", [128, NKB], F32)
        self.RCb = [Buf(None, "RC%d" % j) for j in range(NKB)]
        self.V2 = sb("V2", [128, NKB, 2, 65], BF16)
        self.pT2 = ps("pT2", [128, 1024], BF16)
        self.pT = self.pT2
        self.pA = ps("pA", [128, 512], F32)
        self.pQ = [ps("pQ%d" % i, [128, 512], F32) for i in range(2)]
        self.pS = [ps("pS%d" % i, [128, 512], F32) for i in range(2)]
        self.pO = [ps("pO%d" % i, [128, 512], F32) for i in range(2)]

    def _alloc_prep(self, es):
        sb = lambda n, sh, dt: self.fw.sb(n, sh, dt, es)
        self.csl = sb("csl", [128, 8, 2], F32)
        self.wmc = [sb("wmc%d" % i, [128, 3 * D], F32) for i in range(2)]
        self.modrow = sb("modrow", [2, 3 * D], F32)
        self.bmod2 = sb("bmod2", [2, 3 * D], F32)
        self.nw2 = sb("nw2", [2, D], F32)
        self.wstage = [sb("wstage%d" % i, [128, DIN], F32) for i in range(2)]
        self.wcast = [sb("wcast%d" % i, [128, DIN], BF16) for i in range(2)]
        self.qnw = sb("qnw", [128, 2], F32)
        self.kvnw = sb("kvnw", [128, 1], F32)
        self.tmpw = sb("tmpw", [128, 1024], F32)

    def _alloc_main(self, es):
        sb = lambda n, sh, dt: self.fw.sb(n, sh, dt, es)
        EXT = TM + 256
        self.xt = [sb("xt%d" % i, [128, D], F32) for i in range(2)]
        self.hb = [sb("hb%d" % i, [128, D], BF16) for i in range(2)]
        self.sm = [sb("sm%d" % i, [128, 16], F32) for i in range(2)]
        self.ckv_tok = [sb("ckv_tok%d" % i, [128, 128], BF16) for i in range(2)]
        self.k_tok = [sb("k_tok%d" % i, [128, 96], BF16) for i in range(2)]
        self.krw = [sb("krw%d" % i, [128, 32], F32) for i in range(2)]
        self.rtmp = [sb("rtmp%d" % i, [128, 4, 16], F32) for i in range(2)]
        self.smk = [sb("smk%d" % i, [128, 16], F32) for i in range(2)]
        self.cs = [sb("cs%d" % i, [128, 32], F32) for i in range(2)]
        self.sqn = sb("sqn", [128, 768], F32)
        self.ssn = sb("ssn", [128, 16], F32)
        self.hTm_t = sb("hTm", [128, 8, EXT], BF16)
        self.hTm = [Buf(None, "hTm%d" % i) for i in range(EXT // 128)]
        self.QT = sb("QT", [96, 8, TM], BF16)
        self.GA = sb("GA", [64, 8, TM], BF16)
        self.GB = sb("GB", [128, 2, TM], BF16)
        self.GC = sb("GC", [128, 2, TM], BF16)
        self.SCB = sb("SCB", [128, 2, TM], BF16)
        self.YB = sb("YB", [128, 2, TM], BF16)
        self.YC = sb("YC", [128, 2, TM], BF16)
        self.ZA = sb("ZA", [128, 2, TM], BF16)
        self.F1 = sb("F1", [128, 2, EXT], BF16)
        self.F2 = sb("F2", [128, 2, EXT], BF16)
        self.F3 = sb("F3", [128, 2, TM], F32)
        self.ZSQ = sb("ZSQ", [128, 2, TM], F32)
        self.dg = [sb("dg%d" % i, [128, 128], BF16) for i in range(4)]
        self.MU = sb("MU", [128, TM], F32)
        self.RS = sb("RS", [128, TM], F32)
        self.cqn_tok = sb("cqn_tok", [128, 256], BF16)
        self.cqnT = sb("cqnT", [128, 2, 128], BF16)
        self.qf = sb("qf", [128, 8, 96], F32)
        self.q_tok = sb("q_tok", [128, 8, 96], BF16)
        self.qtmp = sb("qtmp", [128, 4, 8, 16], F32)
        self.PT = [sb("PT%d" % i, [128, 512], BF16) for i in range(4)]
        self.ONR = sb("ONR", [128, 512], F32)
        self.rb = sb("rb", [64, 512], F32)

    def mm(self, out, lhsT, rhs, start, stop, reads, writes, signal=True):
        return self.fw.op("pe", lambda e: e.matmul(out, lhsT=lhsT, rhs=rhs, start=start, stop=stop), reads, writes, signal)

    def tp(self, out, in_, ident, reads, writes, signal=True):
        return self.fw.op("pe", lambda e: e.transpose(out=out, in_=in_, identity=ident), reads, writes, signal)

    def tt(self, eng, out, in0, in1, op, reads, writes):
        return self.fw.op(eng, lambda e: e.tensor_tensor(out=out, in0=in0, in1=in1, op=op), reads, writes)

    def ts(self, eng, out, in0, s1, op0, reads, writes, s2=None, op1=None):
        if op1 is None:
            return self.fw.op(eng, lambda e: e.tensor_scalar(out=out, in0=in0, scalar1=s1, scalar2=None, op0=op0), reads, writes)
        return self.fw.op(eng, lambda e: e.tensor_scalar(out=out, in0=in0, scalar1=s1, scalar2=s2, op0=op0, op1=op1), reads, writes)

    def stt(self, eng, out, in0, scalar, in1, op0, op1, reads, writes):
        return self.fw.op(eng, lambda e: e.scalar_tensor_tensor(out=out, in0=in0, scalar=scalar, in1=in1, op0=op0, op1=op1), reads, writes)

    def cp(self, eng, out, in_, reads, writes):
        if eng == "act":
            return self.fw.act(out, in_, AF.Copy, reads, writes)
        return self.fw.op(eng, lambda e: e.tensor_copy(out=out, in_=in_), reads, writes)

    def rsqrt(self, out, in_, scale, eps, buf_in, buf_out, power=-0.5):
        self.fw.act(out, in_, AF.Ln, [buf_in], [buf_out], scale=scale, bias=eps)
        self.fw.act(out, out, AF.Exp, [buf_out], [buf_out], scale=power)

    def _emit(self):
        fw = self.fw
        fw.dma("sp", self.ident_f.t[:], self.ident_d.t[:, :], [], [self.ident_f])
        fw.dma("sp", self.sel.t[:], self.sel_d.t[:, :], [], [self.sel])
        fw.dma("sp", self.hm.t[:], self.hmask.t[:, :], [], [self.hm])
        self.cp("dve", self.ident_b.t[:], self.ident_f.t[:], [self.ident_f], [self.ident_b])
        fw.op("pool", lambda e: e.memset(self.ones_f.t[:], 1.0), [], [self.ones_f])
        self.V2g = [Buf(None, "V2g%d" % g) for g in range((self.NKB + 3) // 4)]
        fw.op("pool", lambda e: e.memset(self.V2.t[:, :, :, 64:65], 1.0), [], self.V2g)
        self.xi = 0
        for li, l in enumerate(self.layers):
            last = (l == self.L - 1)
            with ExitStack() as es:
                self._alloc_prep(es)
                self._emit_prep(l)
                fw.barrier()
            if self.stop == "prep":
                return
            first = (li == 0)
            OWN = self.OWN
            if first:
                xfull = lambda t0: (self.x_full.t[t0:t0 + 128, :], self.x_full)
                xown = lambda t0: (self.x_own.t[t0:t0 + 128, :], self.x_own)
                cx = lambda t0: (self.ctx_in.t[t0:t0 + 128, :], self.ctx_in)
            else:
                def xfull(t0):
                    half, o = t0 // OWN, t0 % OWN
                    ch, r = o // 512, half * 512 + o % 512
                    return (self.x1g[ch].t[r:r + 128, :], self.x1g[ch])
                xown = lambda t0: (self.x1o[t0 // 512].t[t0 % 512:t0 % 512 + 128, :], self.x1o[t0 // 512])
                cx = lambda t0: (self.xc1.t[t0:t0 + 128, :], self.xc1)
            if last or len(self.layers) == 1:
                yd = lambda t0: (self.y.t[t0:t0 + 128, :], self.y)
                ycd = lambda t0: (self.yc.t[t0:t0 + 128, :], self.yc)
            else:
                yd = lambda t0: (self.x1o[t0 // 512].t[t0 % 512:t0 % 512 + 128, :], self.x1o[t0 // 512])
                ycd = lambda t0: (self.xc1.t[t0:t0 + 128, :], self.xc1)
            with ExitStack() as es:
                self._alloc_main(es)
                self._emit_main(l, last, xfull, xown, cx, yd, ycd)
                fw.barrier()
            if len(self.layers) > 1 and not last:
                fw.sem["cc"] = self.es.enter_context(self.nc.semaphore("s_cc"))
                fw.inc["cc"] = 1
                for i in range(self.NCH):
                    inst = self.nc.gpsimd.collective_compute(
                        "AllGather", ALU.bypass, replica_groups=[[0, 1], [2, 3], [4, 5], [6, 7]],
                        ins=[self.x1o_t[i].ap().opt()], outs=[self.x1g_t[i].ap().opt()])
                    inst.then_inc(fw.sem["cc"])
                    fw.log["pool"].append(("s", "cc"))
                    self.x1g[i].last_w = Ref("cc", i + 1)
                    self.x1g[i].readers = {}
                fw.cnt["cc"] = self.NCH
                fw.barrier()

    def _emit_prep(self, l):
        fw = self.fw
        banks = [self.pA, self.pQ[0], self.pQ[1], self.pS[0], self.pS[1], self.pO[0]]
        csl = self.csl
        fw.dma("sp", csl.t[:], self.csil.t[:, :, :], [], [csl])
        fw.act(csl.t[:], csl.t[:], AF.Silu, [csl], [csl])
        for r in range(2):
            fw.dma("sp", self.bmod2.t[r:r + 1, :], self.b_mod.t[l:l + 1, :], [], [self.bmod2])
            fw.dma("sp", self.nw2.t[r:r + 1, :], self.norm_w.t[l:l + 1, :], [], [self.nw2])
        for k in range(8):
            wm = self.wmc[k % 2]
            fw.dma("sp", wm.t[:], self.w_mod.t[l, k * 128:(k + 1) * 128, :], [], [wm])
            for j in range(6):
                self.mm(banks[j].t[0:2, :], csl.t[:, k, :], wm.t[:, j * 512:(j + 1) * 512], k == 0, k == 7,
                        [csl, wm], [banks[j]], signal=(j == 5 or k == 7))
        mr = self.modrow
        for j in range(6):
            self.tt("dve", mr.t[:, j * 512:(j + 1) * 512], banks[j].t[0:2, :], self.bmod2.t[:, j * 512:(j + 1) * 512], ADD,
                    [banks[j], self.bmod2], [mr])
        self.stt("dve", mr.t[:, D:2 * D], mr.t[:, D:2 * D], 1.0, self.nw2.t[:, :], ADD, MUL, [mr, self.nw2], [mr])
        bi = 0
        for v in range(2):
            for dst, c0 in ((self.Sh[v], 0), (self.G[v], D), (self.GATE[v], 2 * D)):
                for hf in range(2):
                    bk = banks[bi % 6]
                    bi += 1
                    self.mm(bk.t[:, :], self.sel.t[0:2, v * 128:(v + 1) * 128], mr.t[0:2, c0 + hf * 512:c0 + (hf + 1) * 512],
                            True, True, [self.sel, mr], [bk])
                    self.cp("act", dst.t[:, hf * 512:(hf + 1) * 512], bk.t[:, :], [bk], [dst])
        for k in range(8):
            st, wc = self.wstage[k % 2], self.wcast[k % 2]
            fw.dma("sp", st.t[:], self.w_in.t[l, k * 128:(k + 1) * 128, :], [], [st])
            self.cp("dve", wc.t[:, 0:1360], st.t[:, 0:1360], [st], [wc])
            self.cp("pool", wc.t[:, 1360:DIN], st.t[:, 1360:DIN], [st], [wc])
            fw.dma("sp", self.w_in_bf.t[k * 128:(k + 1) * 128, :], wc.t[:], [wc], [self.w_in_bf])
            self.cp("pool", self.w_kv.t[:, k, :], wc.t[:, 256:416], [wc], [self.w_kv])
            self.cp("pool", self.w_cq.t[:, k, :], wc.t[:, 0:256], [wc], [self.w_cq])
        for k in range(8):
            st, wc = self.wstage[k % 2], self.wcast[k % 2]
            fw.dma("sp", st.t[:, 0:D], self.w_out.t[l, k * 128:(k + 1) * 128, :], [], [st])
            self.cp("dve", wc.t[:, 0:D], st.t[:, 0:D], [st], [wc])
            fw.dma("sp", self.w_out_bf.t[k * 128:(k + 1) * 128, :], wc.t[:, 0:D], [wc], [self.w_out_bf])
        fw.dma("sp", self.qnw.t[:], self.qnw_col.t[l, :, :], [], [self.qnw])
        fw.dma("sp", self.kvnw.t[:], self.kvnw_col.t[l, :, :], [], [self.kvnw])
        for c in range(2):
            st = self.wstage[c]
            fw.dma("sp", st.t[:, 0:768], self.w_uq.t[l, c * 128:(c + 1) * 128, :], [], [st])
            self.ts("dve", self.w_uqb.t[:, c, :], st.t[:, 0:768], self.qnw.t[:, c:c + 1], MUL, [st, self.qnw], [self.w_uqb])
        fw.dma("sp", self.wk_b.t[:], self.khw.t[l, :].partition_broadcast(128), [], [self.wk_b])
        fw.dma("sp", self.wq_b.t[:], self.qhw.t[l, :].partition_broadcast(128), [], [self.wq_b])
        tw = self.tmpw
        fw.dma("sp", tw.t[:], self.w_ukv.t[l, :, :], [], [tw])
        twv = tw.t[:].rearrange("p (h t d) -> p h t d", h=8, t=2)
        v3 = lambda b: b.t[:].rearrange("p (h d) -> p h d", h=8)
        self.ts("dve", v3(self.Wkn1), twv[:, :, 0, :], self.kvnw.t[:, 0:1], MUL, [tw, self.kvnw], [self.Wkn1])
        self.ts("dve", v3(self.Wv), twv[:, :, 1, :], self.kvnw.t[:, 0:1], MUL, [tw, self.kvnw], [self.Wv])
        self.stt("dve", v3(self.Wkn2), twv[:, :, 0, :], self.kvnw.t[:, 0:1],
                 self.wk_b.t[:, 0:64].unsqueeze(1).to_broadcast([128, 8, 64]), MUL, MUL, [tw, self.kvnw, self.wk_b], [self.Wkn2])
        for c in range(2):
            st = self.wstage[c]
            fw.dma("sp", st.t[:, 0:256], self.cf_pw.t[l, c * 128:(c + 1) * 128, :], [], [st])
            self.cp("dve", self.pwb.t[:, c, :], st.t[:, 0:256], [st], [self.pwb])
        fw.dma("sp", self.scw.t[:], self.scw_col.t[l, :, :, :], [], [self.scw])
        fw.dma("sp", self.cfw.t[:], self.cfw_col.t[l, :, :, :], [], [self.cfw])
        fw.dma("sp", self.cfv.t[:], self.cfv_col.t[l, :, :, :], [], [self.cfv])

    def x_to_hT(self, src_ap, src_buf, v, mask_ap, dst_buf, dst_view):
        fw = self.fw
        i = self.xi % 2
        self.xi += 1
        self.last_par = i
        xt, hb, sm = self.xt[i], self.hb[i], self.sm[i]
        fw.dma("sp", xt.t[:], src_ap, [src_buf], [xt])
        fw.act(hb.t[:], xt.t[:], AF.Square, [xt], [hb, sm], accum_out=sm.t[:, 0:1])
        self.rsqrt(sm.t[:, 1:2], sm.t[:, 0:1], 1.0 / D, EPS, sm, sm)
        self.stt("dve", xt.t[:], xt.t[:], sm.t[:, 1:2], self.G[v].t[:], MUL, MUL, [xt, sm, self.G[v]], [xt])
        self.tt("dve", hb.t[:], xt.t[:], self.Sh[v].t[:], ADD, [xt, self.Sh[v]], [hb])
        if mask_ap is not None:
            self.ts("dve", hb.t[:], hb.t[:], mask_ap, MUL, [hb, self.hm], [hb])
        for k in range(8):
            self.tp(self.pT.t[:, k * 128:(k + 1) * 128], hb.t[:, k * 128:(k + 1) * 128], self.ident_b.t[:],
                    [hb, self.ident_b], [self.pT], signal=(k == 7))
        self.cp("act", dst_view, self.pT.t[:].rearrange("p (k t) -> p k t", k=8), [self.pT], [dst_buf])

    def phase_k_A(self, j, src_ap, src_buf, v):
        fw = self.fw
        slot = j % len(self.hTm)
        hTb = self.hTm[slot]
        hT = self.hTm_t.t[:, :, slot * 128:(slot + 1) * 128]
        self.x_to_hT(src_ap, src_buf, v, None, hTb, hT)
        p = j % 2
        cs, sm = self.cs[p], self.smk[p]
        fw.dma("sp", cs.t[:], self.rope_k.t[j * 128:(j + 1) * 128, :], [], [cs])
        pA = self.pA
        for k in range(8):
            self.mm(pA.t[:, 0:160], hT[:, k, :], self.w_kv.t[:, k, :], k == 0, k == 7, [hTb, self.w_kv], [pA], signal=(k == 7))
        sq = self.qf.t[:].rearrange("p h d -> p (h d)")
        fw.act(sq[:, 0:160], pA.t[:, 0:160], AF.Square, [pA], [self.qf])
        fw.op("dve", lambda e: e.tensor_reduce(out=sm.t[:, 2:3], in_=sq[:, 0:128], axis=AX.X, op=ADD), [self.qf], [sm])
        fw.op("dve", lambda e: e.tensor_reduce(out=sm.t[:, 3:4], in_=sq[:, 128:160], axis=AX.X, op=ADD), [self.qf], [sm])
        self.cp("dve", self.ckv_tok[p].t[:], pA.t[:, 0:128], [pA], [self.ckv_tok[p]])
        krw, rt = self.krw[p], self.rtmp[p]
        self.tt("dve", krw.t[:], pA.t[:, 128:160], self.wk_b.t[:, 64:96], MUL, [pA, self.wk_b], [krw])
        x1, x2, cos, sin = krw.t[:, 0:16], krw.t[:, 16:32], cs.t[:, 0:16], cs.t[:, 16:32]
        self.tt("dve", rt.t[:, 0, :], x1, cos, MUL, [krw, cs], [rt])
        self.tt("dve", rt.t[:, 1, :], x2, sin, MUL, [krw, cs], [rt])
        self.tt("dve", rt.t[:, 2, :], x2, cos, MUL, [krw, cs], [rt])
        self.tt("dve", rt.t[:, 3, :], x1, sin, MUL, [krw, cs], [rt])
        self.tt("dve", krw.t[:, 0:16], rt.t[:, 0, :], rt.t[:, 1, :], SUB, [rt], [krw])
        self.tt("dve", krw.t[:, 16:32], rt.t[:, 2, :], rt.t[:, 3, :], ADD, [rt], [krw])

    def phase_k_B(self, j):
        fw = self.fw
        p = j % 2
        sm, krw, k_tok, ckv_tok = self.smk[p], self.krw[p], self.k_tok[p], self.ckv_tok[p]
        fw.act(sm.t[:, 4:5], sm.t[:, 2:3], AF.Ln, [sm], [sm], scale=1.0 / 128, bias=EPS)
        fw.act(sm.t[:, 5:6], sm.t[:, 4:5], AF.Exp, [sm], [sm], scale=0.5)
        fw.act(self.RCKV.t[:, j:j + 1], sm.t[:, 4:5], AF.Exp, [sm], [self.RCb[j]], scale=-0.5)
        self.ts("dve", k_tok.t[:, 64:96], krw.t[:], sm.t[:, 5:6], MUL, [krw, sm], [k_tok])
        self.tp(self.pT2.t[:, 0:128], ckv_tok.t[:], self.ident_b.t[:], [ckv_tok, self.ident_b], [self.pT2], signal=False)
        self.tp(self.pT2.t[0:96, 128:256], k_tok.t[:], self.ident_b.t[:], [k_tok, self.ident_b], [self.pT2])
        self.cp("dve", self.ckvT_t.t[:, j * 128:(j + 1) * 128], self.pT2.t[:, 0:128], [self.pT2], [self.ckvT[j]])
        self.cp("act", self.KB_t.t[64:96, j * 128:(j + 1) * 128], self.pT2.t[64:96, 128:256], [self.pT2], [self.KBr[j]])
        pq = self.pQ[j % 2]
        self.mm(pq.t[:, :], self.ckvT_t.t[:, j * 128:(j + 1) * 128], self.Wkn1.t[:, :], True, True, [self.ckvT[j], self.Wkn1], [pq])
        fw.act(self.sqn.t[:, 0:512], pq.t[:, :], AF.Square, [pq], [self.sqn])
        fw.op("dve", lambda e: e.tensor_reduce(out=self.ssn.t[:, 0:8], in_=self.sqn.t[:, 0:512].rearrange("p (h d) -> p h d", h=8),
                                               axis=AX.X, op=ADD), [self.sqn], [self.ssn])
        self.tt("dve", sm.t[:, 6:7], self.RCKV.t[:, j:j + 1], self.RCKV.t[:, j:j + 1], MUL, [self.RCb[j]], [sm])
        self.ts("dve", self.ssn.t[:, 0:8], self.ssn.t[:, 0:8], sm.t[:, 6:7], MUL, [self.ssn, sm], [self.ssn], s2=sm.t[:, 3:4], op1=ADD)
        self.rsqrt(self.ssn.t[:, 8:16], self.ssn.t[:, 0:8], 1.0 / 96, EPS, self.ssn, self.ssn)
        self.ts("dve", self.ALPHA.t[:, j, :], self.ssn.t[:, 8:16], self.RCKV.t[:, j:j + 1], MUL, [self.ssn, self.RCb[j]], [self.ALb[j]],
                s2=SM_SCALE, op1=MUL)

    def fill_hT(self, e, s, v):
        view = self.hTm_t.t[:, :, e * 128:(e + 1) * 128]
        if s is None:
            self.fw.op("pool", lambda en: en.memset(view, 0.0), [], [self.hTm[e]])
        else:
            self.x_to_hT(s[0], s[1], v, s[2], self.hTm[e], view)

    def macro(self, ext_srcs, q_rope_row0, key_blocks, v, x_res, y_dst, next_srcs=None, next_v=0):
        fw = self.fw
        nblk = len(ext_srcs) - 2
        ntok = nblk * 128
        ext = ntok + 256
        hTt = self.hTm_t.t
        if not self.hT_prefilled:
            for e, s in enumerate(ext_srcs):
                self.fill_hT(e, s, v)
        self.hT_prefilled = False
        self.next_srcs, self.next_v = next_srcs, next_v
        hT_all = self.hTm[:len(ext_srcs)]
        hT_c = self.hTm[1:1 + nblk]
        QTb = Buf(None, "QTb")
        self.QTb = QTb
        for b in range(nblk):
            e = b + 1
            hT = hTt[:, :, e * 128:(e + 1) * 128]
            sm = self.sm[b % 2]
            cs = self.cs[b % 2]
            fw.dma("sp", cs.t[:], self.rope_q.t[q_rope_row0 + b * 128:q_rope_row0 + (b + 1) * 128, :], [], [cs])
            pA = self.pA
            for k in range(8):
                self.mm(pA.t[:, 0:256], hT[:, k, :], self.w_cq.t[:, k, :], k == 0, k == 7, [self.hTm[e], self.w_cq], [pA], signal=(k == 7))
            fw.act(self.sqn.t[:, 0:256], pA.t[:, 0:256], AF.Square, [pA], [self.sqn, sm], accum_out=sm.t[:, 8:9])
            self.rsqrt(sm.t[:, 9:10], sm.t[:, 8:9], 1.0 / 256, EPS, sm, sm)
            fw.act(self.cqn_tok.t[:], pA.t[:, 0:256], AF.Copy, [pA, sm], [self.cqn_tok], scale=sm.t[:, 9:10])
            for c in range(2):
                self.tp(self.pT2.t[:, c * 128:(c + 1) * 128], self.cqn_tok.t[:, c * 128:(c + 1) * 128], self.ident_b.t[:],
                        [self.cqn_tok, self.ident_b], [self.pT2], signal=(c == 1))
            self.cp("dve", self.cqnT.t[:].rearrange("p c t -> p (c t)"), self.pT2.t[:, 0:256], [self.pT2], [self.cqnT])
            for (bk, c0, c1) in ((self.pQ[0], 0, 480), (self.pQ[1], 480, 768)):
                for c in range(2):
                    self.mm(bk.t[:, 0:c1 - c0], self.cqnT.t[:, c, :], self.w_uqb.t[:, c, c0:c1], c == 0, c == 1,
                            [self.cqnT, self.w_uqb], [bk], signal=(c == 1))
                fw.act(self.sqn.t[:, c0:c1], bk.t[:, 0:c1 - c0], AF.Square, [bk], [self.sqn])
            fw.op("dve", lambda en: en.tensor_reduce(out=self.ssn.t[:, 0:8], in_=self.sqn.t[:, 0:768].rearrange("p (h d) -> p h d", h=8),
                                                     axis=AX.X, op=ADD), [self.sqn], [self.ssn])
            self.rsqrt(self.ssn.t[:, 8:16], self.ssn.t[:, 0:8], 1.0 / 96, EPS, self.ssn, self.ssn)
            qf = self.qf
            self.tt("dve", qf.t[:, 0:5, :], self.pQ[0].t[:, 0:480].rearrange("p (h d) -> p h d", h=5),
                    self.ssn.t[:, 8:13].unsqueeze(2).to_broadcast([128, 5, 96]), MUL, [self.pQ[0], self.ssn], [qf])
            self.tt("dve", qf.t[:, 5:8, :], self.pQ[1].t[:, 0:288].rearrange("p (h d) -> p h d", h=3),
                    self.ssn.t[:, 13:16].unsqueeze(2).to_broadcast([128, 3, 96]), MUL, [self.pQ[1], self.ssn], [qf])
            self.tt("dve", qf.t[:, :, :], qf.t[:, :, :], self.wq_b.t[:, :].unsqueeze(1).to_broadcast([128, 8, 96]), MUL, [qf, self.wq_b], [qf])
            qt_, tmp = self.q_tok, self.qtmp
            self.cp("pool", qt_.t[:, :, 0:64], qf.t[:, :, 0:64], [qf], [qt_])
            x1, x2 = qf.t[:, :, 64:80], qf.t[:, :, 80:96]
            cos = cs.t[:, 0:16].unsqueeze(1).to_broadcast([128, 8, 16])
            sin = cs.t[:, 16:32].unsqueeze(1).to_broadcast([128, 8, 16])
            self.tt("dve", tmp.t[:, 0, :, :], x1, cos, MUL, [qf, cs], [tmp])
            self.tt("dve", tmp.t[:, 1, :, :], x2, sin, MUL, [qf, cs], [tmp])
            self.tt("dve", tmp.t[:, 2, :, :], x2, cos, MUL, [qf, cs], [tmp])
            self.tt("dve", tmp.t[:, 3, :, :], x1, sin, MUL, [qf, cs], [tmp])
            self.tt("dve", qt_.t[:, :, 64:80], tmp.t[:, 0, :, :], tmp.t[:, 1, :, :], SUB, [tmp], [qt_])
            self.tt("dve", qt_.t[:, :, 80:96], tmp.t[:, 2, :, :], tmp.t[:, 3, :, :], ADD, [tmp], [qt_])
            for h in range(H):
                self.tp(self.pT2.t[0:96, h * 128:(h + 1) * 128], qt_.t[:, h, :], self.ident_b.t[:], [qt_, self.ident_b], [self.pT2],
                        signal=(h == H - 1))
            self.cp("act", self.QT.t[:, :, b * 128:(b + 1) * 128], self.pT2.t[0:96, :].rearrange("p (h t) -> p h t", h=8), [self.pT2], [QTb])

        if self.stop == "macro_b":
            return
        banks = [self.pQ[0], self.pQ[1], self.pS[0], self.pS[1]]
        self._bk = getattr(self, "_bk", 0)
        self._wc = getattr(self, "_wc", 0)
        w_in_v = self.w_in_bf.t.rearrange("(k p) n -> p k n", p=128)

        def load_chunk(c0):
            wch = self.wch[self._wc % 3]
            self._wc += 1
            fw.dma("sp", wch.t[:], w_in_v[:, :, c0:c0 + 128], [self.w_in_bf], [wch])
            return wch

        def proj(wch, m0, m1, t0, tw, hbufs):
            bk = banks[self._bk % 4]
            self._bk += 1
            for k in range(8):
                self.mm(bk.t[0:m1 - m0, 0:tw], wch.t[:, k, m0:m1], hTt[:, k, t0:t0 + tw], k == 0, k == 7, [wch] + hbufs, [bk], signal=(k == 7))
            return bk

        ext_tiles = [(t0, min(512, ext - t0)) for t0 in range(0, ext, 512)]
        F1, F2, F3 = self.F1, self.F2, self.F3
        for name in ("scin", "scC", "cfg", "cfa"):
            for c in range(2):
                wch = load_chunk(OFF[name] + c * 128)
                for (t0, tw) in ext_tiles:
                    bk = proj(wch, 0, 128, t0, tw, hT_all)
                    if name == "scin":
                        self.cp("act", F1.t[:, c, t0:t0 + tw], bk.t[:, 0:tw], [bk], [F1])
                    elif name == "scC":
                        self.tt("dve", F1.t[:, c, t0:t0 + tw], bk.t[:, 0:tw], F1.t[:, c, t0:t0 + tw], MUL, [bk, F1], [F1])
                    elif name == "cfg":
                        fw.act(F2.t[:, c, t0:t0 + tw], bk.t[:, 0:tw], AF.Sigmoid, [bk], [F2])
                    else:
                        self.tt("dve", F2.t[:, c, t0:t0 + tw], bk.t[:, 0:tw], F2.t[:, c, t0:t0 + tw], MUL, [bk, F2], [F2])
        for name, dst, fn in (("scB", self.SCB, None), ("gB", self.GB, AF.Silu), ("gC", self.GC, AF.Silu)):
            for c in range(2):
                wch = load_chunk(OFF[name] + c * 128)
                bk = proj(wch, 0, 128, 128, ntok, hT_c)
                if fn is None:
                    self.cp("dve", dst.t[:, c, 0:ntok], bk.t[:, 0:ntok], [bk], [dst])
                else:
                    fw.act(dst.t[:, c, 0:ntok], bk.t[:, 0:ntok], fn, [bk], [dst])
        GAb = [Buf(None, "GA%d" % h) for h in range(H)]
        for c in range(4):
            wch = load_chunk(OFF["gA"] + c * 128)
            for hh in range(2):
                h = 2 * c + hh
                bk = proj(wch, hh * 64, hh * 64 + 64, 128, ntok, hT_c)
                fw.act(self.GA.t[0:64, h, 0:ntok], bk.t[0:64, 0:ntok], AF.Silu, [bk], [GAb[h]])

        self._dg = getattr(self, "_dg", 0)

        def dwconv(src, c, wcol, ntap, off0):
            bk = banks[self._bk % 4]
            self._bk += 1
            for j in range(ntap):
                dg = self.dg[self._dg % 4]
                self._dg += 1
                self.ts("dve", dg.t[:], self.ident_b.t[:], wcol[:, c, j:j + 1], MUL, [self.ident_b, self.scw, self.cfw], [dg])
                self.mm(bk.t[:, 0:ntok], dg.t[:], src.t[:, c, off0 + j:off0 + j + ntok], j == 0, j == ntap - 1, [dg, src], [bk],
                        signal=True)
            return bk

        accs = [self.MU, self.RS]
        for c in range(2):
            bk = dwconv(F1, c, self.scw.t, 3, 127)
            a = accs[c].t[:, 0:ntok]
            self.tt("dve", a, bk.t[:, 0:ntok], self.SCB.t[:, c, 0:ntok], MUL, [bk, self.SCB], [accs[c]])
            self.tt("dve", self.YB.t[:, c, 0:ntok], a, self.GB.t[:, c, 0:ntok], MUL, [accs[c], self.GB], [self.YB])
        for c in range(2):
            bk = dwconv(F2, c, self.cfw.t, 31, 113)
            self.ts("dve", F3.t[:, c, 0:ntok], bk.t[:, 0:ntok], self.cfv.t[:, c, 0:1], ADD, [bk, self.cfv], [F3])
        ZSQ = self.ZSQ
        for c in range(2):
            fw.act(ZSQ.t[:, c, 0:ntok], F3.t[:, c, 0:ntok], AF.Square, [F3], [ZSQ])
        for c in range(2):
            self.mm(self.pS[0].t[:, 0:ntok], self.ones_f.t[:, :], F3.t[:, c, 0:ntok], c == 0, c == 1, [self.ones_f, F3], [self.pS[0]], signal=(c == 1))
        for c in range(2):
            self.mm(self.pS[1].t[:, 0:ntok], self.ones_f.t[:, :], ZSQ.t[:, c, 0:ntok], c == 0, c == 1, [self.ones_f, ZSQ], [self.pS[1]], signal=(c == 1))
        MU, RS = self.MU, self.RS
        musq = self.sqn.t[:, 0:ntok]
        fw.act(MU.t[:, 0:ntok], self.pS[0].t[:, 0:ntok], AF.Copy, [self.pS[0]], [MU], scale=1.0 / 256)
        self.tt("dve", musq, MU.t[:, 0:ntok], MU.t[:, 0:ntok], MUL, [MU], [self.sqn])
        self.stt("dve", RS.t[:, 0:ntok], self.pS[1].t[:, 0:ntok], 1.0 / 256, musq, MUL, SUB, [self.pS[1], self.sqn], [RS])
        self.rsqrt(RS.t[:, 0:ntok], RS.t[:, 0:ntok], 1.0, LN_EPS, RS, RS)
        for c in range(2):
            z = F3.t[:, c, 0:ntok]
            self.tt("dve", z, z, MU.t[:, 0:ntok], SUB, [F3, MU], [F3])
            self.tt("dve", z, z, RS.t[:, 0:ntok], MUL, [F3, RS], [F3])
            fw.act(self.ZA.t[:, c, 0:ntok], z, AF.Silu, [F3, self.cfv], [self.ZA], scale=self.cfv.t[:, c, 1:2], bias=self.cfv.t[:, c, 2:3])
        for oc in range(2):
            bk = banks[self._bk % 4]
            self._bk += 1
            for c in range(2):
                self.mm(bk.t[:, 0:ntok], self.pwb.t[:, c, oc * 128:(oc + 1) * 128], self.ZA.t[:, c, 0:ntok], c == 0, c == 1,
                        [self.pwb, self.ZA], [bk], signal=(c == 1))
            self.stt("dve", self.YC.t[:, oc, 0:ntok], bk.t[:, 0:ntok], self.cfv.t[:, oc, 3:4], self.GC.t[:, oc, 0:ntok], ADD, MUL,
                     [bk, self.cfv, self.GC], [self.YC])

        if self.stop == "macro_g":
            return
        self.attention(ntok, key_blocks, GAb)
        if self.stop == "attn":
            return

        self.out_proj(nblk, v, GAb, x_res, y_dst)

    def build_K_tile(self, h, kt, nkeys):
        k0 = kt * 512
        w = min(512, nkeys - k0)
        bk = self.pQ[kt % 2]
        blks = [self.ckvT[j] for j in range(k0 // 128, (k0 + w) // 128)]
        self.mm(bk.t[0:64, 0:w], self.Wkn2.t[:, h * 64:(h + 1) * 64], self.ckvT_t.t[:, k0:k0 + w], True, True, [self.Wkn2] + blks, [bk])
        self.cp("dve", self.KB_t.t[0:64, k0:k0 + w], bk.t[0:64, 0:w], [bk], [self.KBn[kt]])

    def build_V_group(self, hp, g, nkb):
        g0 = g * 4
        ng = min(4, nkb - g0)
        for q in range(ng):
            j = g0 + q
            self.mm(self.pA.t[:, q * 128:(q + 1) * 128], self.ckvT_t.t[:, j * 128:(j + 1) * 128], self.Wv.t[:, hp * 64:hp * 64 + 128],
                    True, True, [self.ckvT[j], self.Wv], [self.pA], signal=(q == ng - 1))
        self.tt("dve", self.V2.t[:, g0:g0 + ng, :, 0:64],
                self.pA.t[:, 0:ng * 128].rearrange("p (g t d) -> p g t d", g=ng, t=2),
                self.RCKV.t[:, g0:g0 + ng].unsqueeze(2).unsqueeze(3).to_broadcast([128, ng, 2, 64]), MUL,
                [self.pA] + [self.RCb[j] for j in range(g0, g0 + ng)], [self.V2g[g]])

    def attention(self, ntok, key_blocks, GAb):
        fw = self.fw
        nkb = len(key_blocks)
        assert key_blocks == list(range(nkb))
        nkeys = nkb * 128
        ngrp = (nkb + 3) // 4
        self._pt = getattr(self, "_pt", 0)
        self._ps = getattr(self, "_ps", 0)
        self._po = getattr(self, "_po", 0)
        qtiles = [(q0, min(512, ntok - q0)) for q0 in range(0, ntok, 512)]
        import os
        INTER = os.environ.get("KV_INTERLEAVE", "1") == "1"
        if self.kv_ready != nkb and INTER:
            for kt in range(ngrp):
                self.build_K_tile(0, kt, nkeys)
                self.build_V_group(0, kt, nkb)
        self.kv_ready = None
        for h in range(H):
            nh = (h + 1) % H
            if not INTER:
                for kt in range(ngrp):
                    self.build_K_tile(h, kt, nkeys)
                    if h % 2 == 0:
                        self.build_V_group(h, kt, nkb)
            for qi, (q0, w) in enumerate(qtiles):
                prefetch = (qi == len(qtiles) - 1)
                pO = self.pO[self._po % 2]
                self._po += 1

                def s_mm(i):
                    j = key_blocks[i]
                    ps_ = self.pS[(self._ps + i) % 2]
                    self.mm(ps_.t[:, 0:w], self.KB_t.t[0:96, j * 128:(j + 1) * 128], self.QT.t[0:96, h, q0:q0 + w], True, True,
                            [self.KBn[j // 4], self.KBr[j], self.QTb], [ps_])
                s_mm(0)
                for i in range(nkb):
                    j = key_blocks[i]
                    if i + 1 < nkb:
                        s_mm(i + 1)
                    ps_ = self.pS[(self._ps + i) % 2]
                    pt = self.PT[self._pt % 4]
                    self._pt += 1
                    fw.act(pt.t[:, 0:w], ps_.t[:, 0:w], AF.Exp, [ps_, self.ALb[j]], [pt], scale=self.ALPHA.t[:, j, h:h + 1])
                    last_i = (i == nkb - 1)
                    self.mm(pO.t[0:65, 0:w], self.V2.t[:, j, h % 2, :], pt.t[:, 0:w], i == 0, last_i, [self.V2g[j // 4], pt], [pO],
                            signal=last_i)
                    if INTER and prefetch and (i % 4 == 3 or last_i):
                        kt = i // 4
                        self.build_K_tile(nh, kt, nkeys)
                        if h % 2 == 1:
                            self.build_V_group((h + 1) % H, kt, nkb)
                self._ps += nkb
                ONR = self.ONR
                fw.op("dve", lambda e: e.reciprocal(out=ONR.t[64:65, 0:w], in_=pO.t[64:65, 0:w]), [pO], [ONR])
                self.mm(self.pA.t[0:64, 0:w], self.ones_f.t[64:65, 0:64], ONR.t[64:65, 0:w], True, True, [self.ones_f, ONR], [self.pA])
                self.cp("act", self.rb.t[:, 0:w], self.pA.t[0:64, 0:w], [self.pA], [self.rb])
                self.tt("dve", ONR.t[0:64, 0:w], pO.t[0:64, 0:w], self.rb.t[:, 0:w], MUL, [pO, self.rb], [ONR])
                self.tt("dve", self.GA.t[0:64, h, q0:q0 + w], ONR.t[0:64, 0:w], self.GA.t[0:64, h, q0:q0 + w], MUL, [ONR, GAb[h]], [GAb[h]])
            if self.next_srcs is not None and h < len(self.next_srcs):
                self.fill_hT(h, self.next_srcs[h], self.next_v)
        if self.next_srcs is not None:
            for e in range(H, len(self.next_srcs)):
                self.fill_hT(e, self.next_srcs[e], self.next_v)
            self.hT_prefilled = True
        self.kv_ready = nkb if INTER else None

    def out_proj(self, nblk, v, GAb, x_res, y_dst):
        fw = self.fw
        self._wo = getattr(self, "_wo", 0)
        chunks = [("a", h) for h in range(H)] + [("b", 0), ("b", 1), ("c", 0), ("c", 1)]
        for p0 in range(0, nblk, 2):
            blks = list(range(p0, min(p0 + 2, nblk)))
            bankset = {blks[0]: (self.pQ[0], self.pQ[1])}
            if len(blks) > 1:
                bankset[blks[1]] = (self.pS[0], self.pS[1])
            for ci, (kind, idx) in enumerate(chunks):
                wo = self.woch[self._wo % 3]
                self._wo += 1
                if kind == "a":
                    K = 64
                    r0 = idx * 64
                    ybuf, yv = GAb[idx], (lambda b: self.GA.t[0:64, idx, b * 128:(b + 1) * 128])
                elif kind == "b":
                    K = 128
                    r0 = 512 + idx * 128
                    ybuf, yv = self.YB, (lambda b: self.YB.t[:, idx, b * 128:(b + 1) * 128])
                else:
                    K = 128
                    r0 = 768 + idx * 128
                    ybuf, yv = self.YC, (lambda b: self.YC.t[:, idx, b * 128:(b + 1) * 128])
                fw.dma("sp", wo.t[0:K, :], self.w_out_bf.t[r0:r0 + K, :], [self.w_out_bf], [wo])
                for b in blks:
                    for ct in range(2):
                        bk = bankset[b][ct]
                        self.mm(bk.t[:, :], yv(b), wo.t[0:K, ct * 512:(ct + 1) * 512], ci == 0, ci == len(chunks) - 1, [ybuf, wo], [bk],
                                signal=(b == blks[-1] and ct == 1))
            for b in blks:
                i = self.xi % 2
                self.xi += 1
                xt = self.xt[i]
                rap, rbuf = x_res(b)
                fw.dma("sp", xt.t[:], rap, [rbuf], [xt])
                for ct in range(2):
                    bk = bankset[b][ct]
                    tmp = self.sqn.t[:, 0:512]
                    self.tt("dve", tmp, bk.t[:, :], self.GATE[v].t[:, ct * 512:(ct + 1) * 512], MUL, [bk, self.GATE[v]], [self.sqn])
                    self.tt("pool", xt.t[:, ct * 512:(ct + 1) * 512], xt.t[:, ct * 512:(ct + 1) * 512], tmp, ADD, [xt, self.sqn], [xt])
                yap, ybuf_ = y_dst(b)
                fw.dma("sp", yap, xt.t[:], [xt], [ybuf_])

    def _emit_main(self, l, last, xfull, xown, cx, yd, ycd):
        fw = self.fw
        S, OWN, NKB = self.S, self.OWN, self.NKB
        for p in range(2):
            fw.op("pool", lambda e: e.memset(self.k_tok[p].t[:, 0:64], 0.0), [], [self.k_tok[p]])
        self.kv_ready = None
        self.hT_prefilled = False
        def ksrc(j):
            return cx(j * 128) + (1,) if j < 2 else xfull((j - 2) * 128) + (0,)
        for j in range(NKB + 1):
            if j < NKB:
                ap, bf, v = ksrc(j)
                self.phase_k_A(j, ap, bf, v)
            if j >= 1:
                self.phase_k_B(j - 1)
        nb = TM // 128
        lat_srcs = []
        for m in range(OWN // TM):
            srcs = []
            for e in range(nb + 2):
                ob = m * nb - 1 + e
                if ob < 0:
                    srcs.append(xfull(S // 2 - 128) + (self.hm.t[:, 0:1],))
                elif ob >= OWN // 128:
                    srcs.append(xfull(S // 2) + (self.hm.t[:, 1:2],))
                else:
                    srcs.append(xown(ob * 128) + (None,))
            lat_srcs.append(srcs)
        if not last:
            srcs = [None] + [cx(b * 128) + (None,) for b in range(2)] + [None]
            self.macro(srcs, 0, [0, 1], 1, lambda b: cx(b * 128), lambda b: ycd(b * 128), next_srcs=lat_srcs[0], next_v=0)
        nm = OWN // TM
        for m in range(nm):
            self.macro(lat_srcs[m], CTX + m * TM, list(range(NKB)), 0,
                       (lambda m_: (lambda b: xown(m_ * TM + b * 128)))(m), (lambda m_: (lambda b: yd(m_ * TM + b * 128)))(m),
                       next_srcs=(lat_srcs[m + 1] if m + 1 < nm else None), next_v=0)


_PROG_CACHE = {}


def _get_prog(S, layers, fused=False):
    key = (S, tuple(layers), fused)
    if key not in _PROG_CACHE:
        _PROG_CACHE[key] = Prog(S, list(layers), fused=fused)
    return _PROG_CACHE[key]


def _rope_tables(S):
    rows = S // 64
    pos_r = np.repeat(np.arange(rows, dtype=np.float32), 64)
    pos_c = np.tile(np.arange(64, dtype=np.float32), rows)
    inv = (np.float32(10000.0) ** (-np.arange(8, dtype=np.float32) / np.float32(8))).astype(np.float32)
    ang = np.concatenate([pos_r[:, None] * inv, pos_c[:, None] * inv], axis=-1).astype(np.float32)
    lat = np.concatenate([np.cos(ang), np.sin(ang)], axis=-1).astype(np.float32)
    ctxr = np.concatenate([np.ones((CTX, 16), np.float32), np.zeros((CTX, 16), np.float32)], axis=-1)
    return lat, ctxr


def _static_inputs(inp):
    f = lambda a: np.ascontiguousarray(np.asarray(a, dtype=np.float32))
    L = inp["norm_w"].shape[0]
    col = lambda a, c: np.ascontiguousarray(np.asarray(a, np.float32).reshape(L, c, 128).transpose(0, 2, 1))
    sel = np.zeros((2, 256), np.float32)
    sel[0, 0:128] = 1.0
    sel[1, 128:256] = 1.0
    scw = np.asarray(inp["sc_conv_w"], np.float32).reshape(L, 3, 2, 128).transpose(0, 3, 2, 1)
    cfw = np.asarray(inp["cf_conv_w"], np.float32).reshape(L, 31, 2, 128).transpose(0, 3, 2, 1)
    cfv = np.stack([col(inp["cf_conv_b"], 2), col(inp["cf_ln_w"], 2), col(inp["cf_ln_b"], 2), col(inp["cf_pw_b"], 2)], axis=-1)
    return dict(
        ident=np.eye(128, dtype=np.float32), sel=sel,
        norm_w=f(inp["norm_w"]), w_mod=f(inp["w_mod"]), b_mod=f(inp["b_mod"]), w_in=f(inp["w_in"]),
        qnw_col=col(inp["q_norm_w"], 2), w_uq=f(inp["w_uq"]), kvnw_col=col(inp["kv_norm_w"], 1), w_ukv=f(inp["w_ukv"]),
        qhw=f(inp["q_head_norm_w"]), khw=f(inp["k_head_norm_w"]),
        scw_col=np.ascontiguousarray(scw), cfw_col=np.ascontiguousarray(cfw), cfv_col=np.ascontiguousarray(cfv),
        cf_pw=f(inp["cf_pw_w"]), w_out=f(inp["w_out"]),
    )


def _core_inputs(static, x, ctx, c, c_ctx, S):
    lat, ctxr = _rope_tables(S)
    OWN = S // 2
    maps = []
    for core in range(8):
        b, half = core // 2, core % 2
        cv = np.stack([np.asarray(c[b], np.float32), np.asarray(c_ctx, np.float32)], axis=0)
        csil = np.ascontiguousarray(cv.reshape(2, 8, 128).transpose(2, 1, 0))
        hm = np.zeros((128, 2), np.float32)
        hm[:, 0] = 1.0 if half == 1 else 0.0
        hm[:, 1] = 1.0 if half == 0 else 0.0
        m = dict(static)
        m.update(
            x_full=np.ascontiguousarray(x[b]), x_own=np.ascontiguousarray(x[b, half * OWN:(half + 1) * OWN]),
            ctx_in=np.ascontiguousarray(ctx[b]), csil=csil,
            rope_k=np.ascontiguousarray(np.concatenate([ctxr, lat], axis=0)),
            rope_q=np.ascontiguousarray(np.concatenate([ctxr, lat[half * OWN:(half + 1) * OWN]], axis=0)),
            hmask=hm,
        )
        maps.append(m)
    return maps


def kernel(x, c, ctx, c_ctx, norm_w, w_mod, b_mod, w_in, q_norm_w, w_uq, kv_norm_w, w_ukv, q_head_norm_w, k_head_norm_w,
           sc_conv_w, cf_conv_w, cf_conv_b, cf_ln_w, cf_ln_b, cf_pw_w, cf_pw_b, w_out):
    inp = dict(norm_w=norm_w, w_mod=w_mod, b_mod=b_mod, w_in=w_in, q_norm_w=q_norm_w, w_uq=w_uq, kv_norm_w=kv_norm_w, w_ukv=w_ukv,
               q_head_norm_w=q_head_norm_w, k_head_norm_w=k_head_norm_w, sc_conv_w=sc_conv_w, cf_conv_w=cf_conv_w,
               cf_conv_b=cf_conv_b, cf_ln_w=cf_ln_w, cf_ln_b=cf_ln_b, cf_pw_w=cf_pw_w, cf_pw_b=cf_pw_b, w_out=w_out)
    x = np.asarray(x, np.float32)
    ctx = np.asarray(ctx, np.float32)
    c = np.asarray(c, np.float32)
    c_ctx = np.asarray(c_ctx, np.float32)
    B, S, _ = x.shape
    OWN = S // 2
    static = _static_inputs(inp)
    L = static["norm_w"].shape[0]
    prog = _get_prog(S, list(range(L)), fused=True)
    maps = _core_inputs(static, x, ctx, c, c_ctx, S)
    res = run_bass_kernel_spmd(prog.nc, maps, core_ids=list(range(8)))
    out = np.empty_like(x)
    for core in range(8):
        b, half = core // 2, core % 2
        out[b, half * OWN:(half + 1) * OWN] = res.results[core]["y"]
    return out
```

```python
import numpy as np
from contextlib import ExitStack
import concourse.bass as bass
import concourse.mybir as mybir
from concourse.bass_utils import run_bass_kernel_spmd

F32 = mybir.dt.float32
BF16 = mybir.dt.bfloat16
AF = mybir.ActivationFunctionType
ALU = mybir.AluOpType
AX = mybir.AxisListType


class Ref:
    __slots__ = ("lane", "v")

    def __init__(self, lane, v=None):
        self.lane = lane
        self.v = v


class Buf:
    def __init__(self, t, name=""):
        self.t = t
        self.name = name
        self.last_w = None
        self.readers = {}


class Fw:
    NDMA = 8

    def __init__(self, nc, es):
        self.nc = nc
        self.es = es
        self.eng = {"pe": nc.tensor, "act": nc.scalar, "dve": nc.vector, "pool": nc.gpsimd, "sp": nc.sync}
        self.sem = {}
        self.cnt = {}
        self.inc = {}
        for n in ["pe", "act", "dve", "pool"]:
            self.sem[n] = es.enter_context(nc.semaphore("s_" + n))
            self.cnt[n] = 0
            self.inc[n] = 1
        for i in range(self.NDMA):
            n = "d%d" % i
            self.sem[n] = es.enter_context(nc.semaphore("s_" + n))
            self.cnt[n] = 0
            self.inc[n] = 16
        self.waited = {e: {} for e in self.eng}
        self.pending = {e: [] for e in self.eng}
        self.dma_rr = 0
        self.n_inst = 0
        self.log = {e: [] for e in self.eng}

    def sb(self, name, shape, dt, es=None):
        self.uid = getattr(self, "uid", 0) + 1
        t = (es or self.es).enter_context(self.nc.sbuf_tensor("sb%d_%s" % (self.uid, name), shape, dt))
        return Buf(t, name)

    def ps(self, name, shape, dt, es=None):
        self.uid = getattr(self, "uid", 0) + 1
        t = (es or self.es).enter_context(self.nc.psum_tensor("ps%d_%s" % (self.uid, name), shape, dt))
        return Buf(t, name)

    def _wait(self, e, ref):
        if e == "pe" and ref.lane == "pe":
            return
        if ref.v is None:
            raise RuntimeError("dependency on unsignalled instruction (lane %s)" % ref.lane)
        if self.waited[e].get(ref.lane, 0) < ref.v:
            self.eng[e].wait_ge(self.sem[ref.lane], ref.v * self.inc[ref.lane])
            self.log[e].append(("w", ref.lane, ref.v))
            self.waited[e][ref.lane] = ref.v

    def _deps(self, e, reads, writes):
        need = {}
        for b in reads:
            r = b.last_w
            if r is not None:
                if r.lane not in need or (need[r.lane].v or 1 << 60) < (r.v or 1 << 60):
                    need[r.lane] = r
        for b in writes:
            cands = list(b.readers.values())
            if b.last_w is not None:
                cands.append(b.last_w)
            for r in cands:
                if r.lane not in need or (need[r.lane].v or 1 << 60) < (r.v or 1 << 60):
                    need[r.lane] = r
        for r in need.values():
            self._wait(e, r)

    def _record(self, ref, reads, writes):
        for b in reads:
            b.readers[ref.lane] = ref
        for b in writes:
            b.last_w = ref
            b.readers = {}

    def op(self, e, fn, reads, writes, signal=True):
        self._deps(e, reads, writes)
        inst = fn(self.eng[e])
        self.n_inst += 1
        if signal:
            self.cnt[e] += 1
            inst.then_inc(self.sem[e], 1)
            self.log[e].append(("s", e))
            ref = Ref(e, self.cnt[e])
            for p in self.pending[e]:
                p.v = ref.v
            self.pending[e] = []
        else:
            ref = Ref(e, None)
            self.pending[e].append(ref)
        self._record(ref, reads, writes)
        return ref

    def act(self, out, in_, func, reads, writes, **kw):
        return self.op("act", lambda e: e.activation(out=out, in_=in_, func=func, **kw), reads, writes)

    def dma(self, q, out, in_, reads, writes, **kw):
        lane = "d%d" % self.dma_rr
        self.dma_rr = (self.dma_rr + 1) % self.NDMA
        if self.cnt[lane] > 0:
            self._wait(q, Ref(lane, self.cnt[lane]))
        self._deps(q, reads, writes)
        inst = self.eng[q].dma_start(out=out, in_=in_, **kw)
        self.n_inst += 1
        self.cnt[lane] += 1
        inst.then_inc(self.sem[lane], 16)
        self.log[q].append(("s", lane))
        ref = Ref(lane, self.cnt[lane])
        self._record(ref, reads, writes)
        return ref

    def simulate(self):
        cnt = {l: 0 for l in self.sem}
        ptr = {e: 0 for e in self.eng}
        progress = True
        while progress:
            progress = False
            for e in self.eng:
                lg = self.log[e]
                while ptr[e] < len(lg):
                    it = lg[ptr[e]]
                    if it[0] == "w":
                        if cnt[it[1]] >= it[2]:
                            ptr[e] += 1
                            progress = True
                        else:
                            break
                    else:
                        cnt[it[1]] += 1
                        ptr[e] += 1
                        progress = True
        stuck = {e: (ptr[e], len(self.log[e]), self.log[e][ptr[e]] if ptr[e] < len(self.log[e]) else None) for e in self.eng}
        return all(ptr[e] == len(self.log[e]) for e in self.eng), stuck, cnt

    def barrier(self):
        for e in self.eng:
            if self.pending[e]:
                raise RuntimeError("barrier with pending unsignalled instr on " + e)
        for e in self.eng:
            for lane in self.sem:
                if self.cnt[lane] > 0:
                    self._wait(e, Ref(lane, self.cnt[lane]))


D = 1024
DIN = 2720
H = 8
CTX = 256
TM = 512
EPS = 1e-6
LN_EPS = 1e-5
SM_SCALE = 96 ** -0.5
OFF = dict(cq=0, ckv=256, kr=384, gA=416, scin=928, scB=1184, scC=1440, gB=1696, cfa=1952, cfg=2208, gC=2464)

MUL, ADD, SUB = ALU.mult, ALU.add, ALU.subtract


class Prog:
    def __init__(self, S, layers, n_layers_total=2, fused=False, stop=None):
        self.stop = stop
        self.S = S
        self.OWN = S // 2
        self.NKB = (CTX + S) // 128
        self.layers = layers
        self.fused = fused
        self.L = n_layers_total
        nc = self.nc = bass.Bass("TRN2", target_bir_lowering=False)
        self.es = ExitStack()
        self.fw = Fw(nc, self.es)
        self.din = {}
        self._declare_io()
        with self.es:
            self._alloc_persist()
            self._emit()

    def dram_in(self, name, shape, dt=F32):
        t = self.nc.dram_tensor(name, list(shape), dt, kind="ExternalInput").ap()
        self.din[name] = Buf(t, name)
        return self.din[name]

    def _declare_io(self):
        S, OWN, L = self.S, self.OWN, self.L
        nc = self.nc
        self.x_full = self.dram_in("x_full", [S, D])
        self.x_own = self.dram_in("x_own", [OWN, D])
        self.ctx_in = self.dram_in("ctx_in", [CTX, D])
        self.csil = self.dram_in("csil", [128, 8, 2])
        self.rope_k = self.dram_in("rope_k", [CTX + S, 32])
        self.rope_q = self.dram_in("rope_q", [CTX + OWN, 32])
        self.hmask = self.dram_in("hmask", [128, 2])
        self.ident_d = self.dram_in("ident", [128, 128])
        self.sel_d = self.dram_in("sel", [2, 256])
        self.norm_w = self.dram_in("norm_w", [L, D])
        self.w_mod = self.dram_in("w_mod", [L, D, 3 * D])
        self.b_mod = self.dram_in("b_mod", [L, 3 * D])
        self.w_in = self.dram_in("w_in", [L, D, DIN])
        self.qnw_col = self.dram_in("qnw_col", [L, 128, 2])
        self.w_uq = self.dram_in("w_uq", [L, 256, 768])
        self.kvnw_col = self.dram_in("kvnw_col", [L, 128, 1])
        self.w_ukv = self.dram_in("w_ukv", [L, 128, 1024])
        self.qhw = self.dram_in("qhw", [L, 96])
        self.khw = self.dram_in("khw", [L, 96])
        self.scw_col = self.dram_in("scw_col", [L, 128, 2, 3])
        self.cfw_col = self.dram_in("cfw_col", [L, 128, 2, 31])
        self.cfv_col = self.dram_in("cfv_col", [L, 128, 2, 4])
        self.cf_pw = self.dram_in("cf_pw", [L, 256, 256])
        self.w_out = self.dram_in("w_out", [L, D, D])
        self.y = Buf(nc.dram_tensor("y", [OWN, D], F32, kind="ExternalOutput").ap(), "y")
        self.yc = None
        if len(self.layers) == 1:
            self.yc = Buf(nc.dram_tensor("yc", [CTX, D], F32, kind="ExternalOutput").ap(), "yc")
        self.NCH = OWN // 512
        self.x1o_t = [nc.dram_tensor("x1o%d" % i, [512, D], F32) for i in range(self.NCH)]
        self.x1g_t = [nc.dram_tensor("x1g%d" % i, [1024, D], F32) for i in range(self.NCH)]
        self.x1o = [Buf(t.ap(), "x1o") for t in self.x1o_t]
        self.x1g = [Buf(t.ap(), "x1g") for t in self.x1g_t]
        self.xc1 = Buf(nc.dram_tensor("xc1", [CTX, D], F32).ap(), "xc1")
        self.w_in_bf = Buf(nc.dram_tensor("w_in_bf", [D, DIN], BF16, kind="Internal").ap(), "w_in_bf")
        self.w_out_bf = Buf(nc.dram_tensor("w_out_bf", [D, D], BF16, kind="Internal").ap(), "w_out_bf")

    def _alloc_persist(self):
        fw = self.fw
        sb, ps = fw.sb, fw.ps
        NKB = self.NKB
        NK = NKB * 128
        self.ident_f = sb("ident_f", [128, 128], F32)
        self.ident_b = sb("ident_b", [128, 128], BF16)
        self.ones_f = sb("ones_f", [128, 128], F32)
        self.E64 = sb("E64", [128, 64], F32)
        self.sel = sb("sel", [2, 256], F32)
        self.hm = sb("hm", [128, 2], F32)
        self.G = [sb("G%d" % i, [128, D], F32) for i in range(2)]
        self.Sh = [sb("Sh%d" % i, [128, D], F32) for i in range(2)]
        self.GATE = [sb("GATE%d" % i, [128, D], F32) for i in range(2)]
        self.w_kv = sb("w_kv", [128, 8, 160], BF16)
        self.w_cq = sb("w_cq", [128, 8, 256], BF16)
        self.w_uqb = sb("w_uqb", [128, 2, 768], BF16)
        self.Wkn1 = sb("Wkn1", [128, 512], BF16)
        self.Wkn2 = sb("Wkn2", [128, 512], BF16)
        self.Wv = sb("Wv", [128, 512], BF16)
        self.pwb = sb("pwb", [128, 2, 256], BF16)
        self.wq_b = sb("wq_b", [128, 96], F32)
        self.wk_b = sb("wk_b", [128, 96], F32)
        self.scw = sb("scw", [128, 2, 3], F32)
        self.cfw = sb("cfw", [128, 2, 31], F32)
        self.cfv = sb("cfv", [128, 2, 4], F32)
        self.wch = [sb("wch%d" % i, [128, 8, 128], BF16) for i in range(3)]
        self.woch = [sb("woch%d" % i, [128, D], BF16) for i in range(3)]
        self.ckvT_t = sb("ckvT", [128, NK], BF16)
        self.ckvT = [Buf(None, "ckvT%d" % j) for j in range(NKB)]
        self.KB_t = sb("KB", [96, NK], BF16)
        self.KBr = [Buf(None, "KBr%d" % j) for j in range(NKB)]
        self.KBn = [Buf(None, "KBn%d" % j) for j in range((NK + 511) // 512)]
        self.ALPHA = sb("ALPHA", [128, NKB, 8], F32)
        self.ALb = [Buf(None, "AL%d" % j) for j in range(NKB)]
        self.RCKV = sb("RCKV", [128, NKB], F32)
        self.RCb = [Buf(None, "RC%d" % j) for j in range(NKB)]
        self.V2 = sb("V2", [128, NKB, 2, 65], BF16)
        self.pT2 = ps("pT2", [128, 1024], BF16)
        self.pT = self.pT2
        self.pA = ps("pA", [128, 512], F32)
        self.pQ = [ps("pQ%d" % i, [128, 512], F32) for i in range(2)]
        self.pS = [ps("pS%d" % i, [128, 512], F32) for i in range(2)]
        self.pO = [ps("pO%d" % i, [128, 512], F32) for i in range(2)]

    def _alloc_prep(self, es):
        sb = lambda n, sh, dt: self.fw.sb(n, sh, dt, es)
        self.csl = sb("csl", [128, 8, 2], F32)
        self.wmc = [sb("wmc%d" % i, [128, 3 * D], F32) for i in range(2)]
        self.modrow = sb("modrow", [2, 3 * D], F32)
        self.bmod2 = sb("bmod2", [2, 3 * D], F32)
        self.nw2 = sb("nw2", [2, D], F32)
        self.wstage = [sb("wstage%d" % i, [128, DIN], F32) for i in range(2)]
        self.wcast = [sb("wcast%d" % i, [128, DIN], BF16) for i in range(2)]
        self.qnw = sb("qnw", [128, 2], F32)
        self.kvnw = sb("kvnw", [128, 1], F32)
        self.tmpw = sb("tmpw", [128, 1024], F32)

    def _alloc_main(self, es):
        sb = lambda n, sh, dt: self.fw.sb(n, sh, dt, es)
        EXT = TM + 256
        self.xt = [sb("xt%d" % i, [128, D], F32) for i in range(2)]
        self.hb = [sb("hb%d" % i, [128, D], BF16) for i in range(2)]
        self.sm = [sb("sm%d" % i, [128, 16], F32) for i in range(2)]
        self.ckv_tok = [sb("ckv_tok%d" % i, [128, 128], BF16) for i in range(2)]
        self.k_tok = [sb("k_tok%d" % i, [128, 96], BF16) for i in range(2)]
        self.krw = [sb("krw%d" % i, [128, 32], F32) for i in range(2)]
        self.rtmp = [sb("rtmp%d" % i, [128, 4, 16], F32) for i in range(2)]
        self.smk = [sb("smk%d" % i, [128, 16], F32) for i in range(2)]
        self.cs = [sb("cs%d" % i, [128, 32], F32) for i in range(2)]
        self.sqn = sb("sqn", [128, 768], F32)
        self.ssn = sb("ssn", [128, 16], F32)
        self.hTm_t = sb("hTm", [128, 8, EXT], BF16)
        self.hTm = [Buf(None, "hTm%d" % i) for i in range(EXT // 128)]
        self.QT = sb("QT", [96, 8, TM], BF16)
        self.GA = sb("GA", [64, 8, TM], BF16)
        self.GB = sb("GB", [128, 2, TM], BF16)
        self.GC = sb("GC", [128, 2, TM], BF16)
        self.SCB = sb("SCB", [128, 2, TM], BF16)
        self.YB = sb("YB", [128, 2, TM], BF16)
        self.YC = sb("YC", [128, 2, TM], BF16)
        self.ZA = sb("ZA", [128, 2, TM], BF16)
        self.F1 = sb("F1", [128, 2, EXT], BF16)
        self.F2 = sb("F2", [128, 2, EXT], BF16)
        self.F3 = sb("F3", [128, 2, TM], F32)
        self.ZSQ = sb("ZSQ", [128, 2, TM], F32)
        self.dg = [sb("dg%d" % i, [128, 128], BF16) for i in range(4)]
        self.MU = sb("MU", [128, TM], F32)
        self.RS = sb("RS", [128, TM], F32)
        self.cqn_tok = sb("cqn_tok", [128, 256], BF16)
        self.cqnT = sb("cqnT", [128, 2, 128], BF16)
        self.qf = sb("qf", [128, 8, 96], F32)
        self.q_tok = sb("q_tok", [128, 8, 96], BF16)
        self.qtmp = sb("qtmp", [128, 4, 8, 16], F32)
        self.PT = [sb("PT%d" % i, [128, 512], BF16) for i in range(4)]
        self.ONR = sb("ONR", [128, 512], F32)
        self.rb = sb("rb", [64, 512], F32)

    def mm(self, out, lhsT, rhs, start, stop, reads, writes, signal=True):
        return self.fw.op("pe", lambda e: e.matmul(out, lhsT=lhsT, rhs=rhs, start=start, stop=stop), reads, writes, signal)

    def tp(self, out, in_, ident, reads, writes, signal=True):
        return self.fw.op("pe", lambda e: e.transpose(out=out, in_=in_, identity=ident), reads, writes, signal)

    def tt(self, eng, out, in0, in1, op, reads, writes):
        return self.fw.op(eng, lambda e: e.tensor_tensor(out=out, in0=in0, in1=in1, op=op), reads, writes)

    def ts(self, eng, out, in0, s1, op0, reads, writes, s2=None, op1=None):
        if op1 is None:
            return self.fw.op(eng, lambda e: e.tensor_scalar(out=out, in0=in0, scalar1=s1, scalar2=None, op0=op0), reads, writes)
        return self.fw.op(eng, lambda e: e.tensor_scalar(out=out, in0=in0, scalar1=s1, scalar2=s2, op0=op0, op1=op1), reads, writes)

    def stt(self, eng, out, in0, scalar, in1, op0, op1, reads, writes):
        return self.fw.op(eng, lambda e: e.scalar_tensor_tensor(out=out, in0=in0, scalar=scalar, in1=in1, op0=op0, op1=op1), reads, writes)

    def cp(self, eng, out, in_, reads, writes):
        if eng == "act":
            return self.fw.act(out, in_, AF.Copy, reads, writes)
        return self.fw.op(eng, lambda e: e.tensor_copy(out=out, in_=in_), reads, writes)

    def rsqrt(self, out, in_, scale, eps, buf_in, buf_out, power=-0.5):
        self.fw.act(out, in_, AF.Ln, [buf_in], [buf_out], scale=scale, bias=eps)
        self.fw.act(out, out, AF.Exp, [buf_out], [buf_out], scale=power)

    def _emit(self):
        fw = self.fw
        fw.dma("sp", self.ident_f.t[:], self.ident_d.t[:, :], [], [self.ident_f])
        fw.dma("sp", self.sel.t[:], self.sel_d.t[:, :], [], [self.sel])
        fw.dma("sp", self.hm.t[:], self.hmask.t[:, :], [], [self.hm])
        self.cp("dve", self.ident_b.t[:], self.ident_f.t[:], [self.ident_f], [self.ident_b])
        fw.op("pool", lambda e: e.memset(self.ones_f.t[:], 1.0), [], [self.ones_f])
        fw.op("pool", lambda e: e.memset(self.E64.t[:], 0.0), [], [self.E64])
        fw.op("pool", lambda e: e.memset(self.E64.t[64:65, :], 1.0), [self.E64], [self.E64])
        self.V2g = [Buf(None, "V2g%d" % g) for g in range((self.NKB + 3) // 4)]
        fw.op("pool", lambda e: e.memset(self.V2.t[:, :, :, 64:65], 1.0), [], self.V2g)
        self.xi = 0
        for li, l in enumerate(self.layers):
            last = (l == self.L - 1)
            with ExitStack() as es:
                self._alloc_prep(es)
                self._emit_prep(l)
                fw.barrier()
            if self.stop == "prep":
                return
            first = (li == 0)
            OWN = self.OWN
            if first:
                xfull = lambda t0: (self.x_full.t[t0:t0 + 128, :], self.x_full)
                xown = lambda t0: (self.x_own.t[t0:t0 + 128, :], self.x_own)
                cx = lambda t0: (self.ctx_in.t[t0:t0 + 128, :], self.ctx_in)
            else:
                def xfull(t0):
                    half, o = t0 // OWN, t0 % OWN
                    ch, r = o // 512, half * 512 + o % 512
                    return (self.x1g[ch].t[r:r + 128, :], self.x1g[ch])
                xown = lambda t0: (self.x1o[t0 // 512].t[t0 % 512:t0 % 512 + 128, :], self.x1o[t0 // 512])
                cx = lambda t0: (self.xc1.t[t0:t0 + 128, :], self.xc1)
            if last or len(self.layers) == 1:
                yd = lambda t0: (self.y.t[t0:t0 + 128, :], self.y)
                ycd = lambda t0: (self.yc.t[t0:t0 + 128, :], self.yc)
            else:
                yd = lambda t0: (self.x1o[t0 // 512].t[t0 % 512:t0 % 512 + 128, :], self.x1o[t0 // 512])
                ycd = lambda t0: (self.xc1.t[t0:t0 + 128, :], self.xc1)
            with ExitStack() as es:
                self._alloc_main(es)
                self._emit_main(l, last, xfull, xown, cx, yd, ycd)
                fw.barrier()
            if len(self.layers) > 1 and not last:
                fw.sem["cc"] = self.es.enter_context(self.nc.semaphore("s_cc"))
                fw.inc["cc"] = 1
                for i in range(self.NCH):
                    inst = self.nc.gpsimd.collective_compute(
                        "AllGather", ALU.bypass, replica_groups=[[0, 1], [2, 3], [4, 5], [6, 7]],
                        ins=[self.x1o_t[i].ap().opt()], outs=[self.x1g_t[i].ap().opt()])
                    inst.then_inc(fw.sem["cc"])
                    fw.log["pool"].append(("s", "cc"))
                    self.x1g[i].last_w = Ref("cc", i + 1)
                    self.x1g[i].readers = {}
                fw.cnt["cc"] = self.NCH
                fw.barrier()

    def _emit_prep(self, l):
        fw = self.fw
        banks = [self.pA, self.pQ[0], self.pQ[1], self.pS[0], self.pS[1], self.pO[0]]
        csl = self.csl
        fw.dma("sp", csl.t[:], self.csil.t[:, :, :], [], [csl])
        fw.act(csl.t[:], csl.t[:], AF.Silu, [csl], [csl])
        for r in range(2):
            fw.dma("sp", self.bmod2.t[r:r + 1, :], self.b_mod.t[l:l + 1, :], [], [self.bmod2])
            fw.dma("sp", self.nw2.t[r:r + 1, :], self.norm_w.t[l:l + 1, :], [], [self.nw2])
        for k in range(8):
            wm = self.wmc[k % 2]
            fw.dma("sp", wm.t[:], self.w_mod.t[l, k * 128:(k + 1) * 128, :], [], [wm])
            for j in range(6):
                self.mm(banks[j].t[0:2, :], csl.t[:, k, :], wm.t[:, j * 512:(j + 1) * 512], k == 0, k == 7,
                        [csl, wm], [banks[j]], signal=(j == 5 or k == 7))
        mr = self.modrow
        for j in range(6):
            self.tt("dve", mr.t[:, j * 512:(j + 1) * 512], banks[j].t[0:2, :], self.bmod2.t[:, j * 512:(j + 1) * 512], ADD,
                    [banks[j], self.bmod2], [mr])
        self.stt("dve", mr.t[:, D:2 * D], mr.t[:, D:2 * D], 1.0, self.nw2.t[:, :], ADD, MUL, [mr, self.nw2], [mr])
        bi = 0
        for v in range(2):
            for dst, c0 in ((self.Sh[v], 0), (self.G[v], D), (self.GATE[v], 2 * D)):
                for hf in range(2):
                    bk = banks[bi % 6]
                    bi += 1
                    self.mm(bk.t[:, :], self.sel.t[0:2, v * 128:(v + 1) * 128], mr.t[0:2, c0 + hf * 512:c0 + (hf + 1) * 512],
                            True, True, [self.sel, mr], [bk])
                    self.cp("act", dst.t[:, hf * 512:(hf + 1) * 512], bk.t[:, :], [bk], [dst])
        for k in range(8):
            st, wc = self.wstage[k % 2], self.wcast[k % 2]
            fw.dma("sp", st.t[:], self.w_in.t[l, k * 128:(k + 1) * 128, :], [], [st])
            self.cp("dve", wc.t[:, 0:1360], st.t[:, 0:1360], [st], [wc])
            self.cp("pool", wc.t[:, 1360:DIN], st.t[:, 1360:DIN], [st], [wc])
            fw.dma("sp", self.w_in_bf.t[k * 128:(k + 1) * 128, :], wc.t[:], [wc], [self.w_in_bf])
            self.cp("pool", self.w_kv.t[:, k, :], wc.t[:, 256:416], [wc], [self.w_kv])
            self.cp("pool", self.w_cq.t[:, k, :], wc.t[:, 0:256], [wc], [self.w_cq])
        for k in range(8):
            st, wc = self.wstage[k % 2], self.wcast[k % 2]
            fw.dma("sp", st.t[:, 0:D], self.w_out.t[l, k * 128:(k + 1) * 128, :], [], [st])
            self.cp("dve", wc.t[:, 0:D], st.t[:, 0:D], [st], [wc])
            fw.dma("sp", self.w_out_bf.t[k * 128:(k + 1) * 128, :], wc.t[:, 0:D], [wc], [self.w_out_bf])
        fw.dma("sp", self.qnw.t[:], self.qnw_col.t[l, :, :], [], [self.qnw])
        fw.dma("sp", self.kvnw.t[:], self.kvnw_col.t[l, :, :], [], [self.kvnw])
        for c in range(2):
            st = self.wstage[c]
            fw.dma("sp", st.t[:, 0:768], self.w_uq.t[l, c * 128:(c + 1) * 128, :], [], [st])
            self.ts("dve", self.w_uqb.t[:, c, :], st.t[:, 0:768], self.qnw.t[:, c:c + 1], MUL, [st, self.qnw], [self.w_uqb])
        fw.dma("sp", self.wk_b.t[:], self.khw.t[l, :].partition_broadcast(128), [], [self.wk_b])
        fw.dma("sp", self.wq_b.t[:], self.qhw.t[l, :].partition_broadcast(128), [], [self.wq_b])
        tw = self.tmpw
        fw.dma("sp", tw.t[:], self.w_ukv.t[l, :, :], [], [tw])
        twv = tw.t[:].rearrange("p (h t d) -> p h t d", h=8, t=2)
        v3 = lambda b: b.t[:].rearrange("p (h d) -> p h d", h=8)
        self.ts("dve", v3(self.Wkn1), twv[:, :, 0, :], self.kvnw.t[:, 0:1], MUL, [tw, self.kvnw], [self.Wkn1])
        self.ts("dve", v3(self.Wv), twv[:, :, 1, :], self.kvnw.t[:, 0:1], MUL, [tw, self.kvnw], [self.Wv])
        self.stt("dve", v3(self.Wkn2), twv[:, :, 0, :], self.kvnw.t[:, 0:1],
                 self.wk_b.t[:, 0:64].unsqueeze(1).to_broadcast([128, 8, 64]), MUL, MUL, [tw, self.kvnw, self.wk_b], [self.Wkn2])
        for c in range(2):
            st = self.wstage[c]
            fw.dma("sp", st.t[:, 0:256], self.cf_pw.t[l, c * 128:(c + 1) * 128, :], [], [st])
            self.cp("dve", self.pwb.t[:, c, :], st.t[:, 0:256], [st], [self.pwb])
        fw.dma("sp", self.scw.t[:], self.scw_col.t[l, :, :, :], [], [self.scw])
        fw.dma("sp", self.cfw.t[:], self.cfw_col.t[l, :, :, :], [], [self.cfw])
        fw.dma("sp", self.cfv.t[:], self.cfv_col.t[l, :, :, :], [], [self.cfv])

    def x_to_hT(self, src_ap, src_buf, v, mask_ap, dst_buf, dst_view):
        fw = self.fw
        i = self.xi % 2
        self.xi += 1
        self.last_par = i
        xt, hb, sm = self.xt[i], self.hb[i], self.sm[i]
        fw.dma("sp", xt.t[:], src_ap, [src_buf], [xt])
        fw.act(hb.t[:], xt.t[:], AF.Square, [xt], [hb, sm], accum_out=sm.t[:, 0:1])
        self.rsqrt(sm.t[:, 1:2], sm.t[:, 0:1], 1.0 / D, EPS, sm, sm)
        self.stt("dve", xt.t[:], xt.t[:], sm.t[:, 1:2], self.G[v].t[:], MUL, MUL, [xt, sm, self.G[v]], [xt])
        self.tt("dve", hb.t[:], xt.t[:], self.Sh[v].t[:], ADD, [xt, self.Sh[v]], [hb])
        if mask_ap is not None:
            self.ts("dve", hb.t[:], hb.t[:], mask_ap, MUL, [hb, self.hm], [hb])
        for k in range(8):
            self.tp(self.pT.t[:, k * 128:(k + 1) * 128], hb.t[:, k * 128:(k + 1) * 128], self.ident_b.t[:],
                    [hb, self.ident_b], [self.pT], signal=(k == 7))
        self.cp("act", dst_view, self.pT.t[:].rearrange("p (k t) -> p k t", k=8), [self.pT], [dst_buf])

    def phase_k_A(self, j, src_ap, src_buf, v):
        fw = self.fw
        slot = j % len(self.hTm)
        hTb = self.hTm[slot]
        hT = self.hTm_t.t[:, :, slot * 128:(slot + 1) * 128]
        self.x_to_hT(src_ap, src_buf, v, None, hTb, hT)
        p = j % 2
        cs, sm = self.cs[p], self.smk[p]
        fw.dma("sp", cs.t[:], self.rope_k.t[j * 128:(j + 1) * 128, :], [], [cs])
        pA = self.pA
        for k in range(8):
            self.mm(pA.t[:, 0:160], hT[:, k, :], self.w_kv.t[:, k, :], k == 0, k == 7, [hTb, self.w_kv], [pA], signal=(k == 7))
        sq = self.qf.t[:].rearrange("p h d -> p (h d)")
        fw.act(sq[:, 0:160], pA.t[:, 0:160], AF.Square, [pA], [self.qf])
        fw.op("dve", lambda e: e.tensor_reduce(out=sm.t[:, 2:3], in_=sq[:, 0:128], axis=AX.X, op=ADD), [self.qf], [sm])
        fw.op("dve", lambda e: e.tensor_reduce(out=sm.t[:, 3:4], in_=sq[:, 128:160], axis=AX.X, op=ADD), [self.qf], [sm])
        self.cp("dve", self.ckv_tok[p].t[:], pA.t[:, 0:128], [pA], [self.ckv_tok[p]])
        krw, rt = self.krw[p], self.rtmp[p]
        self.tt("dve", krw.t[:], pA.t[:, 128:160], self.wk_b.t[:, 64:96], MUL, [pA, self.wk_b], [krw])
        x1, x2, cos, sin = krw.t[:, 0:16], krw.t[:, 16:32], cs.t[:, 0:16], cs.t[:, 16:32]
        self.tt("dve", rt.t[:, 0, :], x1, cos, MUL, [krw, cs], [rt])
        self.tt("dve", rt.t[:, 1, :], x2, sin, MUL, [krw, cs], [rt])
        self.tt("dve", rt.t[:, 2, :], x2, cos, MUL, [krw, cs], [rt])
        self.tt("dve", rt.t[:, 3, :], x1, sin, MUL, [krw, cs], [rt])
        self.tt("dve", krw.t[:, 0:16], rt.t[:, 0, :], rt.t[:, 1, :], SUB, [rt], [krw])
        self.tt("dve", krw.t[:, 16:32], rt.t[:, 2, :], rt.t[:, 3, :], ADD, [rt], [krw])

    def phase_k_B(self, j):
        fw = self.fw
        p = j % 2
        sm, krw, k_tok, ckv_tok = self.smk[p], self.krw[p], self.k_tok[p], self.ckv_tok[p]
        fw.act(sm.t[:, 4:5], sm.t[:, 2:3], AF.Ln, [sm], [sm], scale=1.0 / 128, bias=EPS)
        fw.act(sm.t[:, 5:6], sm.t[:, 4:5], AF.Exp, [sm], [sm], scale=0.5)
        fw.act(self.RCKV.t[:, j:j + 1], sm.t[:, 4:5], AF.Exp, [sm], [self.RCb[j]], scale=-0.5)
        self.ts("dve", k_tok.t[:, 64:96], krw.t[:], sm.t[:, 5:6], MUL, [krw, sm], [k_tok])
        self.tp(self.pT2.t[:, 0:128], ckv_tok.t[:], self.ident_b.t[:], [ckv_tok, self.ident_b], [self.pT2], signal=False)
        self.tp(self.pT2.t[0:96, 128:256], k_tok.t[:], self.ident_b.t[:], [k_tok, self.ident_b], [self.pT2])
        self.cp("dve", self.ckvT_t.t[:, j * 128:(j + 1) * 128], self.pT2.t[:, 0:128], [self.pT2], [self.ckvT[j]])
        self.cp("act", self.KB_t.t[64:96, j * 128:(j + 1) * 128], self.pT2.t[64:96, 128:256], [self.pT2], [self.KBr[j]])
        pq = self.pQ[j % 2]
        self.mm(pq.t[:, :], self.ckvT_t.t[:, j * 128:(j + 1) * 128], self.Wkn1.t[:, :], True, True, [self.ckvT[j], self.Wkn1], [pq])
        fw.act(self.sqn.t[:, 0:512], pq.t[:, :], AF.Square, [pq], [self.sqn])
        fw.op("dve", lambda e: e.tensor_reduce(out=self.ssn.t[:, 0:8], in_=self.sqn.t[:, 0:512].rearrange("p (h d) -> p h d", h=8),
                                               axis=AX.X, op=ADD), [self.sqn], [self.ssn])
        self.tt("dve", sm.t[:, 6:7], self.RCKV.t[:, j:j + 1], self.RCKV.t[:, j:j + 1], MUL, [self.RCb[j]], [sm])
        self.ts("dve", self.ssn.t[:, 0:8], self.ssn.t[:, 0:8], sm.t[:, 6:7], MUL, [self.ssn, sm], [self.ssn], s2=sm.t[:, 3:4], op1=ADD)
        self.rsqrt(self.ssn.t[:, 8:16], self.ssn.t[:, 0:8], 1.0 / 96, EPS, self.ssn, self.ssn)
        self.ts("dve", self.ALPHA.t[:, j, :], self.ssn.t[:, 8:16], self.RCKV.t[:, j:j + 1], MUL, [self.ssn, self.RCb[j]], [self.ALb[j]],
                s2=SM_SCALE, op1=MUL)

    def fill_hT(self, e, s, v):
        view = self.hTm_t.t[:, :, e * 128:(e + 1) * 128]
        if s is None:
            self.fw.op("pool", lambda en: en.memset(view, 0.0), [], [self.hTm[e]])
        else:
            self.x_to_hT(s[0], s[1], v, s[2], self.hTm[e], view)

    def macro(self, ext_srcs, q_rope_row0, key_blocks, v, x_res, y_dst, next_srcs=None, next_v=0):
        fw = self.fw
        nblk = len(ext_srcs) - 2
        ntok = nblk * 128
        ext = ntok + 256
        hTt = self.hTm_t.t
        if not self.hT_prefilled:
            for e, s in enumerate(ext_srcs):
                self.fill_hT(e, s, v)
        self.hT_prefilled = False
        self.next_srcs, self.next_v = next_srcs, next_v
        hT_all = self.hTm[:len(ext_srcs)]
        hT_c = self.hTm[1:1 + nblk]
        QTb = Buf(None, "QTb")
        self.QTb = QTb
        for b in range(nblk):
            e = b + 1
            hT = hTt[:, :, e * 128:(e + 1) * 128]
            sm = self.sm[b % 2]
            cs = self.cs[b % 2]
            fw.dma("sp", cs.t[:], self.rope_q.t[q_rope_row0 + b * 128:q_rope_row0 + (b + 1) * 128, :], [], [cs])
            pA = self.pA
            for k in range(8):
                self.mm(pA.t[:, 0:256], hT[:, k, :], self.w_cq.t[:, k, :], k == 0, k == 7, [self.hTm[e], self.w_cq], [pA], signal=(k == 7))
            fw.act(self.sqn.t[:, 0:256], pA.t[:, 0:256], AF.Square, [pA], [self.sqn, sm], accum_out=sm.t[:, 8:9])
            self.rsqrt(sm.t[:, 9:10], sm.t[:, 8:9], 1.0 / 256, EPS, sm, sm)
            fw.act(self.cqn_tok.t[:], pA.t[:, 0:256], AF.Copy, [pA, sm], [self.cqn_tok], scale=sm.t[:, 9:10])
            for c in range(2):
                self.tp(self.pT2.t[:, c * 128:(c + 1) * 128], self.cqn_tok.t[:, c * 128:(c + 1) * 128], self.ident_b.t[:],
                        [self.cqn_tok, self.ident_b], [self.pT2], signal=(c == 1))
            self.cp("dve", self.cqnT.t[:].rearrange("p c t -> p (c t)"), self.pT2.t[:, 0:256], [self.pT2], [self.cqnT])
            for (bk, c0, c1) in ((self.pQ[0], 0, 480), (self.pQ[1], 480, 768)):
                for c in range(2):
                    self.mm(bk.t[:, 0:c1 - c0], self.cqnT.t[:, c, :], self.w_uqb.t[:, c, c0:c1], c == 0, c == 1,
                            [self.cqnT, self.w_uqb], [bk], signal=(c == 1))
                fw.act(self.sqn.t[:, c0:c1], bk.t[:, 0:c1 - c0], AF.Square, [bk], [self.sqn])
            fw.op("dve", lambda en: en.tensor_reduce(out=self.ssn.t[:, 0:8], in_=self.sqn.t[:, 0:768].rearrange("p (h d) -> p h d", h=8),
                                                     axis=AX.X, op=ADD), [self.sqn], [self.ssn])
            self.rsqrt(self.ssn.t[:, 8:16], self.ssn.t[:, 0:8], 1.0 / 96, EPS, self.ssn, self.ssn)
            qf = self.qf
            self.tt("dve", qf.t[:, 0:5, :], self.pQ[0].t[:, 0:480].rearrange("p (h d) -> p h d", h=5),
                    self.ssn.t[:, 8:13].unsqueeze(2).to_broadcast([128, 5, 96]), MUL, [self.pQ[0], self.ssn], [qf])
            self.tt("dve", qf.t[:, 5:8, :], self.pQ[1].t[:, 0:288].rearrange("p (h d) -> p h d", h=3),
                    self.ssn.t[:, 13:16].unsqueeze(2).to_broadcast([128, 3, 96]), MUL, [self.pQ[1], self.ssn], [qf])
            self.tt("dve", qf.t[:, :, :], qf.t[:, :, :], self.wq_b.t[:, :].unsqueeze(1).to_broadcast([128, 8, 96]), MUL, [qf, self.wq_b], [qf])
            qt_, tmp = self.q_tok, self.qtmp
            self.cp("pool", qt_.t[:, :, 0:64], qf.t[:, :, 0:64], [qf], [qt_])
            x1, x2 = qf.t[:, :, 64:80], qf.t[:, :, 80:96]
            cos = cs.t[:, 0:16].unsqueeze(1).to_broadcast([128, 8, 16])
            sin = cs.t[:, 16:32].unsqueeze(1).to_broadcast([128, 8, 16])
            self.tt("dve", tmp.t[:, 0, :, :], x1, cos, MUL, [qf, cs], [tmp])
            self.tt("dve", tmp.t[:, 1, :, :], x2, sin, MUL, [qf, cs], [tmp])
            self.tt("dve", tmp.t[:, 2, :, :], x2, cos, MUL, [qf, cs], [tmp])
            self.tt("dve", tmp.t[:, 3, :, :], x1, sin, MUL, [qf, cs], [tmp])
            self.tt("dve", qt_.t[:, :, 64:80], tmp.t[:, 0, :, :], tmp.t[:, 1, :, :], SUB, [tmp], [qt_])
            self.tt("dve", qt_.t[:, :, 80:96], tmp.t[:, 2, :, :], tmp.t[:, 3, :, :], ADD, [tmp], [qt_])
            for h in range(H):
                self.tp(self.pT2.t[0:96, h * 128:(h + 1) * 128], qt_.t[:, h, :], self.ident_b.t[:], [qt_, self.ident_b], [self.pT2],
                        signal=(h == H - 1))
            self.cp("act", self.QT.t[:, :, b * 128:(b + 1) * 128], self.pT2.t[0:96, :].rearrange("p (h t) -> p h t", h=8), [self.pT2], [QTb])

        if self.stop == "macro_b":
            return
        banks = [self.pQ[0], self.pQ[1], self.pS[0], self.pS[1]]
        self._bk = getattr(self, "_bk", 0)
        self._wc = getattr(self, "_wc", 0)
        w_in_v = self.w_in_bf.t.rearrange("(k p) n -> p k n", p=128)

        def load_chunk(c0):
            wch = self.wch[self._wc % 3]
            self._wc += 1
            fw.dma("sp", wch.t[:], w_in_v[:, :, c0:c0 + 128], [self.w_in_bf], [wch])
            return wch

        def proj(wch, m0, m1, t0, tw, hbufs):
            bk = banks[self._bk % 4]
            self._bk += 1
            for k in range(8):
                self.mm(bk.t[0:m1 - m0, 0:tw], wch.t[:, k, m0:m1], hTt[:, k, t0:t0 + tw], k == 0, k == 7, [wch] + hbufs, [bk], signal=(k == 7))
            return bk

        ext_tiles = [(t0, min(512, ext - t0)) for t0 in range(0, ext, 512)]
        F1, F2, F3 = self.F1, self.F2, self.F3
        for name in ("scin", "scC", "cfg", "cfa"):
            for c in range(2):
                wch = load_chunk(OFF[name] + c * 128)
                for (t0, tw) in ext_tiles:
                    bk = proj(wch, 0, 128, t0, tw, hT_all)
                    if name == "scin":
                        self.cp("act", F1.t[:, c, t0:t0 + tw], bk.t[:, 0:tw], [bk], [F1])
                    elif name == "scC":
                        self.tt("dve", F1.t[:, c, t0:t0 + tw], bk.t[:, 0:tw], F1.t[:, c, t0:t0 + tw], MUL, [bk, F1], [F1])
                    elif name == "cfg":
                        fw.act(F2.t[:, c, t0:t0 + tw], bk.t[:, 0:tw], AF.Sigmoid, [bk], [F2])
                    else:
                        self.tt("dve", F2.t[:, c, t0:t0 + tw], bk.t[:, 0:tw], F2.t[:, c, t0:t0 + tw], MUL, [bk, F2], [F2])
        for name, dst, fn in (("scB", self.SCB, None), ("gB", self.GB, AF.Silu), ("gC", self.GC, AF.Silu)):
            for c in range(2):
                wch = load_chunk(OFF[name] + c * 128)
                bk = proj(wch, 0, 128, 128, ntok, hT_c)
                if fn is None:
                    self.cp("dve", dst.t[:, c, 0:ntok], bk.t[:, 0:ntok], [bk], [dst])
                else:
                    fw.act(dst.t[:, c, 0:ntok], bk.t[:, 0:ntok], fn, [bk], [dst])
        GAb = [Buf(None, "GA%d" % h) for h in range(H)]
        for c in range(4):
            wch = load_chunk(OFF["gA"] + c * 128)
            for hh in range(2):
                h = 2 * c + hh
                bk = proj(wch, hh * 64, hh * 64 + 64, 128, ntok, hT_c)
                fw.act(self.GA.t[0:64, h, 0:ntok], bk.t[0:64, 0:ntok], AF.Silu, [bk], [GAb[h]])

        self._dg = getattr(self, "_dg", 0)

        def dwconv(src, c, wcol, ntap, off0):
            bk = banks[self._bk % 4]
            self._bk += 1
            for j in range(ntap):
                dg = self.dg[self._dg % 4]
                self._dg += 1
                self.ts("dve", dg.t[:], self.ident_b.t[:], wcol[:, c, j:j + 1], MUL, [self.ident_b, self.scw, self.cfw], [dg])
                self.mm(bk.t[:, 0:ntok], dg.t[:], src.t[:, c, off0 + j:off0 + j + ntok], j == 0, j == ntap - 1, [dg, src], [bk],
                        signal=True)
            return bk

        accs = [self.MU, self.RS]
        for c in range(2):
            bk = dwconv(F1, c, self.scw.t, 3, 127)
            a = accs[c].t[:, 0:ntok]
            self.tt("dve", a, bk.t[:, 0:ntok], self.SCB.t[:, c, 0:ntok], MUL, [bk, self.SCB], [accs[c]])
            self.tt("dve", self.YB.t[:, c, 0:ntok], a, self.GB.t[:, c, 0:ntok], MUL, [accs[c], self.GB], [self.YB])
        for c in range(2):
            bk = dwconv(F2, c, self.cfw.t, 31, 113)
            self.ts("dve", F3.t[:, c, 0:ntok], bk.t[:, 0:ntok], self.cfv.t[:, c, 0:1], ADD, [bk, self.cfv], [F3])
        ZSQ = self.ZSQ
        for c in range(2):
            fw.act(ZSQ.t[:, c, 0:ntok], F3.t[:, c, 0:ntok], AF.Square, [F3], [ZSQ])
        for c in range(2):
            self.mm(self.pS[0].t[:, 0:ntok], self.ones_f.t[:, :], F3.t[:, c, 0:ntok], c == 0, c == 1, [self.ones_f, F3], [self.pS[0]], signal=(c == 1))
        for c in range(2):
            self.mm(self.pS[1].t[:, 0:ntok], self.ones_f.t[:, :], ZSQ.t[:, c, 0:ntok], c == 0, c == 1, [self.ones_f, ZSQ], [self.pS[1]], signal=(c == 1))
        MU, RS = self.MU, self.RS
        musq = self.sqn.t[:, 0:ntok]
        fw.act(MU.t[:, 0:ntok], self.pS[0].t[:, 0:ntok], AF.Copy, [self.pS[0]], [MU], scale=1.0 / 256)
        self.tt("dve", musq, MU.t[:, 0:ntok], MU.t[:, 0:ntok], MUL, [MU], [self.sqn])
        self.stt("dve", RS.t[:, 0:ntok], self.pS[1].t[:, 0:ntok], 1.0 / 256, musq, MUL, SUB, [self.pS[1], self.sqn], [RS])
        self.rsqrt(RS.t[:, 0:ntok], RS.t[:, 0:ntok], 1.0, LN_EPS, RS, RS)
        for c in range(2):
            z = F3.t[:, c, 0:ntok]
            self.tt("dve", z, z, MU.t[:, 0:ntok], SUB, [F3, MU], [F3])
            self.tt("dve", z, z, RS.t[:, 0:ntok], MUL, [F3, RS], [F3])
            fw.act(self.ZA.t[:, c, 0:ntok], z, AF.Silu, [F3, self.cfv], [self.ZA], scale=self.cfv.t[:, c, 1:2], bias=self.cfv.t[:, c, 2:3])
        for oc in range(2):
            bk = banks[self._bk % 4]
            self._bk += 1
            for c in range(2):
                self.mm(bk.t[:, 0:ntok], self.pwb.t[:, c, oc * 128:(oc + 1) * 128], self.ZA.t[:, c, 0:ntok], c == 0, c == 1,
                        [self.pwb, self.ZA], [bk], signal=(c == 1))
            self.stt("dve", self.YC.t[:, oc, 0:ntok], bk.t[:, 0:ntok], self.cfv.t[:, oc, 3:4], self.GC.t[:, oc, 0:ntok], ADD, MUL,
                     [bk, self.cfv, self.GC], [self.YC])

        if self.stop == "macro_g":
            return
        self.attention(ntok, key_blocks, GAb)
        if self.stop == "attn":
            return

        self.out_proj(nblk, v, GAb, x_res, y_dst)

    def build_K_tile(self, h, kt, nkeys):
        k0 = kt * 512
        w = min(512, nkeys - k0)
        bk = self.pQ[0]
        blks = [self.ckvT[j] for j in range(k0 // 128, (k0 + w) // 128)]
        self.mm(bk.t[0:64, 0:w], self.Wkn2.t[:, h * 64:(h + 1) * 64], self.ckvT_t.t[:, k0:k0 + w], True, True, [self.Wkn2] + blks, [bk])
        self.cp("dve", self.KB_t.t[0:64, k0:k0 + w], bk.t[0:64, 0:w], [bk], [self.KBn[kt]])

    def build_V_group(self, hp, g, nkb):
        g0 = g * 4
        ng = min(4, nkb - g0)
        for q in range(ng):
            j = g0 + q
            self.mm(self.pA.t[:, q * 128:(q + 1) * 128], self.ckvT_t.t[:, j * 128:(j + 1) * 128], self.Wv.t[:, hp * 64:hp * 64 + 128],
                    True, True, [self.ckvT[j], self.Wv], [self.pA], signal=(q == ng - 1))
        self.tt("dve", self.V2.t[:, g0:g0 + ng, :, 0:64],
                self.pA.t[:, 0:ng * 128].rearrange("p (g t d) -> p g t d", g=ng, t=2),
                self.RCKV.t[:, g0:g0 + ng].unsqueeze(2).unsqueeze(3).to_broadcast([128, ng, 2, 64]), MUL,
                [self.pA] + [self.RCb[j] for j in range(g0, g0 + ng)], [self.V2g[g]])

    def attention(self, ntok, key_blocks, GAb):
        fw = self.fw
        nkb = len(key_blocks)
        assert key_blocks == list(range(nkb))
        nkeys = nkb * 128
        ngrp = (nkb + 3) // 4
        self._pt = getattr(self, "_pt", 0)
        self._ps = getattr(self, "_ps", 0)
        self._po = getattr(self, "_po", 0)
        qtiles = [(q0, min(512, ntok - q0)) for q0 in range(0, ntok, 512)]
        import os
        INTER = os.environ.get("KV_INTERLEAVE", "1") == "1"
        if self.kv_ready != nkb and INTER:
            for kt in range(ngrp):
                self.build_K_tile(0, kt, nkeys)
                self.build_V_group(0, kt, nkb)
        self.kv_ready = None
        pS3 = [self.pS[0], self.pS[1], self.pQ[1]]
        pend = []

        def make_epilogue(h, q0, w, pO):
            def epi():
                ONR = self.ONR
                fw.op("dve", lambda e: e.reciprocal(out=ONR.t[64:65, 0:w], in_=pO.t[64:65, 0:w]), [pO], [ONR])
                self.mm(self.pA.t[0:64, 0:w], self.E64.t[:, :], ONR.t[:, 0:w], True, True, [self.E64, ONR], [self.pA])
                self.cp("act", self.rb.t[:, 0:w], self.pA.t[0:64, 0:w], [self.pA], [self.rb])
                self.tt("dve", ONR.t[0:64, 0:w], pO.t[0:64, 0:w], self.rb.t[:, 0:w], MUL, [pO, self.rb], [ONR])
                self.tt("dve", self.GA.t[0:64, h, q0:q0 + w], ONR.t[0:64, 0:w], self.GA.t[0:64, h, q0:q0 + w], MUL, [ONR, GAb[h]], [GAb[h]])
            return epi

        for h in range(H):
            nh = (h + 1) % H
            if not INTER:
                for kt in range(ngrp):
                    self.build_K_tile(h, kt, nkeys)
                    if h % 2 == 0:
                        self.build_V_group(h, kt, nkb)
            for qi, (q0, w) in enumerate(qtiles):
                prefetch = (qi == len(qtiles) - 1)
                pO = self.pO[self._po % 2]
                self._po += 1

                def s_mm(i):
                    j = key_blocks[i]
                    ps_ = pS3[(self._ps + i) % 3]
                    self.mm(ps_.t[:, 0:w], self.KB_t.t[0:96, j * 128:(j + 1) * 128], self.QT.t[0:96, h, q0:q0 + w], True, True,
                            [self.KBn[j // 4], self.KBr[j], self.QTb], [ps_])
                s_mm(0)
                if nkb > 1:
                    s_mm(1)
                for i in range(nkb):
                    j = key_blocks[i]
                    if i + 2 < nkb:
                        s_mm(i + 2)
                    ps_ = pS3[(self._ps + i) % 3]
                    pt = self.PT[self._pt % 4]
                    self._pt += 1
                    fw.act(pt.t[:, 0:w], ps_.t[:, 0:w], AF.Exp, [ps_, self.ALb[j]], [pt], scale=self.ALPHA.t[:, j, h:h + 1])
                    last_i = (i == nkb - 1)
                    self.mm(pO.t[0:65, 0:w], self.V2.t[:, j, h % 2, :], pt.t[:, 0:w], i == 0, last_i, [self.V2g[j // 4], pt], [pO],
                            signal=last_i)
                    if INTER and prefetch and (i % 4 == 3 or last_i):
                        kt = i // 4
                        self.build_K_tile(nh, kt, nkeys)
                        if h % 2 == 1:
                            self.build_V_group((h + 1) % H, kt, nkb)
                    if pend and i == min(6, nkb - 1):
                        pend.pop(0)()
                self._ps += nkb
                pend.append(make_epilogue(h, q0, w, pO))
            if self.next_srcs is not None and h < len(self.next_srcs):
                self.fill_hT(h, self.next_srcs[h], self.next_v)
        while pend:
            pend.pop(0)()
        if self.next_srcs is not None:
            for e in range(H, len(self.next_srcs)):
                self.fill_hT(e, self.next_srcs[e], self.next_v)
            self.hT_prefilled = True
        self.kv_ready = nkb if INTER else None

    def out_proj(self, nblk, v, GAb, x_res, y_dst):
        fw = self.fw
        self._wo = getattr(self, "_wo", 0)
        chunks = [("a", h) for h in range(H)] + [("b", 0), ("b", 1), ("c", 0), ("c", 1)]
        for p0 in range(0, nblk, 2):
            blks = list(range(p0, min(p0 + 2, nblk)))
            bankset = {blks[0]: (self.pQ[0], self.pQ[1])}
            if len(blks) > 1:
                bankset[blks[1]] = (self.pS[0], self.pS[1])
            for ci, (kind, idx) in enumerate(chunks):
                wo = self.woch[self._wo % 3]
                self._wo += 1
                if kind == "a":
                    K = 64
                    r0 = idx * 64
                    ybuf, yv = GAb[idx], (lambda b: self.GA.t[0:64, idx, b * 128:(b + 1) * 128])
                elif kind == "b":
                    K = 128
                    r0 = 512 + idx * 128
                    ybuf, yv = self.YB, (lambda b: self.YB.t[:, idx, b * 128:(b + 1) * 128])
                else:
                    K = 128
                    r0 = 768 + idx * 128
                    ybuf, yv = self.YC, (lambda b: self.YC.t[:, idx, b * 128:(b + 1) * 128])
                fw.dma("sp", wo.t[0:K, :], self.w_out_bf.t[r0:r0 + K, :], [self.w_out_bf], [wo])
                for b in blks:
                    for ct in range(2):
                        bk = bankset[b][ct]
                        self.mm(bk.t[:, :], yv(b), wo.t[0:K, ct * 512:(ct + 1) * 512], ci == 0, ci == len(chunks) - 1, [ybuf, wo], [bk],
                                signal=(b == blks[-1] and ct == 1))
            for b in blks:
                i = self.xi % 2
                self.xi += 1
                xt = self.xt[i]
                rap, rbuf = x_res(b)
                fw.dma("sp", xt.t[:], rap, [rbuf], [xt])
                for ct in range(2):
                    bk = bankset[b][ct]
                    tmp = self.sqn.t[:, 0:512]
                    self.tt("dve", tmp, bk.t[:, :], self.GATE[v].t[:, ct * 512:(ct + 1) * 512], MUL, [bk, self.GATE[v]], [self.sqn])
                    self.tt("pool", xt.t[:, ct * 512:(ct + 1) * 512], xt.t[:, ct * 512:(ct + 1) * 512], tmp, ADD, [xt, self.sqn], [xt])
                yap, ybuf_ = y_dst(b)
                fw.dma("sp", yap, xt.t[:], [xt], [ybuf_])

    def _emit_main(self, l, last, xfull, xown, cx, yd, ycd):
        fw = self.fw
        S, OWN, NKB = self.S, self.OWN, self.NKB
        fw.op("pool", lambda e: e.memset(self.ONR.t[:], 0.0), [], [self.ONR])
        for p in range(2):
            fw.op("pool", lambda e: e.memset(self.k_tok[p].t[:, 0:64], 0.0), [], [self.k_tok[p]])
        self.kv_ready = None
        self.hT_prefilled = False
        def ksrc(j):
            return cx(j * 128) + (1,) if j < 2 else xfull((j - 2) * 128) + (0,)
        for j in range(NKB + 1):
            if j < NKB:
                ap, bf, v = ksrc(j)
                self.phase_k_A(j, ap, bf, v)
            if j >= 1:
                self.phase_k_B(j - 1)
        nb = TM // 128
        lat_srcs = []
        for m in range(OWN // TM):
            srcs = []
            for e in range(nb + 2):
                ob = m * nb - 1 + e
                if ob < 0:
                    srcs.append(xfull(S // 2 - 128) + (self.hm.t[:, 0:1],))
                elif ob >= OWN // 128:
                    srcs.append(xfull(S // 2) + (self.hm.t[:, 1:2],))
                else:
                    srcs.append(xown(ob * 128) + (None,))
            lat_srcs.append(srcs)
        if not last:
            srcs = [None] + [cx(b * 128) + (None,) for b in range(2)] + [None]
            self.macro(srcs, 0, [0, 1], 1, lambda b: cx(b * 128), lambda b: ycd(b * 128), next_srcs=lat_srcs[0], next_v=0)
        nm = OWN // TM
        for m in range(nm):
            self.macro(lat_srcs[m], CTX + m * TM, list(range(NKB)), 0,
                       (lambda m_: (lambda b: xown(m_ * TM + b * 128)))(m), (lambda m_: (lambda b: yd(m_ * TM + b * 128)))(m),
                       next_srcs=(lat_srcs[m + 1] if m + 1 < nm else None), next_v=0)


_PROG_CACHE = {}


def _get_prog(S, layers, fused=False):
    key = (S, tuple(layers), fused)
    if key not in _PROG_CACHE:
        _PROG_CACHE[key] = Prog(S, list(layers), fused=fused)
    return _PROG_CACHE[key]


def _rope_tables(S):
    rows = S // 64
    pos_r = np.repeat(np.arange(rows, dtype=np.float32), 64)
    pos_c = np.tile(np.arange(64, dtype=np.float32), rows)
    inv = (np.float32(10000.0) ** (-np.arange(8, dtype=np.float32) / np.float32(8))).astype(np.float32)
    ang = np.concatenate([pos_r[:, None] * inv, pos_c[:, None] * inv], axis=-1).astype(np.float32)
    lat = np.concatenate([np.cos(ang), np.sin(ang)], axis=-1).astype(np.float32)
    ctxr = np.concatenate([np.ones((CTX, 16), np.float32), np.zeros((CTX, 16), np.float32)], axis=-1)
    return lat, ctxr


def _static_inputs(inp):
    f = lambda a: np.ascontiguousarray(np.asarray(a, dtype=np.float32))
    L = inp["norm_w"].shape[0]
    col = lambda a, c: np.ascontiguousarray(np.asarray(a, np.float32).reshape(L, c, 128).transpose(0, 2, 1))
    sel = np.zeros((2, 256), np.float32)
    sel[0, 0:128] = 1.0
    sel[1, 128:256] = 1.0
    scw = np.asarray(inp["sc_conv_w"], np.float32).reshape(L, 3, 2, 128).transpose(0, 3, 2, 1)
    cfw = np.asarray(inp["cf_conv_w"], np.float32).reshape(L, 31, 2, 128).transpose(0, 3, 2, 1)
    cfv = np.stack([col(inp["cf_conv_b"], 2), col(inp["cf_ln_w"], 2), col(inp["cf_ln_b"], 2), col(inp["cf_pw_b"], 2)], axis=-1)
    return dict(
        ident=np.eye(128, dtype=np.float32), sel=sel,
        norm_w=f(inp["norm_w"]), w_mod=f(inp["w_mod"]), b_mod=f(inp["b_mod"]), w_in=f(inp["w_in"]),
        qnw_col=col(inp["q_norm_w"], 2), w_uq=f(inp["w_uq"]), kvnw_col=col(inp["kv_norm_w"], 1), w_ukv=f(inp["w_ukv"]),
        qhw=f(inp["q_head_norm_w"]), khw=f(inp["k_head_norm_w"]),
        scw_col=np.ascontiguousarray(scw), cfw_col=np.ascontiguousarray(cfw), cfv_col=np.ascontiguousarray(cfv),
        cf_pw=f(inp["cf_pw_w"]), w_out=f(inp["w_out"]),
    )


def _core_inputs(static, x, ctx, c, c_ctx, S):
    lat, ctxr = _rope_tables(S)
    OWN = S // 2
    maps = []
    for core in range(8):
        b, half = core // 2, core % 2
        cv = np.stack([np.asarray(c[b], np.float32), np.asarray(c_ctx, np.float32)], axis=0)
        csil = np.ascontiguousarray(cv.reshape(2, 8, 128).transpose(2, 1, 0))
        hm = np.zeros((128, 2), np.float32)
        hm[:, 0] = 1.0 if half == 1 else 0.0
        hm[:, 1] = 1.0 if half == 0 else 0.0
        m = dict(static)
        m.update(
            x_full=np.ascontiguousarray(x[b]), x_own=np.ascontiguousarray(x[b, half * OWN:(half + 1) * OWN]),
            ctx_in=np.ascontiguousarray(ctx[b]), csil=csil,
            rope_k=np.ascontiguousarray(np.concatenate([ctxr, lat], axis=0)),
            rope_q=np.ascontiguousarray(np.concatenate([ctxr, lat[half * OWN:(half + 1) * OWN]], axis=0)),
            hmask=hm,
        )
        maps.append(m)
    return maps


def kernel(x, c, ctx, c_ctx, norm_w, w_mod, b_mod, w_in, q_norm_w, w_uq, kv_norm_w, w_ukv, q_head_norm_w, k_head_norm_w,
           sc_conv_w, cf_conv_w, cf_conv_b, cf_ln_w, cf_ln_b, cf_pw_w, cf_pw_b, w_out):
    inp = dict(norm_w=norm_w, w_mod=w_mod, b_mod=b_mod, w_in=w_in, q_norm_w=q_norm_w, w_uq=w_uq, kv_norm_w=kv_norm_w, w_ukv=w_ukv,
               q_head_norm_w=q_head_norm_w, k_head_norm_w=k_head_norm_w, sc_conv_w=sc_conv_w, cf_conv_w=cf_conv_w,
               cf_conv_b=cf_conv_b, cf_ln_w=cf_ln_w, cf_ln_b=cf_ln_b, cf_pw_w=cf_pw_w, cf_pw_b=cf_pw_b, w_out=w_out)
    x = np.asarray(x, np.float32)
    ctx = np.asarray(ctx, np.float32)
    c = np.asarray(c, np.float32)
    c_ctx = np.asarray(c_ctx, np.float32)
    B, S, _ = x.shape
    OWN = S // 2
    static = _static_inputs(inp)
    L = static["norm_w"].shape[0]
    prog = _get_prog(S, list(range(L)), fused=True)
    maps = _core_inputs(static, x, ctx, c, c_ctx, S)
    res = run_bass_kernel_spmd(prog.nc, maps, core_ids=list(range(8)))
    out = np.empty_like(x)
    for core in range(8):
        b, half = core // 2, core % 2
        out[b, half * OWN:(half + 1) * OWN] = res.results[core]["y"]
    return out
```

```python
import numpy as np
from contextlib import ExitStack
import concourse.bass as bass
import concourse.mybir as mybir
from concourse.bass_utils import run_bass_kernel_spmd

F32 = mybir.dt.float32
BF16 = mybir.dt.bfloat16
AF = mybir.ActivationFunctionType
ALU = mybir.AluOpType
AX = mybir.AxisListType


class Ref:
    __slots__ = ("lane", "v")

    def __init__(self, lane, v=None):
        self.lane = lane
        self.v = v


class Buf:
    def __init__(self, t, name=""):
        self.t = t
        self.name = name
        self.last_w = None
        self.readers = {}


class Fw:
    NDMA = 8

    def __init__(self, nc, es):
        self.nc = nc
        self.es = es
        self.eng = {"pe": nc.tensor, "act": nc.scalar, "dve": nc.vector, "pool": nc.gpsimd, "sp": nc.sync}
        self.sem = {}
        self.cnt = {}
        self.inc = {}
        for n in ["pe", "act", "dve", "pool"]:
            self.sem[n] = es.enter_context(nc.semaphore("s_" + n))
            self.cnt[n] = 0
            self.inc[n] = 1
        for i in range(self.NDMA):
            n = "d%d" % i
            self.sem[n] = es.enter_context(nc.semaphore("s_" + n))
            self.cnt[n] = 0
            self.inc[n] = 16
        self.waited = {e: {} for e in self.eng}
        self.pending = {e: [] for e in self.eng}
        self.dma_rr = 0
        self.n_inst = 0
        self.log = {e: [] for e in self.eng}

    def sb(self, name, shape, dt, es=None):
        self.uid = getattr(self, "uid", 0) + 1
        t = (es or self.es).enter_context(self.nc.sbuf_tensor("sb%d_%s" % (self.uid, name), shape, dt))
        return Buf(t, name)

    def ps(self, name, shape, dt, es=None):
        self.uid = getattr(self, "uid", 0) + 1
        t = (es or self.es).enter_context(self.nc.psum_tensor("ps%d_%s" % (self.uid, name), shape, dt))
        return Buf(t, name)

    def _wait(self, e, ref):
        if e == "pe" and ref.lane == "pe":
            return
        if ref.v is None:
            raise RuntimeError("dependency on unsignalled instruction (lane %s)" % ref.lane)
        if self.waited[e].get(ref.lane, 0) < ref.v:
            self.eng[e].wait_ge(self.sem[ref.lane], ref.v * self.inc[ref.lane])
            self.log[e].append(("w", ref.lane, ref.v))
            self.waited[e][ref.lane] = ref.v

    def _deps(self, e, reads, writes):
        need = {}
        for b in reads:
            r = b.last_w
            if r is not None:
                if r.lane not in need or (need[r.lane].v or 1 << 60) < (r.v or 1 << 60):
                    need[r.lane] = r
        for b in writes:
            cands = list(b.readers.values())
            if b.last_w is not None:
                cands.append(b.last_w)
            for r in cands:
                if r.lane not in need or (need[r.lane].v or 1 << 60) < (r.v or 1 << 60):
                    need[r.lane] = r
        for r in need.values():
            self._wait(e, r)

    def _record(self, ref, reads, writes):
        for b in reads:
            b.readers[ref.lane] = ref
        for b in writes:
            b.last_w = ref
            b.readers = {}

    def op(self, e, fn, reads, writes, signal=True):
        self._deps(e, reads, writes)
        inst = fn(self.eng[e])
        self.n_inst += 1
        if signal:
            self.cnt[e] += 1
            inst.then_inc(self.sem[e], 1)
            self.log[e].append(("s", e))
            ref = Ref(e, self.cnt[e])
            for p in self.pending[e]:
                p.v = ref.v
            self.pending[e] = []
        else:
            ref = Ref(e, None)
            self.pending[e].append(ref)
        self._record(ref, reads, writes)
        return ref

    def act(self, out, in_, func, reads, writes, **kw):
        return self.op("act", lambda e: e.activation(out=out, in_=in_, func=func, **kw), reads, writes)

    def dma(self, q, out, in_, reads, writes, **kw):
        lane = "d%d" % self.dma_rr
        self.dma_rr = (self.dma_rr + 1) % self.NDMA
        if self.cnt[lane] > 0:
            self._wait(q, Ref(lane, self.cnt[lane]))
        self._deps(q, reads, writes)
        inst = self.eng[q].dma_start(out=out, in_=in_, **kw)
        self.n_inst += 1
        self.cnt[lane] += 1
        inst.then_inc(self.sem[lane], 16)
        self.log[q].append(("s", lane))
        ref = Ref(lane, self.cnt[lane])
        self._record(ref, reads, writes)
        return ref

    def simulate(self):
        cnt = {l: 0 for l in self.sem}
        ptr = {e: 0 for e in self.eng}
        progress = True
        while progress:
            progress = False
            for e in self.eng:
                lg = self.log[e]
                while ptr[e] < len(lg):
                    it = lg[ptr[e]]
                    if it[0] == "w":
                        if cnt[it[1]] >= it[2]:
                            ptr[e] += 1
                            progress = True
                        else:
                            break
                    else:
                        cnt[it[1]] += 1
                        ptr[e] += 1
                        progress = True
        stuck = {e: (ptr[e], len(self.log[e]), self.log[e][ptr[e]] if ptr[e] < len(self.log[e]) else None) for e in self.eng}
        return all(ptr[e] == len(self.log[e]) for e in self.eng), stuck, cnt

    def barrier(self):
        for e in self.eng:
            if self.pending[e]:
                raise RuntimeError("barrier with pending unsignalled instr on " + e)
        for e in self.eng:
            for lane in self.sem:
                if self.cnt[lane] > 0:
                    self._wait(e, Ref(lane, self.cnt[lane]))


D = 1024
DIN = 2720
H = 8
CTX = 256
TM = 512
EPS = 1e-6
LN_EPS = 1e-5
SM_SCALE = 96 ** -0.5
OFF = dict(cq=0, ckv=256, kr=384, gA=416, scin=928, scB=1184, scC=1440, gB=1696, cfa=1952, cfg=2208, gC=2464)

MUL, ADD, SUB = ALU.mult, ALU.add, ALU.subtract


class Prog:
    def __init__(self, S, layers, n_layers_total=2, fused=False, stop=None):
        self.stop = stop
        self.S = S
        self.OWN = S // 2
        self.NKB = (CTX + S) // 128
        self.layers = layers
        self.fused = fused
        self.L = n_layers_total
        nc = self.nc = bass.Bass("TRN2", target_bir_lowering=False)
        self.es = ExitStack()
        self.fw = Fw(nc, self.es)
        self.din = {}
        self._declare_io()
        with self.es:
            self._alloc_persist()
            self._emit()

    def dram_in(self, name, shape, dt=F32):
        t = self.nc.dram_tensor(name, list(shape), dt, kind="ExternalInput").ap()
        self.din[name] = Buf(t, name)
        return self.din[name]

    def _declare_io(self):
        S, OWN, L = self.S, self.OWN, self.L
        nc = self.nc
        self.x_full = self.dram_in("x_full", [S, D])
        self.x_own = self.dram_in("x_own", [OWN, D])
        self.ctx_in = self.dram_in("ctx_in", [CTX, D])
        self.csil = self.dram_in("csil", [128, 8, 2])
        self.rope_k = self.dram_in("rope_k", [CTX + S, 32])
        self.rope_q = self.dram_in("rope_q", [CTX + OWN, 32])
        self.hmask = self.dram_in("hmask", [128, 2])
        self.ident_d = self.dram_in("ident", [128, 128])
        self.sel_d = self.dram_in("sel", [2, 256])
        self.norm_w = self.dram_in("norm_w", [L, D])
        self.w_mod = self.dram_in("w_mod", [L, D, 3 * D])
        self.b_mod = self.dram_in("b_mod", [L, 3 * D])
        self.w_in = self.dram_in("w_in", [L, D, DIN])
        self.qnw_col = self.dram_in("qnw_col", [L, 128, 2])
        self.w_uq = self.dram_in("w_uq", [L, 256, 768])
        self.kvnw_col = self.dram_in("kvnw_col", [L, 128, 1])
        self.w_ukv = self.dram_in("w_ukv", [L, 128, 1024])
        self.qhw = self.dram_in("qhw", [L, 96])
        self.khw = self.dram_in("khw", [L, 96])
        self.scw_col = self.dram_in("scw_col", [L, 128, 2, 3])
        self.cfw_col = self.dram_in("cfw_col", [L, 128, 2, 31])
        self.cfv_col = self.dram_in("cfv_col", [L, 128, 2, 4])
        self.cf_pw = self.dram_in("cf_pw", [L, 256, 256])
        self.w_out = self.dram_in("w_out", [L, D, D])
        self.y = Buf(nc.dram_tensor("y", [OWN, D], F32, kind="ExternalOutput").ap(), "y")
        self.yc = None
        if len(self.layers) == 1:
            self.yc = Buf(nc.dram_tensor("yc", [CTX, D], F32, kind="ExternalOutput").ap(), "yc")
        self.NCH = OWN // 512
        self.x1o_t = [nc.dram_tensor("x1o%d" % i, [512, D], F32) for i in range(self.NCH)]
        self.x1g_t = [nc.dram_tensor("x1g%d" % i, [1024, D], F32) for i in range(self.NCH)]
        self.x1o = [Buf(t.ap(), "x1o") for t in self.x1o_t]
        self.x1g = [Buf(t.ap(), "x1g") for t in self.x1g_t]
        self.xc1 = Buf(nc.dram_tensor("xc1", [CTX, D], F32).ap(), "xc1")
        self.w_in_bf = Buf(nc.dram_tensor("w_in_bf", [D, DIN], BF16, kind="Internal").ap(), "w_in_bf")
        self.w_out_bf = Buf(nc.dram_tensor("w_out_bf", [D, D], BF16, kind="Internal").ap(), "w_out_bf")

    def _alloc_persist(self):
        fw = self.fw
        sb, ps = fw.sb, fw.ps
        NKB = self.NKB
        NK = NKB * 128
        self.ident_f = sb("ident_f", [128, 128], F32)
        self.ident_b = sb("ident_b", [128, 128], BF16)
        self.ones_f = sb("ones_f", [128, 128], F32)
        self.E64 = sb("E64", [128, 64], F32)
        self.sel = sb("sel", [2, 256], F32)
        self.hm = sb("hm", [128, 2], F32)
        self.G = [sb("G%d" % i, [128, D], F32) for i in range(2)]
        self.Sh = [sb("Sh%d" % i, [128, D], F32) for i in range(2)]
        self.GATE = [sb("GATE%d" % i, [128, D], F32) for i in range(2)]
        self.w_kv = sb("w_kv", [128, 8, 160], BF16)
        self.w_cq = sb("w_cq", [128, 8, 256], BF16)
        self.w_uqb = sb("w_uqb", [128, 2, 768], BF16)
        self.Wkn1 = sb("Wkn1", [128, 512], BF16)
        self.Wkn2 = sb("Wkn2", [128, 512], BF16)
        self.Wv = sb("Wv", [128, 512], BF16)
        self.pwb = sb("pwb", [128, 2, 256], BF16)
        self.wq_b = sb("wq_b", [128, 96], F32)
        self.wk_b = sb("wk_b", [128, 96], F32)
        self.scw = sb("scw", [128, 2, 3], F32)
        self.cfw = sb("cfw", [128, 2, 31], F32)
        self.cfv = sb("cfv", [128, 2, 4], F32)
        self.wch = [sb("wch%d" % i, [128, 8, 128], BF16) for i in range(3)]
        self.woch = [sb("woch%d" % i, [128, D], BF16) for i in range(3)]
        self.ckvT_t = sb("ckvT", [128, NK], BF16)
        self.ckvT = [Buf(None, "ckvT%d" % j) for j in range(NKB)]
        self.KB_t = sb("KB", [96, NK], BF16)
        self.KBr = [Buf(None, "KBr%d" % j) for j in range(NKB)]
        self.KBn = [Buf(None, "KBn%d" % j) for j in range((NK + 511) // 512)]
        self.ALPHA = sb("ALPHA", [128, NKB, 8], F32)
        self.ALb = [Buf(None, "AL%d" % j) for j in range(NKB)]
        self.RCKV = sb("RCKV", [128, NKB], F32)
        self.RCb = [Buf(None, "RC%d" % j) for j in range(NKB)]
        self.V2 = sb("V2", [128, NKB, 2, 65], BF16)
        self.pT2 = ps("pT2", [128, 1024], BF16)
        self.pT = self.pT2
        self.pA = ps("pA", [128, 512], F32)
        self.pQ = [ps("pQ%d" % i, [128, 512], F32) for i in range(2)]
        self.pS = [ps("pS%d" % i, [128, 512], F32) for i in range(2)]
        self.pO = [ps("pO%d" % i, [128, 512], F32) for i in range(2)]

    def _alloc_prep(self, es):
        sb = lambda n, sh, dt: self.fw.sb(n, sh, dt, es)
        self.csl = sb("csl", [128, 8, 2], F32)
        self.wmc = [sb("wmc%d" % i, [128, 3 * D], F32) for i in range(2)]
        self.modrow = sb("modrow", [2, 3 * D], F32)
        self.bmod2 = sb("bmod2", [2, 3 * D], F32)
        self.nw2 = sb("nw2", [2, D], F32)
        self.wstage = [sb("wstage%d" % i, [128, DIN], F32) for i in range(2)]
        self.wcast = [sb("wcast%d" % i, [128, DIN], BF16) for i in range(2)]
        self.qnw = sb("qnw", [128, 2], F32)
        self.kvnw = sb("kvnw", [128, 1], F32)
        self.tmpw = sb("tmpw", [128, 1024], F32)

    def _alloc_main(self, es):
        sb = lambda n, sh, dt: self.fw.sb(n, sh, dt, es)
        EXT = TM + 256
        self.xt = [sb("xt%d" % i, [128, D], F32) for i in range(2)]
        self.hb = [sb("hb%d" % i, [128, D], BF16) for i in range(2)]
        self.sm = [sb("sm%d" % i, [128, 16], F32) for i in range(2)]
        self.ckv_tok = [sb("ckv_tok%d" % i, [128, 128], BF16) for i in range(2)]
        self.k_tok = [sb("k_tok%d" % i, [128, 96], BF16) for i in range(2)]
        self.krw = [sb("krw%d" % i, [128, 32], F32) for i in range(2)]
        self.rtmp = [sb("rtmp%d" % i, [128, 4, 16], F32) for i in range(2)]
        self.smk = [sb("smk%d" % i, [128, 16], F32) for i in range(3)]
        self.cs = [sb("cs%d" % i, [128, 32], F32) for i in range(2)]
        self.sqn = sb("sqn", [128, 768], F32)
        self.ssn = sb("ssn", [128, 16], F32)
        self.hTm_t = sb("hTm", [128, 8, EXT], BF16)
        self.hTm = [Buf(None, "hTm%d" % i) for i in range(EXT // 128)]
        self.QT = sb("QT", [96, 8, TM], BF16)
        self.GA = sb("GA", [64, 8, TM], BF16)
        self.GB = sb("GB", [128, 2, TM], BF16)
        self.GC = sb("GC", [128, 2, TM], BF16)
        self.SCB = sb("SCB", [128, 2, TM], BF16)
        self.YB = sb("YB", [128, 2, TM], BF16)
        self.YC = sb("YC", [128, 2, TM], BF16)
        self.ZA = sb("ZA", [128, 2, TM], BF16)
        self.F1 = sb("F1", [128, 2, EXT], BF16)
        self.F2 = sb("F2", [128, 2, EXT], BF16)
        self.F3 = sb("F3", [128, 2, TM], F32)
        self.ZSQ = sb("ZSQ", [128, 2, TM], F32)
        self.dg = [sb("dg%d" % i, [128, 128], BF16) for i in range(4)]
        self.MU = sb("MU", [128, TM], F32)
        self.RS = sb("RS", [128, TM], F32)
        self.cqn_tok = sb("cqn_tok", [128, 256], BF16)
        self.cqnT = sb("cqnT", [128, 2, 128], BF16)
        self.qf = sb("qf", [128, 8, 96], F32)
        self.q_tok = sb("q_tok", [128, 8, 96], BF16)
        self.qtmp = sb("qtmp", [128, 4, 8, 16], F32)
        self.PT = [sb("PT%d" % i, [128, 512], BF16) for i in range(4)]
        self.ONR = sb("ONR", [128, 512], F32)
        self.rb = sb("rb", [64, 512], F32)

    def mm(self, out, lhsT, rhs, start, stop, reads, writes, signal=True):
        return self.fw.op("pe", lambda e: e.matmul(out, lhsT=lhsT, rhs=rhs, start=start, stop=stop), reads, writes, signal)

    def tp(self, out, in_, ident, reads, writes, signal=True):
        return self.fw.op("pe", lambda e: e.transpose(out=out, in_=in_, identity=ident), reads, writes, signal)

    def tt(self, eng, out, in0, in1, op, reads, writes):
        return self.fw.op(eng, lambda e: e.tensor_tensor(out=out, in0=in0, in1=in1, op=op), reads, writes)

    def ts(self, eng, out, in0, s1, op0, reads, writes, s2=None, op1=None):
        if op1 is None:
            return self.fw.op(eng, lambda e: e.tensor_scalar(out=out, in0=in0, scalar1=s1, scalar2=None, op0=op0), reads, writes)
        return self.fw.op(eng, lambda e: e.tensor_scalar(out=out, in0=in0, scalar1=s1, scalar2=s2, op0=op0, op1=op1), reads, writes)

    def stt(self, eng, out, in0, scalar, in1, op0, op1, reads, writes):
        return self.fw.op(eng, lambda e: e.scalar_tensor_tensor(out=out, in0=in0, scalar=scalar, in1=in1, op0=op0, op1=op1), reads, writes)

    def cp(self, eng, out, in_, reads, writes):
        if eng == "act":
            return self.fw.act(out, in_, AF.Copy, reads, writes)
        return self.fw.op(eng, lambda e: e.tensor_copy(out=out, in_=in_), reads, writes)

    def rsqrt(self, out, in_, scale, eps, buf_in, buf_out, power=-0.5):
        self.fw.act(out, in_, AF.Ln, [buf_in], [buf_out], scale=scale, bias=eps)
        self.fw.act(out, out, AF.Exp, [buf_out], [buf_out], scale=power)

    def _emit(self):
        fw = self.fw
        fw.dma("sp", self.ident_f.t[:], self.ident_d.t[:, :], [], [self.ident_f])
        fw.dma("sp", self.sel.t[:], self.sel_d.t[:, :], [], [self.sel])
        fw.dma("sp", self.hm.t[:], self.hmask.t[:, :], [], [self.hm])
        self.cp("dve", self.ident_b.t[:], self.ident_f.t[:], [self.ident_f], [self.ident_b])
        fw.op("pool", lambda e: e.memset(self.ones_f.t[:], 1.0), [], [self.ones_f])
        fw.op("pool", lambda e: e.memset(self.E64.t[:], 0.0), [], [self.E64])
        fw.op("pool", lambda e: e.memset(self.E64.t[64:65, :], 1.0), [self.E64], [self.E64])
        self.V2g = [Buf(None, "V2g%d" % g) for g in range((self.NKB + 3) // 4)]
        fw.op("pool", lambda e: e.memset(self.V2.t[:, :, :, 64:65], 1.0), [], self.V2g)
        self.xi = 0
        for li, l in enumerate(self.layers):
            last = (l == self.L - 1)
            with ExitStack() as es:
                self._alloc_prep(es)
                self._emit_prep(l)
                fw.barrier()
            if self.stop == "prep":
                return
            first = (li == 0)
            OWN = self.OWN
            if first:
                xfull = lambda t0: (self.x_full.t[t0:t0 + 128, :], self.x_full)
                xown = lambda t0: (self.x_own.t[t0:t0 + 128, :], self.x_own)
                cx = lambda t0: (self.ctx_in.t[t0:t0 + 128, :], self.ctx_in)
            else:
                def xfull(t0):
                    half, o = t0 // OWN, t0 % OWN
                    ch, r = o // 512, half * 512 + o % 512
                    return (self.x1g[ch].t[r:r + 128, :], self.x1g[ch])
                xown = lambda t0: (self.x1o[t0 // 512].t[t0 % 512:t0 % 512 + 128, :], self.x1o[t0 // 512])
                cx = lambda t0: (self.xc1.t[t0:t0 + 128, :], self.xc1)
            if last or len(self.layers) == 1:
                yd = lambda t0: (self.y.t[t0:t0 + 128, :], self.y)
                ycd = lambda t0: (self.yc.t[t0:t0 + 128, :], self.yc)
            else:
                yd = lambda t0: (self.x1o[t0 // 512].t[t0 % 512:t0 % 512 + 128, :], self.x1o[t0 // 512])
                ycd = lambda t0: (self.xc1.t[t0:t0 + 128, :], self.xc1)
            with ExitStack() as es:
                self._alloc_main(es)
                self._emit_main(l, last, xfull, xown, cx, yd, ycd)
                fw.barrier()
            if len(self.layers) > 1 and not last:
                fw.sem["cc"] = self.es.enter_context(self.nc.semaphore("s_cc"))
                fw.inc["cc"] = 1
                for i in range(self.NCH):
                    inst = self.nc.gpsimd.collective_compute(
                        "AllGather", ALU.bypass, replica_groups=[[0, 1], [2, 3], [4, 5], [6, 7]],
                        ins=[self.x1o_t[i].ap().opt()], outs=[self.x1g_t[i].ap().opt()])
                    inst.then_inc(fw.sem["cc"])
                    fw.log["pool"].append(("s", "cc"))
                    self.x1g[i].last_w = Ref("cc", i + 1)
                    self.x1g[i].readers = {}
                fw.cnt["cc"] = self.NCH
                fw.barrier()

    def _emit_prep(self, l):
        fw = self.fw
        banks = [self.pA, self.pQ[0], self.pQ[1], self.pS[0], self.pS[1], self.pO[0]]
        csl = self.csl
        fw.dma("sp", csl.t[:], self.csil.t[:, :, :], [], [csl])
        fw.act(csl.t[:], csl.t[:], AF.Silu, [csl], [csl])
        for r in range(2):
            fw.dma("sp", self.bmod2.t[r:r + 1, :], self.b_mod.t[l:l + 1, :], [], [self.bmod2])
            fw.dma("sp", self.nw2.t[r:r + 1, :], self.norm_w.t[l:l + 1, :], [], [self.nw2])
        for k in range(8):
            wm = self.wmc[k % 2]
            fw.dma("sp", wm.t[:], self.w_mod.t[l, k * 128:(k + 1) * 128, :], [], [wm])
            for j in range(6):
                self.mm(banks[j].t[0:2, :], csl.t[:, k, :], wm.t[:, j * 512:(j + 1) * 512], k == 0, k == 7,
                        [csl, wm], [banks[j]], signal=(j == 5 or k == 7))
        mr = self.modrow
        for j in range(6):
            self.tt("dve", mr.t[:, j * 512:(j + 1) * 512], banks[j].t[0:2, :], self.bmod2.t[:, j * 512:(j + 1) * 512], ADD,
                    [banks[j], self.bmod2], [mr])
        self.stt("dve", mr.t[:, D:2 * D], mr.t[:, D:2 * D], 1.0, self.nw2.t[:, :], ADD, MUL, [mr, self.nw2], [mr])
        bi = 0
        for v in range(2):
            for dst, c0 in ((self.Sh[v], 0), (self.G[v], D), (self.GATE[v], 2 * D)):
                for hf in range(2):
                    bk = banks[bi % 6]
                    bi += 1
                    self.mm(bk.t[:, :], self.sel.t[0:2, v * 128:(v + 1) * 128], mr.t[0:2, c0 + hf * 512:c0 + (hf + 1) * 512],
                            True, True, [self.sel, mr], [bk])
                    self.cp("act", dst.t[:, hf * 512:(hf + 1) * 512], bk.t[:, :], [bk], [dst])
        for k in range(8):
            st, wc = self.wstage[k % 2], self.wcast[k % 2]
            fw.dma("sp", st.t[:], self.w_in.t[l, k * 128:(k + 1) * 128, :], [], [st])
            self.cp("dve", wc.t[:, 0:1360], st.t[:, 0:1360], [st], [wc])
            self.cp("pool", wc.t[:, 1360:DIN], st.t[:, 1360:DIN], [st], [wc])
            fw.dma("sp", self.w_in_bf.t[k * 128:(k + 1) * 128, :], wc.t[:], [wc], [self.w_in_bf])
            self.cp("pool", self.w_kv.t[:, k, :], wc.t[:, 256:416], [wc], [self.w_kv])
            self.cp("pool", self.w_cq.t[:, k, :], wc.t[:, 0:256], [wc], [self.w_cq])
        for k in range(8):
            st, wc = self.wstage[k % 2], self.wcast[k % 2]
            fw.dma("sp", st.t[:, 0:D], self.w_out.t[l, k * 128:(k + 1) * 128, :], [], [st])
            self.cp("dve", wc.t[:, 0:D], st.t[:, 0:D], [st], [wc])
            fw.dma("sp", self.w_out_bf.t[k * 128:(k + 1) * 128, :], wc.t[:, 0:D], [wc], [self.w_out_bf])
        fw.dma("sp", self.qnw.t[:], self.qnw_col.t[l, :, :], [], [self.qnw])
        fw.dma("sp", self.kvnw.t[:], self.kvnw_col.t[l, :, :], [], [self.kvnw])
        for c in range(2):
            st = self.wstage[c]
            fw.dma("sp", st.t[:, 0:768], self.w_uq.t[l, c * 128:(c + 1) * 128, :], [], [st])
            self.ts("dve", self.w_uqb.t[:, c, :], st.t[:, 0:768], self.qnw.t[:, c:c + 1], MUL, [st, self.qnw], [self.w_uqb])
        fw.dma("sp", self.wk_b.t[:], self.khw.t[l, :].partition_broadcast(128), [], [self.wk_b])
        fw.dma("sp", self.wq_b.t[:], self.qhw.t[l, :].partition_broadcast(128), [], [self.wq_b])
        tw = self.tmpw
        fw.dma("sp", tw.t[:], self.w_ukv.t[l, :, :], [], [tw])
        twv = tw.t[:].rearrange("p (h t d) -> p h t d", h=8, t=2)
        v3 = lambda b: b.t[:].rearrange("p (h d) -> p h d", h=8)
        self.ts("dve", v3(self.Wkn1), twv[:, :, 0, :], self.kvnw.t[:, 0:1], MUL, [tw, self.kvnw], [self.Wkn1])
        self.ts("dve", v3(self.Wv), twv[:, :, 1, :], self.kvnw.t[:, 0:1], MUL, [tw, self.kvnw], [self.Wv])
        self.stt("dve", v3(self.Wkn2), twv[:, :, 0, :], self.kvnw.t[:, 0:1],
                 self.wk_b.t[:, 0:64].unsqueeze(1).to_broadcast([128, 8, 64]), MUL, MUL, [tw, self.kvnw, self.wk_b], [self.Wkn2])
        for c in range(2):
            st = self.wstage[c]
            fw.dma("sp", st.t[:, 0:256], self.cf_pw.t[l, c * 128:(c + 1) * 128, :], [], [st])
            self.cp("dve", self.pwb.t[:, c, :], st.t[:, 0:256], [st], [self.pwb])
        fw.dma("sp", self.scw.t[:], self.scw_col.t[l, :, :, :], [], [self.scw])
        fw.dma("sp", self.cfw.t[:], self.cfw_col.t[l, :, :, :], [], [self.cfw])
        fw.dma("sp", self.cfv.t[:], self.cfv_col.t[l, :, :, :], [], [self.cfv])

    def x_to_hT(self, src_ap, src_buf, v, mask_ap, dst_buf, dst_view):
        fw = self.fw
        i = self.xi % 2
        self.xi += 1
        self.last_par = i
        xt, hb, sm = self.xt[i], self.hb[i], self.sm[i]
        fw.dma("sp", xt.t[:], src_ap, [src_buf], [xt])
        fw.act(hb.t[:], xt.t[:], AF.Square, [xt], [hb, sm], accum_out=sm.t[:, 0:1])
        self.rsqrt(sm.t[:, 1:2], sm.t[:, 0:1], 1.0 / D, EPS, sm, sm)
        self.stt("dve", xt.t[:], xt.t[:], sm.t[:, 1:2], self.G[v].t[:], MUL, MUL, [xt, sm, self.G[v]], [xt])
        self.tt("dve", hb.t[:], xt.t[:], self.Sh[v].t[:], ADD, [xt, self.Sh[v]], [hb])
        if mask_ap is not None:
            self.ts("dve", hb.t[:], hb.t[:], mask_ap, MUL, [hb, self.hm], [hb])
        for k in range(8):
            self.tp(self.pT.t[:, k * 128:(k + 1) * 128], hb.t[:, k * 128:(k + 1) * 128], self.ident_b.t[:],
                    [hb, self.ident_b], [self.pT], signal=(k == 7))
        self.cp("dve" if getattr(self, "in_attn", False) else "act", dst_view, self.pT.t[:].rearrange("p (k t) -> p k t", k=8), [self.pT], [dst_buf])

    def phase_k_A(self, j, src_ap, src_buf, v):
        fw = self.fw
        slot = j % len(self.hTm)
        hTb = self.hTm[slot]
        hT = self.hTm_t.t[:, :, slot * 128:(slot + 1) * 128]
        self.x_to_hT(src_ap, src_buf, v, None, hTb, hT)
        p = j % 2
        cs, sm = self.cs[p], self.smk[j % 3]
        fw.dma("sp", cs.t[:], self.rope_k.t[j * 128:(j + 1) * 128, :], [], [cs])
        pA = self.pA
        for k in range(8):
            self.mm(pA.t[:, 0:160], hT[:, k, :], self.w_kv.t[:, k, :], k == 0, k == 7, [hTb, self.w_kv], [pA], signal=(k == 7))
        sq = self.qf.t[:].rearrange("p h d -> p (h d)")
        fw.act(sq[:, 0:160], pA.t[:, 0:160], AF.Square, [pA], [self.qf])
        fw.op("dve", lambda e: e.tensor_reduce(out=sm.t[:, 2:3], in_=sq[:, 0:128], axis=AX.X, op=ADD), [self.qf], [sm])
        fw.op("dve", lambda e: e.tensor_reduce(out=sm.t[:, 3:4], in_=sq[:, 128:160], axis=AX.X, op=ADD), [self.qf], [sm])
        self.cp("dve", self.ckv_tok[p].t[:], pA.t[:, 0:128], [pA], [self.ckv_tok[p]])
        krw, rt = self.krw[p], self.rtmp[p]
        self.tt("dve", krw.t[:], pA.t[:, 128:160], self.wk_b.t[:, 64:96], MUL, [pA, self.wk_b], [krw])
        x1, x2, cos, sin = krw.t[:, 0:16], krw.t[:, 16:32], cs.t[:, 0:16], cs.t[:, 16:32]
        self.tt("dve", rt.t[:, 0, :], x1, cos, MUL, [krw, cs], [rt])
        self.tt("dve", rt.t[:, 1, :], x2, sin, MUL, [krw, cs], [rt])
        self.tt("dve", rt.t[:, 2, :], x2, cos, MUL, [krw, cs], [rt])
        self.tt("dve", rt.t[:, 3, :], x1, sin, MUL, [krw, cs], [rt])
        self.tt("dve", krw.t[:, 0:16], rt.t[:, 0, :], rt.t[:, 1, :], SUB, [rt], [krw])
        self.tt("dve", krw.t[:, 16:32], rt.t[:, 2, :], rt.t[:, 3, :], ADD, [rt], [krw])

    def phase_k_B(self, j):
        fw = self.fw
        p = j % 2
        sm, krw, k_tok, ckv_tok = self.smk[j % 3], self.krw[p], self.k_tok[p], self.ckv_tok[p]
        fw.act(sm.t[:, 4:5], sm.t[:, 2:3], AF.Ln, [sm], [sm], scale=1.0 / 128, bias=EPS)
        fw.act(sm.t[:, 5:6], sm.t[:, 4:5], AF.Exp, [sm], [sm], scale=0.5)
        fw.act(self.RCKV.t[:, j:j + 1], sm.t[:, 4:5], AF.Exp, [sm], [self.RCb[j]], scale=-0.5)
        self.ts("dve", k_tok.t[:, 64:96], krw.t[:], sm.t[:, 5:6], MUL, [krw, sm], [k_tok])
        self.tp(self.pT2.t[:, 0:128], ckv_tok.t[:], self.ident_b.t[:], [ckv_tok, self.ident_b], [self.pT2], signal=False)
        self.tp(self.pT2.t[0:96, 128:256], k_tok.t[:], self.ident_b.t[:], [k_tok, self.ident_b], [self.pT2])
        self.cp("dve", self.ckvT_t.t[:, j * 128:(j + 1) * 128], self.pT2.t[:, 0:128], [self.pT2], [self.ckvT[j]])
        self.cp("act", self.KB_t.t[64:96, j * 128:(j + 1) * 128], self.pT2.t[64:96, 128:256], [self.pT2], [self.KBr[j]])

    def phase_k_C(self, j):
        fw = self.fw
        sm = self.smk[j % 3]
        pq = self.pQ[j % 2]
        self.mm(pq.t[:, :], self.ckvT_t.t[:, j * 128:(j + 1) * 128], self.Wkn1.t[:, :], True, True, [self.ckvT[j], self.Wkn1], [pq])
        fw.act(self.sqn.t[:, 0:512], pq.t[:, :], AF.Square, [pq], [self.sqn])
        fw.op("dve", lambda e: e.tensor_reduce(out=self.ssn.t[:, 0:8], in_=self.sqn.t[:, 0:512].rearrange("p (h d) -> p h d", h=8),
                                               axis=AX.X, op=ADD), [self.sqn], [self.ssn])
        self.tt("dve", sm.t[:, 6:7], self.RCKV.t[:, j:j + 1], self.RCKV.t[:, j:j + 1], MUL, [self.RCb[j]], [sm])
        self.ts("dve", self.ssn.t[:, 0:8], self.ssn.t[:, 0:8], sm.t[:, 6:7], MUL, [self.ssn, sm], [self.ssn], s2=sm.t[:, 3:4], op1=ADD)
        self.rsqrt(self.ssn.t[:, 8:16], self.ssn.t[:, 0:8], 1.0 / 96, EPS, self.ssn, self.ssn)
        self.ts("dve", self.ALPHA.t[:, j, :], self.ssn.t[:, 8:16], self.RCKV.t[:, j:j + 1], MUL, [self.ssn, self.RCb[j]], [self.ALb[j]],
                s2=SM_SCALE, op1=MUL)

    def fill_hT(self, e, s, v):
        view = self.hTm_t.t[:, :, e * 128:(e + 1) * 128]
        if s is None:
            self.fw.op("pool", lambda en: en.memset(view, 0.0), [], [self.hTm[e]])
        else:
            self.x_to_hT(s[0], s[1], v, s[2], self.hTm[e], view)

    def macro(self, ext_srcs, q_rope_row0, key_blocks, v, x_res, y_dst, next_srcs=None, next_v=0):
        fw = self.fw
        nblk = len(ext_srcs) - 2
        ntok = nblk * 128
        ext = ntok + 256
        hTt = self.hTm_t.t
        if not self.hT_prefilled:
            for e, s in enumerate(ext_srcs):
                self.fill_hT(e, s, v)
        self.hT_prefilled = False
        self.next_srcs, self.next_v = next_srcs, next_v
        hT_all = self.hTm[:len(ext_srcs)]
        hT_c = self.hTm[1:1 + nblk]
        QTb = Buf(None, "QTb")
        self.QTb = QTb
        for b in range(nblk):
            e = b + 1
            hT = hTt[:, :, e * 128:(e + 1) * 128]
            sm = self.sm[b % 2]
            cs = self.cs[b % 2]
            fw.dma("sp", cs.t[:], self.rope_q.t[q_rope_row0 + b * 128:q_rope_row0 + (b + 1) * 128, :], [], [cs])
            pA = self.pA
            for k in range(8):
                self.mm(pA.t[:, 0:256], hT[:, k, :], self.w_cq.t[:, k, :], k == 0, k == 7, [self.hTm[e], self.w_cq], [pA], signal=(k == 7))
            fw.act(self.sqn.t[:, 0:256], pA.t[:, 0:256], AF.Square, [pA], [self.sqn, sm], accum_out=sm.t[:, 8:9])
            self.rsqrt(sm.t[:, 9:10], sm.t[:, 8:9], 1.0 / 256, EPS, sm, sm)
            fw.act(self.cqn_tok.t[:], pA.t[:, 0:256], AF.Copy, [pA, sm], [self.cqn_tok], scale=sm.t[:, 9:10])
            for c in range(2):
                self.tp(self.pT2.t[:, c * 128:(c + 1) * 128], self.cqn_tok.t[:, c * 128:(c + 1) * 128], self.ident_b.t[:],
                        [self.cqn_tok, self.ident_b], [self.pT2], signal=(c == 1))
            self.cp("dve", self.cqnT.t[:].rearrange("p c t -> p (c t)"), self.pT2.t[:, 0:256], [self.pT2], [self.cqnT])
            for (bk, c0, c1) in ((self.pQ[0], 0, 480), (self.pQ[1], 480, 768)):
                for c in range(2):
                    self.mm(bk.t[:, 0:c1 - c0], self.cqnT.t[:, c, :], self.w_uqb.t[:, c, c0:c1], c == 0, c == 1,
                            [self.cqnT, self.w_uqb], [bk], signal=(c == 1))
                fw.act(self.sqn.t[:, c0:c1], bk.t[:, 0:c1 - c0], AF.Square, [bk], [self.sqn])
            fw.op("dve", lambda en: en.tensor_reduce(out=self.ssn.t[:, 0:8], in_=self.sqn.t[:, 0:768].rearrange("p (h d) -> p h d", h=8),
                                                     axis=AX.X, op=ADD), [self.sqn], [self.ssn])
            self.rsqrt(self.ssn.t[:, 8:16], self.ssn.t[:, 0:8], 1.0 / 96, EPS, self.ssn, self.ssn)
            qf = self.qf
            self.tt("dve", qf.t[:, 0:5, :], self.pQ[0].t[:, 0:480].rearrange("p (h d) -> p h d", h=5),
                    self.ssn.t[:, 8:13].unsqueeze(2).to_broadcast([128, 5, 96]), MUL, [self.pQ[0], self.ssn], [qf])
            self.tt("dve", qf.t[:, 5:8, :], self.pQ[1].t[:, 0:288].rearrange("p (h d) -> p h d", h=3),
                    self.ssn.t[:, 13:16].unsqueeze(2).to_broadcast([128, 3, 96]), MUL, [self.pQ[1], self.ssn], [qf])
            self.tt("dve", qf.t[:, :, :], qf.t[:, :, :], self.wq_b.t[:, :].unsqueeze(1).to_broadcast([128, 8, 96]), MUL, [qf, self.wq_b], [qf])
            qt_, tmp = self.q_tok, self.qtmp
            self.cp("pool", qt_.t[:, :, 0:64], qf.t[:, :, 0:64], [qf], [qt_])
            x1, x2 = qf.t[:, :, 64:80], qf.t[:, :, 80:96]
            cos = cs.t[:, 0:16].unsqueeze(1).to_broadcast([128, 8, 16])
            sin = cs.t[:, 16:32].unsqueeze(1).to_broadcast([128, 8, 16])
            self.tt("dve", tmp.t[:, 0, :, :], x1, cos, MUL, [qf, cs], [tmp])
            self.tt("dve", tmp.t[:, 1, :, :], x2, sin, MUL, [qf, cs], [tmp])
            self.tt("dve", tmp.t[:, 2, :, :], x2, cos, MUL, [qf, cs], [tmp])
            self.tt("dve", tmp.t[:, 3, :, :], x1, sin, MUL, [qf, cs], [tmp])
            self.tt("dve", qt_.t[:, :, 64:80], tmp.t[:, 0, :, :], tmp.t[:, 1, :, :], SUB, [tmp], [qt_])
            self.tt("dve", qt_.t[:, :, 80:96], tmp.t[:, 2, :, :], tmp.t[:, 3, :, :], ADD, [tmp], [qt_])
            for h in range(H):
                self.tp(self.pT2.t[0:96, h * 128:(h + 1) * 128], qt_.t[:, h, :], self.ident_b.t[:], [qt_, self.ident_b], [self.pT2],
                        signal=(h == H - 1))
            self.cp("act", self.QT.t[:, :, b * 128:(b + 1) * 128], self.pT2.t[0:96, :].rearrange("p (h t) -> p h t", h=8), [self.pT2], [QTb])

        if self.stop == "macro_b":
            return
        banks = [self.pQ[0], self.pQ[1], self.pS[0], self.pS[1]]
        self._bk = getattr(self, "_bk", 0)
        self._wc = getattr(self, "_wc", 0)
        w_in_v = self.w_in_bf.t.rearrange("(k p) n -> p k n", p=128)

        def load_chunk(c0):
            wch = self.wch[self._wc % 3]
            self._wc += 1
            fw.dma("sp", wch.t[:], w_in_v[:, :, c0:c0 + 128], [self.w_in_bf], [wch])
            return wch

        def proj(wch, m0, m1, t0, tw, hbufs):
            bk = banks[self._bk % 4]
            self._bk += 1
            for k in range(8):
                self.mm(bk.t[0:m1 - m0, 0:tw], wch.t[:, k, m0:m1], hTt[:, k, t0:t0 + tw], k == 0, k == 7, [wch] + hbufs, [bk], signal=(k == 7))
            return bk

        ext_tiles = [(t0, min(512, ext - t0)) for t0 in range(0, ext, 512)]
        F1, F2, F3 = self.F1, self.F2, self.F3
        for name in ("scin", "scC", "cfg", "cfa"):
            for c in range(2):
                wch = load_chunk(OFF[name] + c * 128)
                for (t0, tw) in ext_tiles:
                    bk = proj(wch, 0, 128, t0, tw, hT_all)
                    if name == "scin":
                        self.cp("act", F1.t[:, c, t0:t0 + tw], bk.t[:, 0:tw], [bk], [F1])
                    elif name == "scC":
                        self.tt("dve", F1.t[:, c, t0:t0 + tw], bk.t[:, 0:tw], F1.t[:, c, t0:t0 + tw], MUL, [bk, F1], [F1])
                    elif name == "cfg":
                        fw.act(F2.t[:, c, t0:t0 + tw], bk.t[:, 0:tw], AF.Sigmoid, [bk], [F2])
                    else:
                        self.tt("dve", F2.t[:, c, t0:t0 + tw], bk.t[:, 0:tw], F2.t[:, c, t0:t0 + tw], MUL, [bk, F2], [F2])
        for name, dst, fn in (("scB", self.SCB, None), ("gB", self.GB, AF.Silu), ("gC", self.GC, AF.Silu)):
            for c in range(2):
                wch = load_chunk(OFF[name] + c * 128)
                bk = proj(wch, 0, 128, 128, ntok, hT_c)
                if fn is None:
                    self.cp("dve", dst.t[:, c, 0:ntok], bk.t[:, 0:ntok], [bk], [dst])
                else:
                    fw.act(dst.t[:, c, 0:ntok], bk.t[:, 0:ntok], fn, [bk], [dst])
        GAb = [Buf(None, "GA%d" % h) for h in range(H)]
        for c in range(4):
            wch = load_chunk(OFF["gA"] + c * 128)
            for hh in range(2):
                h = 2 * c + hh
                bk = proj(wch, hh * 64, hh * 64 + 64, 128, ntok, hT_c)
                fw.act(self.GA.t[0:64, h, 0:ntok], bk.t[0:64, 0:ntok], AF.Silu, [bk], [GAb[h]])

        self._dg = getattr(self, "_dg", 0)

        def dwconv(src, c, wcol, ntap, off0):
            bk = banks[self._bk % 4]
            self._bk += 1
            for j in range(ntap):
                dg = self.dg[self._dg % 4]
                self._dg += 1
                self.ts("dve", dg.t[:], self.ident_b.t[:], wcol[:, c, j:j + 1], MUL, [self.ident_b, self.scw, self.cfw], [dg])
                self.mm(bk.t[:, 0:ntok], dg.t[:], src.t[:, c, off0 + j:off0 + j + ntok], j == 0, j == ntap - 1, [dg, src], [bk],
                        signal=True)
            return bk

        accs = [self.MU, self.RS]
        for c in range(2):
            bk = dwconv(F1, c, self.scw.t, 3, 127)
            a = accs[c].t[:, 0:ntok]
            self.tt("dve", a, bk.t[:, 0:ntok], self.SCB.t[:, c, 0:ntok], MUL, [bk, self.SCB], [accs[c]])
            self.tt("dve", self.YB.t[:, c, 0:ntok], a, self.GB.t[:, c, 0:ntok], MUL, [accs[c], self.GB], [self.YB])
        for c in range(2):
            bk = dwconv(F2, c, self.cfw.t, 31, 113)
            self.ts("dve", F3.t[:, c, 0:ntok], bk.t[:, 0:ntok], self.cfv.t[:, c, 0:1], ADD, [bk, self.cfv], [F3])
        ZSQ = self.ZSQ
        for c in range(2):
            fw.act(ZSQ.t[:, c, 0:ntok], F3.t[:, c, 0:ntok], AF.Square, [F3], [ZSQ])
        for c in range(2):
            self.mm(self.pS[0].t[:, 0:ntok], self.ones_f.t[:, :], F3.t[:, c, 0:ntok], c == 0, c == 1, [self.ones_f, F3], [self.pS[0]], signal=(c == 1))
        for c in range(2):
            self.mm(self.pS[1].t[:, 0:ntok], self.ones_f.t[:, :], ZSQ.t[:, c, 0:ntok], c == 0, c == 1, [self.ones_f, ZSQ], [self.pS[1]], signal=(c == 1))
        MU, RS = self.MU, self.RS
        musq = self.sqn.t[:, 0:ntok]
        fw.act(MU.t[:, 0:ntok], self.pS[0].t[:, 0:ntok], AF.Copy, [self.pS[0]], [MU], scale=1.0 / 256)
        self.tt("dve", musq, MU.t[:, 0:ntok], MU.t[:, 0:ntok], MUL, [MU], [self.sqn])
        self.stt("dve", RS.t[:, 0:ntok], self.pS[1].t[:, 0:ntok], 1.0 / 256, musq, MUL, SUB, [self.pS[1], self.sqn], [RS])
        self.rsqrt(RS.t[:, 0:ntok], RS.t[:, 0:ntok], 1.0, LN_EPS, RS, RS)
        for c in range(2):
            z = F3.t[:, c, 0:ntok]
            self.tt("dve", z, z, MU.t[:, 0:ntok], SUB, [F3, MU], [F3])
            self.tt("dve", z, z, RS.t[:, 0:ntok], MUL, [F3, RS], [F3])
            fw.act(self.ZA.t[:, c, 0:ntok], z, AF.Silu, [F3, self.cfv], [self.ZA], scale=self.cfv.t[:, c, 1:2], bias=self.cfv.t[:, c, 2:3])
        for oc in range(2):
            bk = banks[self._bk % 4]
            self._bk += 1
            for c in range(2):
                self.mm(bk.t[:, 0:ntok], self.pwb.t[:, c, oc * 128:(oc + 1) * 128], self.ZA.t[:, c, 0:ntok], c == 0, c == 1,
                        [self.pwb, self.ZA], [bk], signal=(c == 1))
            self.stt("dve", self.YC.t[:, oc, 0:ntok], bk.t[:, 0:ntok], self.cfv.t[:, oc, 3:4], self.GC.t[:, oc, 0:ntok], ADD, MUL,
                     [bk, self.cfv, self.GC], [self.YC])

        if self.stop == "macro_g":
            return
        self.attention(ntok, key_blocks, GAb)
        if self.stop == "attn":
            return

        self.out_proj(nblk, v, GAb, x_res, y_dst)

    def build_K_tile(self, h, kt, nkeys):
        k0 = kt * 512
        w = min(512, nkeys - k0)
        bk = self.pQ[0]
        blks = [self.ckvT[j] for j in range(k0 // 128, (k0 + w) // 128)]
        self.mm(bk.t[0:64, 0:w], self.Wkn2.t[:, h * 64:(h + 1) * 64], self.ckvT_t.t[:, k0:k0 + w], True, True, [self.Wkn2] + blks, [bk])
        self.cp("dve", self.KB_t.t[0:64, k0:k0 + w], bk.t[0:64, 0:w], [bk], [self.KBn[kt]])

    def build_V_group(self, hp, g, nkb):
        g0 = g * 4
        ng = min(4, nkb - g0)
        for q in range(ng):
            j = g0 + q
            self.mm(self.pA.t[:, q * 128:(q + 1) * 128], self.ckvT_t.t[:, j * 128:(j + 1) * 128], self.Wv.t[:, hp * 64:hp * 64 + 128],
                    True, True, [self.ckvT[j], self.Wv], [self.pA], signal=(q == ng - 1))
        self.tt("dve", self.V2.t[:, g0:g0 + ng, :, 0:64],
                self.pA.t[:, 0:ng * 128].rearrange("p (g t d) -> p g t d", g=ng, t=2),
                self.RCKV.t[:, g0:g0 + ng].unsqueeze(2).unsqueeze(3).to_broadcast([128, ng, 2, 64]), MUL,
                [self.pA] + [self.RCb[j] for j in range(g0, g0 + ng)], [self.V2g[g]])

    def attention(self, ntok, key_blocks, GAb):
        fw = self.fw
        nkb = len(key_blocks)
        assert key_blocks == list(range(nkb))
        nkeys = nkb * 128
        ngrp = (nkb + 3) // 4
        self._pt = getattr(self, "_pt", 0)
        self._ps = getattr(self, "_ps", 0)
        self._po = getattr(self, "_po", 0)
        qtiles = [(q0, min(512, ntok - q0)) for q0 in range(0, ntok, 512)]
        import os
        INTER = os.environ.get("KV_INTERLEAVE", "1") == "1"
        if self.kv_ready != nkb and INTER:
            for kt in range(ngrp):
                self.build_K_tile(0, kt, nkeys)
                self.build_V_group(0, kt, nkb)
        self.kv_ready = None
        pS3 = [self.pS[0], self.pS[1], self.pQ[1]]
        pend = []

        def make_epilogue(h, q0, w, pO):
            def epi():
                ONR = self.ONR
                fw.op("dve", lambda e: e.reciprocal(out=ONR.t[64:65, 0:w], in_=pO.t[64:65, 0:w]), [pO], [ONR])
                self.mm(self.pA.t[0:64, 0:w], self.E64.t[:, :], ONR.t[:, 0:w], True, True, [self.E64, ONR], [self.pA])
                self.cp("dve", self.rb.t[:, 0:w], self.pA.t[0:64, 0:w], [self.pA], [self.rb])
                self.tt("dve", ONR.t[0:64, 0:w], pO.t[0:64, 0:w], self.rb.t[:, 0:w], MUL, [pO, self.rb], [ONR])
                self.tt("dve", self.GA.t[0:64, h, q0:q0 + w], ONR.t[0:64, 0:w], self.GA.t[0:64, h, q0:q0 + w], MUL, [ONR, GAb[h]], [GAb[h]])
            return epi

        for h in range(H):
            nh = (h + 1) % H
            if not INTER:
                for kt in range(ngrp):
                    self.build_K_tile(h, kt, nkeys)
                    if h % 2 == 0:
                        self.build_V_group(h, kt, nkb)
            for qi, (q0, w) in enumerate(qtiles):
                prefetch = (qi == len(qtiles) - 1)
                pO = self.pO[self._po % 2]
                self._po += 1

                def s_mm(i):
                    j = key_blocks[i]
                    ps_ = pS3[(self._ps + i) % 3]
                    self.mm(ps_.t[:, 0:w], self.KB_t.t[0:96, j * 128:(j + 1) * 128], self.QT.t[0:96, h, q0:q0 + w], True, True,
                            [self.KBn[j // 4], self.KBr[j], self.QTb], [ps_])
                s_mm(0)
                if nkb > 1:
                    s_mm(1)
                for i in range(nkb):
                    j = key_blocks[i]
                    if i + 2 < nkb:
                        s_mm(i + 2)
                    ps_ = pS3[(self._ps + i) % 3]
                    pt = self.PT[self._pt % 4]
                    self._pt += 1
                    fw.act(pt.t[:, 0:w], ps_.t[:, 0:w], AF.Exp, [ps_, self.ALb[j]], [pt], scale=self.ALPHA.t[:, j, h:h + 1])
                    last_i = (i == nkb - 1)
                    self.mm(pO.t[0:65, 0:w], self.V2.t[:, j, h % 2, :], pt.t[:, 0:w], i == 0, last_i, [self.V2g[j // 4], pt], [pO],
                            signal=last_i)
                    if INTER and prefetch and (i % 4 == 3 or last_i):
                        kt = i // 4
                        self.build_K_tile(nh, kt, nkeys)
                        if h % 2 == 1:
                            self.build_V_group((h + 1) % H, kt, nkb)
                    if pend and i == min(6, nkb - 1):
                        pend.pop(0)()
                self._ps += nkb
                pend.append(make_epilogue(h, q0, w, pO))
            if self.next_srcs is not None and h < len(self.next_srcs):
                self.in_attn = False
                self.fill_hT(h, self.next_srcs[h], self.next_v)
                self.in_attn = False
        while pend:
            pend.pop(0)()
        if self.next_srcs is not None:
            for e in range(H, len(self.next_srcs)):
                self.fill_hT(e, self.next_srcs[e], self.next_v)
            self.hT_prefilled = True
        self.kv_ready = nkb if INTER else None

    def out_proj(self, nblk, v, GAb, x_res, y_dst):
        fw = self.fw
        self._wo = getattr(self, "_wo", 0)
        chunks = [("a", h) for h in range(H)] + [("b", 0), ("b", 1), ("c", 0), ("c", 1)]
        for p0 in range(0, nblk, 2):
            blks = list(range(p0, min(p0 + 2, nblk)))
            if p0 == 0:
                bankset = {blks[0]: (self.pS[0], self.pS[1])}
                if len(blks) > 1:
                    bankset[blks[1]] = (self.pQ[0], self.pQ[1])
            else:
                bankset = {blks[0]: (self.pO[0], self.pO[1])}
                if len(blks) > 1:
                    bankset[blks[1]] = (self.pA, self.pS[0])
            for ci, (kind, idx) in enumerate(chunks):
                wo = self.woch[self._wo % 3]
                self._wo += 1
                if kind == "a":
                    K = 64
                    r0 = idx * 64
                    ybuf, yv = GAb[idx], (lambda b: self.GA.t[0:64, idx, b * 128:(b + 1) * 128])
                elif kind == "b":
                    K = 128
                    r0 = 512 + idx * 128
                    ybuf, yv = self.YB, (lambda b: self.YB.t[:, idx, b * 128:(b + 1) * 128])
                else:
                    K = 128
                    r0 = 768 + idx * 128
                    ybuf, yv = self.YC, (lambda b: self.YC.t[:, idx, b * 128:(b + 1) * 128])
                fw.dma("sp", wo.t[0:K, :], self.w_out_bf.t[r0:r0 + K, :], [self.w_out_bf], [wo])
                for b in blks:
                    for ct in range(2):
                        bk = bankset[b][ct]
                        self.mm(bk.t[:, :], yv(b), wo.t[0:K, ct * 512:(ct + 1) * 512], ci == 0, ci == len(chunks) - 1, [ybuf, wo], [bk],
                                signal=(b == blks[-1] and ct == 1))
            for b in blks:
                i = self.xi % 2
                self.xi += 1
                xt = self.xt[i]
                rap, rbuf = x_res(b)
                fw.dma("sp", xt.t[:], rap, [rbuf], [xt])
                for ct in range(2):
                    bk = bankset[b][ct]
                    tmp = self.sqn.t[:, 0:512]
                    self.tt("dve", tmp, bk.t[:, :], self.GATE[v].t[:, ct * 512:(ct + 1) * 512], MUL, [bk, self.GATE[v]], [self.sqn])
                    self.tt("pool", xt.t[:, ct * 512:(ct + 1) * 512], xt.t[:, ct * 512:(ct + 1) * 512], tmp, ADD, [xt, self.sqn], [xt])
                yap, ybuf_ = y_dst(b)
                fw.dma("sp", yap, xt.t[:], [xt], [ybuf_])

    def _emit_main(self, l, last, xfull, xown, cx, yd, ycd):
        fw = self.fw
        S, OWN, NKB = self.S, self.OWN, self.NKB
        fw.op("pool", lambda e: e.memset(self.ONR.t[:], 0.0), [], [self.ONR])
        for p in range(2):
            fw.op("pool", lambda e: e.memset(self.k_tok[p].t[:, 0:64], 0.0), [], [self.k_tok[p]])
        self.kv_ready = None
        self.hT_prefilled = False
        def ksrc(j):
            return cx(j * 128) + (1,) if j < 2 else xfull((j - 2) * 128) + (0,)
        for j in range(NKB + 1):
            if j < NKB:
                ap, bf, v = ksrc(j)
                self.phase_k_A(j, ap, bf, v)
            if j >= 1:
                self.phase_k_B(j - 1)
                self.phase_k_C(j - 1)
        nb = TM // 128
        lat_srcs = []
        for m in range(OWN // TM):
            srcs = []
            for e in range(nb + 2):
                ob = m * nb - 1 + e
                if ob < 0:
                    srcs.append(xfull(S // 2 - 128) + (self.hm.t[:, 0:1],))
                elif ob >= OWN // 128:
                    srcs.append(xfull(S // 2) + (self.hm.t[:, 1:2],))
                else:
                    srcs.append(xown(ob * 128) + (None,))
            lat_srcs.append(srcs)
        if not last:
            srcs = [None] + [cx(b * 128) + (None,) for b in range(2)] + [None]
            self.macro(srcs, 0, [0, 1], 1, lambda b: cx(b * 128), lambda b: ycd(b * 128), next_srcs=lat_srcs[0], next_v=0)
        nm = OWN // TM
        for m in range(nm):
            self.macro(lat_srcs[m], CTX + m * TM, list(range(NKB)), 0,
                       (lambda m_: (lambda b: xown(m_ * TM + b * 128)))(m), (lambda m_: (lambda b: yd(m_ * TM + b * 128)))(m),
                       next_srcs=(lat_srcs[m + 1] if m + 1 < nm else None), next_v=0)


_PROG_CACHE = {}


def _get_prog(S, layers, fused=False):
    key = (S, tuple(layers), fused)
    if key not in _PROG_CACHE:
        _PROG_CACHE[key] = Prog(S, list(layers), fused=fused)
    return _PROG_CACHE[key]


def _rope_tables(S):
    rows = S // 64
    pos_r = np.repeat(np.arange(rows, dtype=np.float32), 64)
    pos_c = np.tile(np.arange(64, dtype=np.float32), rows)
    inv = (np.float32(10000.0) ** (-np.arange(8, dtype=np.float32) / np.float32(8))).astype(np.float32)
    ang = np.concatenate([pos_r[:, None] * inv, pos_c[:, None] * inv], axis=-1).astype(np.float32)
    lat = np.concatenate([np.cos(ang), np.sin(ang)], axis=-1).astype(np.float32)
    ctxr = np.concatenate([np.ones((CTX, 16), np.float32), np.zeros((CTX, 16), np.float32)], axis=-1)
    return lat, ctxr


def _static_inputs(inp):
    f = lambda a: np.ascontiguousarray(np.asarray(a, dtype=np.float32))
    L = inp["norm_w"].shape[0]
    col = lambda a, c: np.ascontiguousarray(np.asarray(a, np.float32).reshape(L, c, 128).transpose(0, 2, 1))
    sel = np.zeros((2, 256), np.float32)
    sel[0, 0:128] = 1.0
    sel[1, 128:256] = 1.0
    scw = np.asarray(inp["sc_conv_w"], np.float32).reshape(L, 3, 2, 128).transpose(0, 3, 2, 1)
    cfw = np.asarray(inp["cf_conv_w"], np.float32).reshape(L, 31, 2, 128).transpose(0, 3, 2, 1)
    cfv = np.stack([col(inp["cf_conv_b"], 2), col(inp["cf_ln_w"], 2), col(inp["cf_ln_b"], 2), col(inp["cf_pw_b"], 2)], axis=-1)
    return dict(
        ident=np.eye(128, dtype=np.float32), sel=sel,
        norm_w=f(inp["norm_w"]), w_mod=f(inp["w_mod"]), b_mod=f(inp["b_mod"]), w_in=f(inp["w_in"]),
        qnw_col=col(inp["q_norm_w"], 2), w_uq=f(inp["w_uq"]), kvnw_col=col(inp["kv_norm_w"], 1), w_ukv=f(inp["w_ukv"]),
        qhw=f(inp["q_head_norm_w"]), khw=f(inp["k_head_norm_w"]),
        scw_col=np.ascontiguousarray(scw), cfw_col=np.ascontiguousarray(cfw), cfv_col=np.ascontiguousarray(cfv),
        cf_pw=f(inp["cf_pw_w"]), w_out=f(inp["w_out"]),
    )


def _core_inputs(static, x, ctx, c, c_ctx, S):
    lat, ctxr = _rope_tables(S)
    OWN = S // 2
    maps = []
    for core in range(8):
        b, half = core // 2, core % 2
        cv = np.stack([np.asarray(c[b], np.float32), np.asarray(c_ctx, np.float32)], axis=0)
        csil = np.ascontiguousarray(cv.reshape(2, 8, 128).transpose(2, 1, 0))
        hm = np.zeros((128, 2), np.float32)
        hm[:, 0] = 1.0 if half == 1 else 0.0
        hm[:, 1] = 1.0 if half == 0 else 0.0
        m = dict(static)
        m.update(
            x_full=np.ascontiguousarray(x[b]), x_own=np.ascontiguousarray(x[b, half * OWN:(half + 1) * OWN]),
            ctx_in=np.ascontiguousarray(ctx[b]), csil=csil,
            rope_k=np.ascontiguousarray(np.concatenate([ctxr, lat], axis=0)),
            rope_q=np.ascontiguousarray(np.concatenate([ctxr, lat[half * OWN:(half + 1) * OWN]], axis=0)),
            hmask=hm,
        )
        maps.append(m)
    return maps


def kernel(x, c, ctx, c_ctx, norm_w, w_mod, b_mod, w_in, q_norm_w, w_uq, kv_norm_w, w_ukv, q_head_norm_w, k_head_norm_w,
           sc_conv_w, cf_conv_w, cf_conv_b, cf_ln_w, cf_ln_b, cf_pw_w, cf_pw_b, w_out):
    inp = dict(norm_w=norm_w, w_mod=w_mod, b_mod=b_mod, w_in=w_in, q_norm_w=q_norm_w, w_uq=w_uq, kv_norm_w=kv_norm_w, w_ukv=w_ukv,
               q_head_norm_w=q_head_norm_w, k_head_norm_w=k_head_norm_w, sc_conv_w=sc_conv_w, cf_conv_w=cf_conv_w,
               cf_conv_b=cf_conv_b, cf_ln_w=cf_ln_w, cf_ln_b=cf_ln_b, cf_pw_w=cf_pw_w, cf_pw_b=cf_pw_b, w_out=w_out)
    x = np.asarray(x, np.float32)
    ctx = np.asarray(ctx, np.float32)
    c = np.asarray(c, np.float32)
    c_ctx = np.asarray(c_ctx, np.float32)
    B, S, _ = x.shape
    OWN = S // 2
    static = _static_inputs(inp)
    L = static["norm_w"].shape[0]
    prog = _get_prog(S, list(range(L)), fused=True)
    maps = _core_inputs(static, x, ctx, c, c_ctx, S)
    res = run_bass_kernel_spmd(prog.nc, maps, core_ids=list(range(8)))
    out = np.empty_like(x)
    for core in range(8):
        b, half = core // 2, core % 2
        out[b, half * OWN:(half + 1) * OWN] = res.results[core]["y"]
    return out
```

```python
import numpy as np
from contextlib import ExitStack
import concourse.bass as bass
import concourse.mybir as mybir
from concourse.bass_utils import run_bass_kernel_spmd

F32 = mybir.dt.float32
BF16 = mybir.dt.bfloat16
AF = mybir.ActivationFunctionType
ALU = mybir.AluOpType
AX = mybir.AxisListType


class Ref:
    __slots__ = ("lane", "v")

    def __init__(self, lane, v=None):
        self.lane = lane
        self.v = v


class Buf:
    def __init__(self, t, name=""):
        self.t = t
        self.name = name
        self.last_w = None
        self.readers = {}


class Fw:
    NDMA = 8

    def __init__(self, nc, es):
        self.nc = nc
        self.es = es
        self.eng = {"pe": nc.tensor, "act": nc.scalar, "dve": nc.vector, "pool": nc.gpsimd, "sp": nc.sync}
        self.sem = {}
        self.cnt = {}
        self.inc = {}
        for n in ["pe", "act", "dve", "pool"]:
            self.sem[n] = es.enter_context(nc.semaphore("s_" + n))
            self.cnt[n] = 0
            self.inc[n] = 1
        for i in range(self.NDMA):
            n = "d%d" % i
            self.sem[n] = es.enter_context(nc.semaphore("s_" + n))
            self.cnt[n] = 0
            self.inc[n] = 16
        self.waited = {e: {} for e in self.eng}
        self.pending = {e: [] for e in self.eng}
        self.dma_rr = 0
        self.n_inst = 0
        self.log = {e: [] for e in self.eng}

    def sb(self, name, shape, dt, es=None):
        self.uid = getattr(self, "uid", 0) + 1
        t = (es or self.es).enter_context(self.nc.sbuf_tensor("sb%d_%s" % (self.uid, name), shape, dt))
        return Buf(t, name)

    def ps(self, name, shape, dt, es=None):
        self.uid = getattr(self, "uid", 0) + 1
        t = (es or self.es).enter_context(self.nc.psum_tensor("ps%d_%s" % (self.uid, name), shape, dt))
        return Buf(t, name)

    def _wait(self, e, ref):
        if e == "pe" and ref.lane == "pe":
            return
        if ref.v is None:
            raise RuntimeError("dependency on unsignalled instruction (lane %s)" % ref.lane)
        if self.waited[e].get(ref.lane, 0) < ref.v:
            self.eng[e].wait_ge(self.sem[ref.lane], ref.v * self.inc[ref.lane])
            self.log[e].append(("w", ref.lane, ref.v))
            self.waited[e][ref.lane] = ref.v

    def _deps(self, e, reads, writes):
        need = {}
        for b in reads:
            r = b.last_w
            if r is not None:
                if r.lane not in need or (need[r.lane].v or 1 << 60) < (r.v or 1 << 60):
                    need[r.lane] = r
        for b in writes:
            cands = list(b.readers.values())
            if b.last_w is not None:
                cands.append(b.last_w)
            for r in cands:
                if r.lane not in need or (need[r.lane].v or 1 << 60) < (r.v or 1 << 60):
                    need[r.lane] = r
        for r in need.values():
            self._wait(e, r)

    def _record(self, ref, reads, writes):
        for b in reads:
            b.readers[ref.lane] = ref
        for b in writes:
            b.last_w = ref
            b.readers = {}

    def op(self, e, fn, reads, writes, signal=True):
        self._deps(e, reads, writes)
        inst = fn(self.eng[e])
        self.n_inst += 1
        if signal:
            self.cnt[e] += 1
            inst.then_inc(self.sem[e], 1)
            self.log[e].append(("s", e))
            ref = Ref(e, self.cnt[e])
            for p in self.pending[e]:
                p.v = ref.v
            self.pending[e] = []
        else:
            ref = Ref(e, None)
            self.pending[e].append(ref)
        self._record(ref, reads, writes)
        return ref

    def act(self, out, in_, func, reads, writes, **kw):
        return self.op("act", lambda e: e.activation(out=out, in_=in_, func=func, **kw), reads, writes)

    def dma(self, q, out, in_, reads, writes, **kw):
        lane = "d%d" % self.dma_rr
        self.dma_rr = (self.dma_rr + 1) % self.NDMA
        if self.cnt[lane] > 0:
            self._wait(q, Ref(lane, self.cnt[lane]))
        self._deps(q, reads, writes)
        inst = self.eng[q].dma_start(out=out, in_=in_, **kw)
        self.n_inst += 1
        self.cnt[lane] += 1
        inst.then_inc(self.sem[lane], 16)
        self.log[q].append(("s", lane))
        ref = Ref(lane, self.cnt[lane])
        self._record(ref, reads, writes)
        return ref

    def simulate(self):
        cnt = {l: 0 for l in self.sem}
        ptr = {e: 0 for e in self.eng}
        progress = True
        while progress:
            progress = False
            for e in self.eng:
                lg = self.log[e]
                while ptr[e] < len(lg):
                    it = lg[ptr[e]]
                    if it[0] == "w":
                        if cnt[it[1]] >= it[2]:
                            ptr[e] += 1
                            progress = True
                        else:
                            break
                    else:
                        cnt[it[1]] += 1
                        ptr[e] += 1
                        progress = True
        stuck = {e: (ptr[e], len(self.log[e]), self.log[e][ptr[e]] if ptr[e] < len(self.log[e]) else None) for e in self.eng}
        return all(ptr[e] == len(self.log[e]) for e in self.eng), stuck, cnt

    def barrier(self):
        for e in self.eng:
            if self.pending[e]:
                raise RuntimeError("barrier with pending unsignalled instr on " + e)
        for e in self.eng:
            for lane in self.sem:
                if self.cnt[lane] > 0:
                    self._wait(e, Ref(lane, self.cnt[lane]))


D = 1024
DIN = 2720
H = 8
CTX = 256
TM = 512
EPS = 1e-6
LN_EPS = 1e-5
SM_SCALE = 96 ** -0.5
OFF = dict(cq=0, ckv=256, kr=384, gA=416, scin=928, scB=1184, scC=1440, gB=1696, cfa=1952, cfg=2208, gC=2464)

MUL, ADD, SUB = ALU.mult, ALU.add, ALU.subtract


class Prog:
    def __init__(self, S, layers, n_layers_total=2, fused=False, stop=None):
        self.stop = stop
        self.S = S
        self.OWN = S // 2
        self.NKB = (CTX + S) // 128
        self.layers = layers
        self.fused = fused
        self.L = n_layers_total
        nc = self.nc = bass.Bass("TRN2", target_bir_lowering=False)
        self.es = ExitStack()
        self.fw = Fw(nc, self.es)
        self.din = {}
        self._declare_io()
        with self.es:
            self._alloc_persist()
            self._emit()

    def dram_in(self, name, shape, dt=F32):
        t = self.nc.dram_tensor(name, list(shape), dt, kind="ExternalInput").ap()
        self.din[name] = Buf(t, name)
        return self.din[name]

    def _declare_io(self):
        S, OWN, L = self.S, self.OWN, self.L
        nc = self.nc
        self.x_full = self.dram_in("x_full", [S, D])
        self.x_own = self.dram_in("x_own", [OWN, D])
        self.ctx_in = self.dram_in("ctx_in", [CTX, D])
        self.csil = self.dram_in("csil", [128, 8, 2])
        self.rope_k = self.dram_in("rope_k", [CTX + S, 32])
        self.rope_q = self.dram_in("rope_q", [CTX + OWN, 32])
        self.hmask = self.dram_in("hmask", [128, 2])
        self.ident_d = self.dram_in("ident", [128, 128])
        self.sel_d = self.dram_in("sel", [2, 256])
        self.norm_w = self.dram_in("norm_w", [L, D])
        self.w_mod = self.dram_in("w_mod", [L, D, 3 * D])
        self.b_mod = self.dram_in("b_mod", [L, 3 * D])
        self.w_in = self.dram_in("w_in", [L, D, DIN])
        self.qnw_col = self.dram_in("qnw_col", [L, 128, 2])
        self.w_uq = self.dram_in("w_uq", [L, 256, 768])
        self.kvnw_col = self.dram_in("kvnw_col", [L, 128, 1])
        self.w_ukv = self.dram_in("w_ukv", [L, 128, 1024])
        self.qhw = self.dram_in("qhw", [L, 96])
        self.khw = self.dram_in("khw", [L, 96])
        self.scw_col = self.dram_in("scw_col", [L, 128, 2, 3])
        self.cfw_col = self.dram_in("cfw_col", [L, 128, 2, 31])
        self.cfv_col = self.dram_in("cfv_col", [L, 128, 2, 4])
        self.cf_pw = self.dram_in("cf_pw", [L, 256, 256])
        self.w_out = self.dram_in("w_out", [L, D, D])
        self.y = Buf(nc.dram_tensor("y", [OWN, D], F32, kind="ExternalOutput").ap(), "y")
        self.yc = None
        if len(self.layers) == 1:
            self.yc = Buf(nc.dram_tensor("yc", [CTX, D], F32, kind="ExternalOutput").ap(), "yc")
        self.NCH = OWN // 512
        self.x1o_t = [nc.dram_tensor("x1o%d" % i, [512, D], F32) for i in range(self.NCH)]
        self.x1g_t = [nc.dram_tensor("x1g%d" % i, [1024, D], F32) for i in range(self.NCH)]
        self.x1o = [Buf(t.ap(), "x1o") for t in self.x1o_t]
        self.x1g = [Buf(t.ap(), "x1g") for t in self.x1g_t]
        self.xc1 = Buf(nc.dram_tensor("xc1", [CTX, D], F32).ap(), "xc1")
        self.w_in_bf = Buf(nc.dram_tensor("w_in_bf", [D, DIN], BF16, kind="Internal").ap(), "w_in_bf")
        self.w_out_bf = Buf(nc.dram_tensor("w_out_bf", [D, D], BF16, kind="Internal").ap(), "w_out_bf")

    def _alloc_persist(self):
        fw = self.fw
        sb, ps = fw.sb, fw.ps
        NKB = self.NKB
        NK = NKB * 128
        self.ident_f = sb("ident_f", [128, 128], F32)
        self.ident_b = sb("ident_b", [128, 128], BF16)
        self.ones_f = sb("ones_f", [128, 128], F32)
        self.E64 = sb("E64", [128, 64], F32)
        self.sel = sb("sel", [2, 256], F32)
        self.hm = sb("hm", [128, 2], F32)
        self.G = [sb("G%d" % i, [128, D], F32) for i in range(2)]
        self.Sh = [sb("Sh%d" % i, [128, D], F32) for i in range(2)]
        self.GATE = [sb("GATE%d" % i, [128, D], F32) for i in range(2)]
        self.w_kv = sb("w_kv", [128, 8, 160], BF16)
        self.w_cq = sb("w_cq", [128, 8, 256], BF16)
        self.w_uqb = sb("w_uqb", [128, 2, 768], BF16)
        self.Wkn1 = sb("Wkn1", [128, 512], BF16)
        self.Wkn2 = sb("Wkn2", [128, 576], BF16)
        self.Wv = sb("Wv", [128, 512], BF16)
        self.pwb = sb("pwb", [128, 2, 256], BF16)
        self.wq_b = sb("wq_b", [128, 96], F32)
        self.wk_b = sb("wk_b", [128, 96], F32)
        self.scw = sb("scw", [128, 2, 3], F32)
        self.cfw = sb("cfw", [128, 2, 31], F32)
        self.cfv = sb("cfv", [128, 2, 4], F32)
        self.wch = [sb("wch%d" % i, [128, 8, 128], BF16) for i in range(3)]
        self.woch = [sb("woch%d" % i, [128, D], BF16) for i in range(3)]
        self.ckvT_t = sb("ckvT", [128, NK], BF16)
        self.ckvT = [Buf(None, "ckvT%d" % j) for j in range(NKB)]
        self.KB_t = sb("KB", [96, NK], BF16)
        self.KBr = [Buf(None, "KBr%d" % j) for j in range(NKB)]
        self.KBn = [Buf(None, "KBn%d" % j) for j in range((NK + 511) // 512)]
        self.ALPHA = sb("ALPHA", [128, NKB, 8], F32)
        self.ALb = [Buf(None, "AL%d" % j) for j in range(NKB)]
        self.RCKV = sb("RCKV", [128, NKB], F32)
        self.RCb = [Buf(None, "RC%d" % j) for j in range(NKB)]
        self.V2 = sb("V2", [128, NKB, 2, 65], BF16)
        self.pT2 = ps("pT2", [128, 1024], BF16)
        self.pT = self.pT2
        self.pA = ps("pA", [128, 512], F32)
        self.pQ = [ps("pQ%d" % i, [128, 512], F32) for i in range(2)]
        self.pS = [ps("pS%d" % i, [128, 512], F32) for i in range(2)]
        self.pO = [ps("pO%d" % i, [128, 512], F32) for i in range(2)]

    def _alloc_prep(self, es):
        sb = lambda n, sh, dt: self.fw.sb(n, sh, dt, es)
        self.csl = sb("csl", [128, 8, 2], F32)
        self.wmc = [sb("wmc%d" % i, [128, 3 * D], F32) for i in range(2)]
        self.modrow = sb("modrow", [2, 3 * D], F32)
        self.bmod2 = sb("bmod2", [2, 3 * D], F32)
        self.nw2 = sb("nw2", [2, D], F32)
        self.wstage = [sb("wstage%d" % i, [128, DIN], F32) for i in range(2)]
        self.wcast = [sb("wcast%d" % i, [128, DIN], BF16) for i in range(2)]
        self.qnw = sb("qnw", [128, 2], F32)
        self.kvnw = sb("kvnw", [128, 1], F32)
        self.tmpw = sb("tmpw", [128, 1024], F32)

    def _alloc_main(self, es):
        sb = lambda n, sh, dt: self.fw.sb(n, sh, dt, es)
        EXT = TM + 256
        self.xt = [sb("xt%d" % i, [128, D], F32) for i in range(2)]
        self.hb = [sb("hb%d" % i, [128, D], BF16) for i in range(2)]
        self.sm = [sb("sm%d" % i, [128, 16], F32) for i in range(2)]
        self.ckv_tok = [sb("ckv_tok%d" % i, [128, 128], BF16) for i in range(2)]
        self.k_tok = [sb("k_tok%d" % i, [128, 96], BF16) for i in range(2)]
        self.krw = [sb("krw%d" % i, [128, 32], F32) for i in range(2)]
        self.rtmp = [sb("rtmp%d" % i, [128, 4, 16], F32) for i in range(2)]
        self.smk = [sb("smk%d" % i, [128, 16], F32) for i in range(3)]
        self.cs = [sb("cs%d" % i, [128, 32], F32) for i in range(2)]
        self.sqn = sb("sqn", [128, 768], F32)
        self.ssn = sb("ssn", [128, 16], F32)
        self.hTm_t = sb("hTm", [128, 8, EXT], BF16)
        self.hTm = [Buf(None, "hTm%d" % i) for i in range(EXT // 128)]
        self.QT = sb("QT", [96, 8, TM], BF16)
        self.GA = sb("GA", [64, 8, TM], BF16)
        self.GB = sb("GB", [128, 2, TM], BF16)
        self.GC = sb("GC", [128, 2, TM], BF16)
        self.SCB = sb("SCB", [128, 2, TM], BF16)
        self.YB = sb("YB", [128, 2, TM], BF16)
        self.YC = sb("YC", [128, 2, TM], BF16)
        self.ZA = sb("ZA", [128, 2, TM], BF16)
        self.F1 = sb("F1", [128, 2, EXT], BF16)
        self.F2 = sb("F2", [128, 2, EXT], BF16)
        self.F3 = sb("F3", [128, 2, TM], F32)
        self.ZSQ = sb("ZSQ", [128, 2, TM], F32)
        self.dg = [sb("dg%d" % i, [128, 128], BF16) for i in range(4)]
        self.MU = sb("MU", [128, TM], F32)
        self.RS = sb("RS", [128, TM], F32)
        self.cqn_tok = sb("cqn_tok", [128, 256], BF16)
        self.cqnT = sb("cqnT", [128, 2, 128], BF16)
        self.qf = sb("qf", [128, 8, 96], F32)
        self.q_tok = sb("q_tok", [128, 8, 96], BF16)
        self.qtmp = sb("qtmp", [128, 4, 8, 16], F32)
        self.PT = [sb("PT%d" % i, [128, 512], BF16) for i in range(4)]
        self.ONR = sb("ONR", [128, 512], F32)
        self.rb = sb("rb", [64, 512], F32)

    def mm(self, out, lhsT, rhs, start, stop, reads, writes, signal=True):
        return self.fw.op("pe", lambda e: e.matmul(out, lhsT=lhsT, rhs=rhs, start=start, stop=stop), reads, writes, signal)

    def tp(self, out, in_, ident, reads, writes, signal=True):
        return self.fw.op("pe", lambda e: e.transpose(out=out, in_=in_, identity=ident), reads, writes, signal)

    def tt(self, eng, out, in0, in1, op, reads, writes):
        return self.fw.op(eng, lambda e: e.tensor_tensor(out=out, in0=in0, in1=in1, op=op), reads, writes)

    def ts(self, eng, out, in0, s1, op0, reads, writes, s2=None, op1=None):
        if op1 is None:
            return self.fw.op(eng, lambda e: e.tensor_scalar(out=out, in0=in0, scalar1=s1, scalar2=None, op0=op0), reads, writes)
        return self.fw.op(eng, lambda e: e.tensor_scalar(out=out, in0=in0, scalar1=s1, scalar2=s2, op0=op0, op1=op1), reads, writes)

    def stt(self, eng, out, in0, scalar, in1, op0, op1, reads, writes):
        return self.fw.op(eng, lambda e: e.scalar_tensor_tensor(out=out, in0=in0, scalar=scalar, in1=in1, op0=op0, op1=op1), reads, writes)

    def cp(self, eng, out, in_, reads, writes):
        if eng == "act":
            return self.fw.act(out, in_, AF.Copy, reads, writes)
        return self.fw.op(eng, lambda e: e.tensor_copy(out=out, in_=in_), reads, writes)

    def rsqrt(self, out, in_, scale, eps, buf_in, buf_out, power=-0.5):
        self.fw.act(out, in_, AF.Ln, [buf_in], [buf_out], scale=scale, bias=eps)
        self.fw.act(out, out, AF.Exp, [buf_out], [buf_out], scale=power)

    def _emit(self):
        fw = self.fw
        fw.dma("sp", self.ident_f.t[:], self.ident_d.t[:, :], [], [self.ident_f])
        fw.dma("sp", self.sel.t[:], self.sel_d.t[:, :], [], [self.sel])
        fw.dma("sp", self.hm.t[:], self.hmask.t[:, :], [], [self.hm])
        self.cp("dve", self.ident_b.t[:], self.ident_f.t[:], [self.ident_f], [self.ident_b])
        fw.op("pool", lambda e: e.memset(self.ones_f.t[:], 1.0), [], [self.ones_f])
        fw.op("pool", lambda e: e.memset(self.E64.t[:], 0.0), [], [self.E64])
        fw.op("pool", lambda e: e.memset(self.E64.t[64:65, :], 1.0), [self.E64], [self.E64])
        self.V2g = [Buf(None, "V2g%d" % g) for g in range((self.NKB + 3) // 4)]
        fw.op("pool", lambda e: e.memset(self.V2.t[:, :, :, 64:65], 1.0), [], self.V2g)
        self.xi = 0
        for li, l in enumerate(self.layers):
            last = (l == self.L - 1)
            with ExitStack() as es:
                self._alloc_prep(es)
                self._emit_prep(l)
                fw.barrier()
            if self.stop == "prep":
                return
            first = (li == 0)
            OWN = self.OWN
            if first:
                xfull = lambda t0: (self.x_full.t[t0:t0 + 128, :], self.x_full)
                xown = lambda t0: (self.x_own.t[t0:t0 + 128, :], self.x_own)
                cx = lambda t0: (self.ctx_in.t[t0:t0 + 128, :], self.ctx_in)
            else:
                def xfull(t0):
                    half, o = t0 // OWN, t0 % OWN
                    ch, r = o // 512, half * 512 + o % 512
                    return (self.x1g[ch].t[r:r + 128, :], self.x1g[ch])
                xown = lambda t0: (self.x1o[t0 // 512].t[t0 % 512:t0 % 512 + 128, :], self.x1o[t0 // 512])
                cx = lambda t0: (self.xc1.t[t0:t0 + 128, :], self.xc1)
            if last or len(self.layers) == 1:
                yd = lambda t0: (self.y.t[t0:t0 + 128, :], self.y)
                ycd = lambda t0: (self.yc.t[t0:t0 + 128, :], self.yc)
            else:
                yd = lambda t0: (self.x1o[t0 // 512].t[t0 % 512:t0 % 512 + 128, :], self.x1o[t0 // 512])
                ycd = lambda t0: (self.xc1.t[t0:t0 + 128, :], self.xc1)
            with ExitStack() as es:
                self._alloc_main(es)
                self._emit_main(l, last, xfull, xown, cx, yd, ycd)
                fw.barrier()
            if len(self.layers) > 1 and not last:
                fw.sem["cc"] = self.es.enter_context(self.nc.semaphore("s_cc"))
                fw.inc["cc"] = 1
                for i in range(self.NCH):
                    inst = self.nc.gpsimd.collective_compute(
                        "AllGather", ALU.bypass, replica_groups=[[0, 1], [2, 3], [4, 5], [6, 7]],
                        ins=[self.x1o_t[i].ap().opt()], outs=[self.x1g_t[i].ap().opt()])
                    inst.then_inc(fw.sem["cc"])
                    fw.log["pool"].append(("s", "cc"))
                    self.x1g[i].last_w = Ref("cc", i + 1)
                    self.x1g[i].readers = {}
                fw.cnt["cc"] = self.NCH
                fw.barrier()

    def _emit_prep(self, l):
        fw = self.fw
        banks = [self.pA, self.pQ[0], self.pQ[1], self.pS[0], self.pS[1], self.pO[0]]
        csl = self.csl
        fw.dma("sp", csl.t[:], self.csil.t[:, :, :], [], [csl])
        fw.act(csl.t[:], csl.t[:], AF.Silu, [csl], [csl])
        for r in range(2):
            fw.dma("sp", self.bmod2.t[r:r + 1, :], self.b_mod.t[l:l + 1, :], [], [self.bmod2])
            fw.dma("sp", self.nw2.t[r:r + 1, :], self.norm_w.t[l:l + 1, :], [], [self.nw2])
        for k in range(8):
            wm = self.wmc[k % 2]
            fw.dma("sp", wm.t[:], self.w_mod.t[l, k * 128:(k + 1) * 128, :], [], [wm])
            for j in range(6):
                self.mm(banks[j].t[0:2, :], csl.t[:, k, :], wm.t[:, j * 512:(j + 1) * 512], k == 0, k == 7,
                        [csl, wm], [banks[j]], signal=(j == 5 or k == 7))
        mr = self.modrow
        for j in range(6):
            self.tt("dve", mr.t[:, j * 512:(j + 1) * 512], banks[j].t[0:2, :], self.bmod2.t[:, j * 512:(j + 1) * 512], ADD,
                    [banks[j], self.bmod2], [mr])
        self.stt("dve", mr.t[:, D:2 * D], mr.t[:, D:2 * D], 1.0, self.nw2.t[:, :], ADD, MUL, [mr, self.nw2], [mr])
        bi = 0
        for v in range(2):
            for dst, c0 in ((self.Sh[v], 0), (self.G[v], D), (self.GATE[v], 2 * D)):
                for hf in range(2):
                    bk = banks[bi % 6]
                    bi += 1
                    self.mm(bk.t[:, :], self.sel.t[0:2, v * 128:(v + 1) * 128], mr.t[0:2, c0 + hf * 512:c0 + (hf + 1) * 512],
                            True, True, [self.sel, mr], [bk])
                    self.cp("act", dst.t[:, hf * 512:(hf + 1) * 512], bk.t[:, :], [bk], [dst])
        for k in range(8):
            st, wc = self.wstage[k % 2], self.wcast[k % 2]
            fw.dma("sp", st.t[:], self.w_in.t[l, k * 128:(k + 1) * 128, :], [], [st])
            self.cp("dve", wc.t[:, 0:1360], st.t[:, 0:1360], [st], [wc])
            self.cp("pool", wc.t[:, 1360:DIN], st.t[:, 1360:DIN], [st], [wc])
            fw.dma("sp", self.w_in_bf.t[k * 128:(k + 1) * 128, :], wc.t[:], [wc], [self.w_in_bf])
            self.cp("pool", self.w_kv.t[:, k, :], wc.t[:, 256:416], [wc], [self.w_kv])
            self.cp("pool", self.w_cq.t[:, k, :], wc.t[:, 0:256], [wc], [self.w_cq])
        for k in range(8):
            st, wc = self.wstage[k % 2], self.wcast[k % 2]
            fw.dma("sp", st.t[:, 0:D], self.w_out.t[l, k * 128:(k + 1) * 128, :], [], [st])
            self.cp("dve", wc.t[:, 0:D], st.t[:, 0:D], [st], [wc])
            fw.dma("sp", self.w_out_bf.t[k * 128:(k + 1) * 128, :], wc.t[:, 0:D], [wc], [self.w_out_bf])
        fw.dma("sp", self.qnw.t[:], self.qnw_col.t[l, :, :], [], [self.qnw])
        fw.dma("sp", self.kvnw.t[:], self.kvnw_col.t[l, :, :], [], [self.kvnw])
        for c in range(2):
            st = self.wstage[c]
            fw.dma("sp", st.t[:, 0:768], self.w_uq.t[l, c * 128:(c + 1) * 128, :], [], [st])
            self.ts("dve", self.w_uqb.t[:, c, :], st.t[:, 0:768], self.qnw.t[:, c:c + 1], MUL, [st, self.qnw], [self.w_uqb])
        fw.dma("sp", self.wk_b.t[:], self.khw.t[l, :].partition_broadcast(128), [], [self.wk_b])
        fw.dma("sp", self.wq_b.t[:], self.qhw.t[l, :].partition_broadcast(128), [], [self.wq_b])
        tw = self.tmpw
        fw.dma("sp", tw.t[:], self.w_ukv.t[l, :, :], [], [tw])
        twv = tw.t[:].rearrange("p (h t d) -> p h t d", h=8, t=2)
        v3 = lambda b: b.t[:].rearrange("p (h d) -> p h d", h=8)
        self.ts("dve", v3(self.Wkn1), twv[:, :, 0, :], self.kvnw.t[:, 0:1], MUL, [tw, self.kvnw], [self.Wkn1])
        self.ts("dve", v3(self.Wv), twv[:, :, 1, :], self.kvnw.t[:, 0:1], MUL, [tw, self.kvnw], [self.Wv])
        fw.op("pool", lambda e: e.memset(self.Wkn2.t[:, 512:576], 0.0), [], [self.Wkn2])
        self.stt("dve", self.Wkn2.t[:, 0:512].rearrange("p (h d) -> p h d", h=8), twv[:, :, 0, :], self.kvnw.t[:, 0:1],
                 self.wk_b.t[:, 0:64].unsqueeze(1).to_broadcast([128, 8, 64]), MUL, MUL, [tw, self.kvnw, self.wk_b], [self.Wkn2])
        for c in range(2):
            st = self.wstage[c]
            fw.dma("sp", st.t[:, 0:256], self.cf_pw.t[l, c * 128:(c + 1) * 128, :], [], [st])
            self.cp("dve", self.pwb.t[:, c, :], st.t[:, 0:256], [st], [self.pwb])
        fw.dma("sp", self.scw.t[:], self.scw_col.t[l, :, :, :], [], [self.scw])
        fw.dma("sp", self.cfw.t[:], self.cfw_col.t[l, :, :, :], [], [self.cfw])
        fw.dma("sp", self.cfv.t[:], self.cfv_col.t[l, :, :, :], [], [self.cfv])

    def x_to_hT(self, src_ap, src_buf, v, mask_ap, dst_buf, dst_view):
        fw = self.fw
        i = self.xi % 2
        self.xi += 1
        self.last_par = i
        xt, hb, sm = self.xt[i], self.hb[i], self.sm[i]
        fw.dma("sp", xt.t[:], src_ap, [src_buf], [xt])
        fw.act(hb.t[:], xt.t[:], AF.Square, [xt], [hb, sm], accum_out=sm.t[:, 0:1])
        self.rsqrt(sm.t[:, 1:2], sm.t[:, 0:1], 1.0 / D, EPS, sm, sm)
        self.stt("dve", xt.t[:], xt.t[:], sm.t[:, 1:2], self.G[v].t[:], MUL, MUL, [xt, sm, self.G[v]], [xt])
        self.tt("dve", hb.t[:], xt.t[:], self.Sh[v].t[:], ADD, [xt, self.Sh[v]], [hb])
        if mask_ap is not None:
            self.ts("dve", hb.t[:], hb.t[:], mask_ap, MUL, [hb, self.hm], [hb])
        for k in range(8):
            self.tp(self.pT.t[:, k * 128:(k + 1) * 128], hb.t[:, k * 128:(k + 1) * 128], self.ident_b.t[:],
                    [hb, self.ident_b], [self.pT], signal=(k == 7))
        self.cp("dve" if getattr(self, "in_attn", False) else "act", dst_view, self.pT.t[:].rearrange("p (k t) -> p k t", k=8), [self.pT], [dst_buf])

    def phase_k_A(self, j, src_ap, src_buf, v):
        fw = self.fw
        slot = j % len(self.hTm)
        hTb = self.hTm[slot]
        hT = self.hTm_t.t[:, :, slot * 128:(slot + 1) * 128]
        self.x_to_hT(src_ap, src_buf, v, None, hTb, hT)
        p = j % 2
        cs, sm = self.cs[p], self.smk[j % 3]
        fw.dma("sp", cs.t[:], self.rope_k.t[j * 128:(j + 1) * 128, :], [], [cs])
        pA = self.pA
        for k in range(8):
            self.mm(pA.t[:, 0:160], hT[:, k, :], self.w_kv.t[:, k, :], k == 0, k == 7, [hTb, self.w_kv], [pA], signal=(k == 7))
        sq = self.qf.t[:].rearrange("p h d -> p (h d)")
        fw.act(sq[:, 0:160], pA.t[:, 0:160], AF.Square, [pA], [self.qf])
        fw.op("dve", lambda e: e.tensor_reduce(out=sm.t[:, 2:3], in_=sq[:, 0:128], axis=AX.X, op=ADD), [self.qf], [sm])
        fw.op("dve", lambda e: e.tensor_reduce(out=sm.t[:, 3:4], in_=sq[:, 128:160], axis=AX.X, op=ADD), [self.qf], [sm])
        self.cp("dve", self.ckv_tok[p].t[:], pA.t[:, 0:128], [pA], [self.ckv_tok[p]])
        krw, rt = self.krw[p], self.rtmp[p]
        self.tt("dve", krw.t[:], pA.t[:, 128:160], self.wk_b.t[:, 64:96], MUL, [pA, self.wk_b], [krw])
        x1, x2, cos, sin = krw.t[:, 0:16], krw.t[:, 16:32], cs.t[:, 0:16], cs.t[:, 16:32]
        self.tt("dve", rt.t[:, 0, :], x1, cos, MUL, [krw, cs], [rt])
        self.tt("dve", rt.t[:, 1, :], x2, sin, MUL, [krw, cs], [rt])
        self.tt("dve", rt.t[:, 2, :], x2, cos, MUL, [krw, cs], [rt])
        self.tt("dve", rt.t[:, 3, :], x1, sin, MUL, [krw, cs], [rt])
        self.tt("dve", krw.t[:, 0:16], rt.t[:, 0, :], rt.t[:, 1, :], SUB, [rt], [krw])
        self.tt("dve", krw.t[:, 16:32], rt.t[:, 2, :], rt.t[:, 3, :], ADD, [rt], [krw])

    def phase_k_B(self, j):
        fw = self.fw
        p = j % 2
        sm, krw, k_tok, ckv_tok = self.smk[j % 3], self.krw[p], self.k_tok[p], self.ckv_tok[p]
        fw.act(sm.t[:, 4:5], sm.t[:, 2:3], AF.Ln, [sm], [sm], scale=1.0 / 128, bias=EPS)
        fw.act(sm.t[:, 5:6], sm.t[:, 4:5], AF.Exp, [sm], [sm], scale=0.5)
        fw.act(self.RCKV.t[:, j:j + 1], sm.t[:, 4:5], AF.Exp, [sm], [self.RCb[j]], scale=-0.5)
        self.ts("dve", k_tok.t[:, 64:96], krw.t[:], sm.t[:, 5:6], MUL, [krw, sm], [k_tok])
        self.tp(self.pT2.t[:, 0:128], ckv_tok.t[:], self.ident_b.t[:], [ckv_tok, self.ident_b], [self.pT2], signal=False)
        self.tp(self.pT2.t[0:96, 128:256], k_tok.t[:], self.ident_b.t[:], [k_tok, self.ident_b], [self.pT2])
        self.cp("dve", self.ckvT_t.t[:, j * 128:(j + 1) * 128], self.pT2.t[:, 0:128], [self.pT2], [self.ckvT[j]])
        self.cp("act", self.KB_t.t[64:96, j * 128:(j + 1) * 128], self.pT2.t[64:96, 128:256], [self.pT2], [self.KBr[j]])

    def phase_k_C(self, j):
        fw = self.fw
        sm = self.smk[j % 3]
        pq = self.pQ[j % 2]
        self.mm(pq.t[:, :], self.ckvT_t.t[:, j * 128:(j + 1) * 128], self.Wkn1.t[:, :], True, True, [self.ckvT[j], self.Wkn1], [pq])
        fw.act(self.sqn.t[:, 0:512], pq.t[:, :], AF.Square, [pq], [self.sqn])
        fw.op("dve", lambda e: e.tensor_reduce(out=self.ssn.t[:, 0:8], in_=self.sqn.t[:, 0:512].rearrange("p (h d) -> p h d", h=8),
                                               axis=AX.X, op=ADD), [self.sqn], [self.ssn])
        self.tt("dve", sm.t[:, 6:7], self.RCKV.t[:, j:j + 1], self.RCKV.t[:, j:j + 1], MUL, [self.RCb[j]], [sm])
        self.ts("dve", self.ssn.t[:, 0:8], self.ssn.t[:, 0:8], sm.t[:, 6:7], MUL, [self.ssn, sm], [self.ssn], s2=sm.t[:, 3:4], op1=ADD)
        self.rsqrt(self.ssn.t[:, 8:16], self.ssn.t[:, 0:8], 1.0 / 96, EPS, self.ssn, self.ssn)
        self.ts("dve", self.ALPHA.t[:, j, :], self.ssn.t[:, 8:16], self.RCKV.t[:, j:j + 1], MUL, [self.ssn, self.RCb[j]], [self.ALb[j]],
                s2=SM_SCALE, op1=MUL)

    def fill_hT(self, e, s, v):
        view = self.hTm_t.t[:, :, e * 128:(e + 1) * 128]
        if s is None:
            self.fw.op("pool", lambda en: en.memset(view, 0.0), [], [self.hTm[e]])
        else:
            self.x_to_hT(s[0], s[1], v, s[2], self.hTm[e], view)

    def macro(self, ext_srcs, q_rope_row0, key_blocks, v, x_res, y_dst, next_srcs=None, next_v=0):
        fw = self.fw
        nblk = len(ext_srcs) - 2
        ntok = nblk * 128
        ext = ntok + 256
        hTt = self.hTm_t.t
        if not self.hT_prefilled:
            for e, s in enumerate(ext_srcs):
                self.fill_hT(e, s, v)
        self.hT_prefilled = False
        self.next_srcs, self.next_v = next_srcs, next_v
        hT_all = self.hTm[:len(ext_srcs)]
        hT_c = self.hTm[1:1 + nblk]
        QTb = Buf(None, "QTb")
        self.QTb = QTb
        for b in range(nblk):
            e = b + 1
            hT = hTt[:, :, e * 128:(e + 1) * 128]
            sm = self.sm[b % 2]
            cs = self.cs[b % 2]
            fw.dma("sp", cs.t[:], self.rope_q.t[q_rope_row0 + b * 128:q_rope_row0 + (b + 1) * 128, :], [], [cs])
            pA = self.pA
            for k in range(8):
                self.mm(pA.t[:, 0:256], hT[:, k, :], self.w_cq.t[:, k, :], k == 0, k == 7, [self.hTm[e], self.w_cq], [pA], signal=(k == 7))
            fw.act(self.sqn.t[:, 0:256], pA.t[:, 0:256], AF.Square, [pA], [self.sqn, sm], accum_out=sm.t[:, 8:9])
            self.rsqrt(sm.t[:, 9:10], sm.t[:, 8:9], 1.0 / 256, EPS, sm, sm)
            fw.act(self.cqn_tok.t[:], pA.t[:, 0:256], AF.Copy, [pA, sm], [self.cqn_tok], scale=sm.t[:, 9:10])
            for c in range(2):
                self.tp(self.pT2.t[:, c * 128:(c + 1) * 128], self.cqn_tok.t[:, c * 128:(c + 1) * 128], self.ident_b.t[:],
                        [self.cqn_tok, self.ident_b], [self.pT2], signal=(c == 1))
            self.cp("dve", self.cqnT.t[:].rearrange("p c t -> p (c t)"), self.pT2.t[:, 0:256], [self.pT2], [self.cqnT])
            for (bk, c0, c1) in ((self.pQ[0], 0, 480), (self.pQ[1], 480, 768)):
                for c in range(2):
                    self.mm(bk.t[:, 0:c1 - c0], self.cqnT.t[:, c, :], self.w_uqb.t[:, c, c0:c1], c == 0, c == 1,
                            [self.cqnT, self.w_uqb], [bk], signal=(c == 1))
                fw.act(self.sqn.t[:, c0:c1], bk.t[:, 0:c1 - c0], AF.Square, [bk], [self.sqn])
            fw.op("dve", lambda en: en.tensor_reduce(out=self.ssn.t[:, 0:8], in_=self.sqn.t[:, 0:768].rearrange("p (h d) -> p h d", h=8),
                                                     axis=AX.X, op=ADD), [self.sqn], [self.ssn])
            self.rsqrt(self.ssn.t[:, 8:16], self.ssn.t[:, 0:8], 1.0 / 96, EPS, self.ssn, self.ssn)
            qf = self.qf
            self.tt("dve", qf.t[:, 0:5, :], self.pQ[0].t[:, 0:480].rearrange("p (h d) -> p h d", h=5),
                    self.ssn.t[:, 8:13].unsqueeze(2).to_broadcast([128, 5, 96]), MUL, [self.pQ[0], self.ssn], [qf])
            self.tt("dve", qf.t[:, 5:8, :], self.pQ[1].t[:, 0:288].rearrange("p (h d) -> p h d", h=3),
                    self.ssn.t[:, 13:16].unsqueeze(2).to_broadcast([128, 3, 96]), MUL, [self.pQ[1], self.ssn], [qf])
            self.tt("dve", qf.t[:, :, :], qf.t[:, :, :], self.wq_b.t[:, :].unsqueeze(1).to_broadcast([128, 8, 96]), MUL, [qf, self.wq_b], [qf])
            qt_, tmp = self.q_tok, self.qtmp
            self.cp("pool", qt_.t[:, :, 0:64], qf.t[:, :, 0:64], [qf], [qt_])
            x1, x2 = qf.t[:, :, 64:80], qf.t[:, :, 80:96]
            cos = cs.t[:, 0:16].unsqueeze(1).to_broadcast([128, 8, 16])
            sin = cs.t[:, 16:32].unsqueeze(1).to_broadcast([128, 8, 16])
            self.tt("dve", tmp.t[:, 0, :, :], x1, cos, MUL, [qf, cs], [tmp])
            self.tt("dve", tmp.t[:, 1, :, :], x2, sin, MUL, [qf, cs], [tmp])
            self.tt("dve", tmp.t[:, 2, :, :], x2, cos, MUL, [qf, cs], [tmp])
            self.tt("dve", tmp.t[:, 3, :, :], x1, sin, MUL, [qf, cs], [tmp])
            self.tt("dve", qt_.t[:, :, 64:80], tmp.t[:, 0, :, :], tmp.t[:, 1, :, :], SUB, [tmp], [qt_])
            self.tt("dve", qt_.t[:, :, 80:96], tmp.t[:, 2, :, :], tmp.t[:, 3, :, :], ADD, [tmp], [qt_])
            for h in range(H):
                self.tp(self.pT2.t[0:96, h * 128:(h + 1) * 128], qt_.t[:, h, :], self.ident_b.t[:], [qt_, self.ident_b], [self.pT2],
                        signal=(h == H - 1))
            self.cp("act", self.QT.t[:, :, b * 128:(b + 1) * 128], self.pT2.t[0:96, :].rearrange("p (h t) -> p h t", h=8), [self.pT2], [QTb])

        if self.stop == "macro_b":
            return
        banks = [self.pQ[0], self.pQ[1], self.pS[0], self.pS[1]]
        self._bk = getattr(self, "_bk", 0)
        self._wc = getattr(self, "_wc", 0)
        w_in_v = self.w_in_bf.t.rearrange("(k p) n -> p k n", p=128)

        def load_chunk(c0):
            wch = self.wch[self._wc % 3]
            self._wc += 1
            fw.dma("sp", wch.t[:], w_in_v[:, :, c0:c0 + 128], [self.w_in_bf], [wch])
            return wch

        def proj(wch, m0, m1, t0, tw, hbufs):
            bk = banks[self._bk % 4]
            self._bk += 1
            for k in range(8):
                self.mm(bk.t[0:m1 - m0, 0:tw], wch.t[:, k, m0:m1], hTt[:, k, t0:t0 + tw], k == 0, k == 7, [wch] + hbufs, [bk], signal=(k == 7))
            return bk

        ext_tiles = [(t0, min(512, ext - t0)) for t0 in range(0, ext, 512)]
        F1, F2, F3 = self.F1, self.F2, self.F3
        for name in ("scin", "scC", "cfg", "cfa"):
            for c in range(2):
                wch = load_chunk(OFF[name] + c * 128)
                for (t0, tw) in ext_tiles:
                    bk = proj(wch, 0, 128, t0, tw, hT_all)
                    if name == "scin":
                        self.cp("act", F1.t[:, c, t0:t0 + tw], bk.t[:, 0:tw], [bk], [F1])
                    elif name == "scC":
                        self.tt("dve", F1.t[:, c, t0:t0 + tw], bk.t[:, 0:tw], F1.t[:, c, t0:t0 + tw], MUL, [bk, F1], [F1])
                    elif name == "cfg":
                        fw.act(F2.t[:, c, t0:t0 + tw], bk.t[:, 0:tw], AF.Sigmoid, [bk], [F2])
                    else:
                        self.tt("dve", F2.t[:, c, t0:t0 + tw], bk.t[:, 0:tw], F2.t[:, c, t0:t0 + tw], MUL, [bk, F2], [F2])
        for name, dst, fn in (("scB", self.SCB, None), ("gB", self.GB, AF.Silu), ("gC", self.GC, AF.Silu)):
            for c in range(2):
                wch = load_chunk(OFF[name] + c * 128)
                bk = proj(wch, 0, 128, 128, ntok, hT_c)
                if fn is None:
                    self.cp("dve", dst.t[:, c, 0:ntok], bk.t[:, 0:ntok], [bk], [dst])
                else:
                    fw.act(dst.t[:, c, 0:ntok], bk.t[:, 0:ntok], fn, [bk], [dst])
        GAb = [Buf(None, "GA%d" % h) for h in range(H)]
        for h in range(H):
            wch = load_chunk(OFF["gA"] + h * 64)
            bk = proj(wch, 0, 128, 128, ntok, hT_c)
            fw.act(self.GA.t[0:64, h, 0:ntok], bk.t[0:64, 0:ntok], AF.Silu, [bk], [GAb[h]])

        self._dg = getattr(self, "_dg", 0)

        def dwconv(src, c, wcol, ntap, off0):
            bk = banks[self._bk % 4]
            self._bk += 1
            for j in range(ntap):
                dg = self.dg[self._dg % 4]
                self._dg += 1
                self.ts("dve", dg.t[:], self.ident_b.t[:], wcol[:, c, j:j + 1], MUL, [self.ident_b, self.scw, self.cfw], [dg])
                self.mm(bk.t[:, 0:ntok], dg.t[:], src.t[:, c, off0 + j:off0 + j + ntok], j == 0, j == ntap - 1, [dg, src], [bk],
                        signal=True)
            return bk

        accs = [self.MU, self.RS]
        for c in range(2):
            bk = dwconv(F1, c, self.scw.t, 3, 127)
            a = accs[c].t[:, 0:ntok]
            self.tt("dve", a, bk.t[:, 0:ntok], self.SCB.t[:, c, 0:ntok], MUL, [bk, self.SCB], [accs[c]])
            self.tt("dve", self.YB.t[:, c, 0:ntok], a, self.GB.t[:, c, 0:ntok], MUL, [accs[c], self.GB], [self.YB])
        for c in range(2):
            bk = dwconv(F2, c, self.cfw.t, 31, 113)
            self.ts("dve", F3.t[:, c, 0:ntok], bk.t[:, 0:ntok], self.cfv.t[:, c, 0:1], ADD, [bk, self.cfv], [F3])
        ZSQ = self.ZSQ
        for c in range(2):
            fw.act(ZSQ.t[:, c, 0:ntok], F3.t[:, c, 0:ntok], AF.Square, [F3], [ZSQ])
        for c in range(2):
            self.mm(self.pS[0].t[:, 0:ntok], self.ones_f.t[:, :], F3.t[:, c, 0:ntok], c == 0, c == 1, [self.ones_f, F3], [self.pS[0]], signal=(c == 1))
        for c in range(2):
            self.mm(self.pS[1].t[:, 0:ntok], self.ones_f.t[:, :], ZSQ.t[:, c, 0:ntok], c == 0, c == 1, [self.ones_f, ZSQ], [self.pS[1]], signal=(c == 1))
        MU, RS = self.MU, self.RS
        musq = self.sqn.t[:, 0:ntok]
        fw.act(MU.t[:, 0:ntok], self.pS[0].t[:, 0:ntok], AF.Copy, [self.pS[0]], [MU], scale=1.0 / 256)
        self.tt("dve", musq, MU.t[:, 0:ntok], MU.t[:, 0:ntok], MUL, [MU], [self.sqn])
        self.stt("dve", RS.t[:, 0:ntok], self.pS[1].t[:, 0:ntok], 1.0 / 256, musq, MUL, SUB, [self.pS[1], self.sqn], [RS])
        self.rsqrt(RS.t[:, 0:ntok], RS.t[:, 0:ntok], 1.0, LN_EPS, RS, RS)
        for c in range(2):
            z = F3.t[:, c, 0:ntok]
            self.tt("dve", z, z, MU.t[:, 0:ntok], SUB, [F3, MU], [F3])
            self.tt("dve", z, z, RS.t[:, 0:ntok], MUL, [F3, RS], [F3])
            fw.act(self.ZA.t[:, c, 0:ntok], z, AF.Silu, [F3, self.cfv], [self.ZA], scale=self.cfv.t[:, c, 1:2], bias=self.cfv.t[:, c, 2:3])
        for oc in range(2):
            bk = banks[self._bk % 4]
            self._bk += 1
            for c in range(2):
                self.mm(bk.t[:, 0:ntok], self.pwb.t[:, c, oc * 128:(oc + 1) * 128], self.ZA.t[:, c, 0:ntok], c == 0, c == 1,
                        [self.pwb, self.ZA], [bk], signal=(c == 1))
            self.stt("dve", self.YC.t[:, oc, 0:ntok], bk.t[:, 0:ntok], self.cfv.t[:, oc, 3:4], self.GC.t[:, oc, 0:ntok], ADD, MUL,
                     [bk, self.cfv, self.GC], [self.YC])

        if self.stop == "macro_g":
            return
        self.attention(ntok, key_blocks, GAb)
        if self.stop == "attn":
            return

        self.out_proj(nblk, v, GAb, x_res, y_dst)

    def build_K_tile(self, h, kt, nkeys):
        k0 = kt * 512
        w = min(512, nkeys - k0)
        bk = self.pQ[0]
        blks = [self.ckvT[j] for j in range(k0 // 128, (k0 + w) // 128)]
        self.mm(bk.t[:, 0:w], self.Wkn2.t[:, h * 64:h * 64 + 128], self.ckvT_t.t[:, k0:k0 + w], True, True, [self.Wkn2] + blks, [bk])
        self.cp("dve", self.KB_t.t[0:64, k0:k0 + w], bk.t[0:64, 0:w], [bk], [self.KBn[kt]])

    def build_V_group(self, hp, g, nkb):
        g0 = g * 4
        ng = min(4, nkb - g0)
        for q in range(ng):
            j = g0 + q
            self.mm(self.pA.t[:, q * 128:(q + 1) * 128], self.ckvT_t.t[:, j * 128:(j + 1) * 128], self.Wv.t[:, hp * 64:hp * 64 + 128],
                    True, True, [self.ckvT[j], self.Wv], [self.pA], signal=(q == ng - 1))
        self.tt("dve", self.V2.t[:, g0:g0 + ng, :, 0:64],
                self.pA.t[:, 0:ng * 128].rearrange("p (g t d) -> p g t d", g=ng, t=2),
                self.RCKV.t[:, g0:g0 + ng].unsqueeze(2).unsqueeze(3).to_broadcast([128, ng, 2, 64]), MUL,
                [self.pA] + [self.RCb[j] for j in range(g0, g0 + ng)], [self.V2g[g]])

    def attention(self, ntok, key_blocks, GAb):
        fw = self.fw
        nkb = len(key_blocks)
        assert key_blocks == list(range(nkb))
        nkeys = nkb * 128
        ngrp = (nkb + 3) // 4
        self._pt = getattr(self, "_pt", 0)
        self._ps = getattr(self, "_ps", 0)
        self._po = getattr(self, "_po", 0)
        qtiles = [(q0, min(512, ntok - q0)) for q0 in range(0, ntok, 512)]
        import os
        INTER = os.environ.get("KV_INTERLEAVE", "1") == "1"
        if self.kv_ready != nkb and INTER:
            for kt in range(ngrp):
                self.build_K_tile(0, kt, nkeys)
                self.build_V_group(0, kt, nkb)
        self.kv_ready = None
        pS3 = [self.pS[0], self.pS[1], self.pQ[1]]
        pend = []

        def make_epilogue(h, q0, w, pO):
            def epi():
                ONR = self.ONR
                fw.op("dve", lambda e: e.reciprocal(out=ONR.t[64:65, 0:w], in_=pO.t[64:65, 0:w]), [pO], [ONR])
                self.mm(self.pA.t[0:64, 0:w], self.E64.t[:, :], ONR.t[:, 0:w], True, True, [self.E64, ONR], [self.pA])
                self.cp("dve", self.rb.t[:, 0:w], self.pA.t[0:64, 0:w], [self.pA], [self.rb])
                self.tt("dve", ONR.t[0:64, 0:w], pO.t[0:64, 0:w], self.rb.t[:, 0:w], MUL, [pO, self.rb], [ONR])
                self.tt("dve", self.GA.t[0:64, h, q0:q0 + w], ONR.t[0:64, 0:w], self.GA.t[0:64, h, q0:q0 + w], MUL, [ONR, GAb[h]], [GAb[h]])
            return epi

        for h in range(H):
            nh = (h + 1) % H
            if not INTER:
                for kt in range(ngrp):
                    self.build_K_tile(h, kt, nkeys)
                    if h % 2 == 0:
                        self.build_V_group(h, kt, nkb)
            for qi, (q0, w) in enumerate(qtiles):
                prefetch = (qi == len(qtiles) - 1)
                pO = self.pO[self._po % 2]
                self._po += 1

                def s_mm(i):
                    j = key_blocks[i]
                    ps_ = pS3[(self._ps + i) % 3]
                    self.mm(ps_.t[:, 0:w], self.KB_t.t[0:96, j * 128:(j + 1) * 128], self.QT.t[0:96, h, q0:q0 + w], True, True,
                            [self.KBn[j // 4], self.KBr[j], self.QTb], [ps_])
                s_mm(0)
                if nkb > 1:
                    s_mm(1)
                for i in range(nkb):
                    j = key_blocks[i]
                    if i + 2 < nkb:
                        s_mm(i + 2)
                    ps_ = pS3[(self._ps + i) % 3]
                    pt = self.PT[self._pt % 4]
                    self._pt += 1
                    fw.act(pt.t[:, 0:w], ps_.t[:, 0:w], AF.Exp, [ps_, self.ALb[j]], [pt], scale=self.ALPHA.t[:, j, h:h + 1])
                    last_i = (i == nkb - 1)
                    self.mm(pO.t[0:65, 0:w], self.V2.t[:, j, h % 2, :], pt.t[:, 0:w], i == 0, last_i, [self.V2g[j // 4], pt], [pO],
                            signal=last_i)
                    if INTER and prefetch and (i % 4 == 3 or last_i):
                        kt = i // 4
                        self.build_K_tile(nh, kt, nkeys)
                        if h % 2 == 1:
                            self.build_V_group((h + 1) % H, kt, nkb)
                    if pend and i == min(6, nkb - 1):
                        pend.pop(0)()
                self._ps += nkb
                pend.append(make_epilogue(h, q0, w, pO))
            if self.next_srcs is not None and h < len(self.next_srcs):
                self.in_attn = False
                self.fill_hT(h, self.next_srcs[h], self.next_v)
                self.in_attn = False
        while pend:
            pend.pop(0)()
        if self.next_srcs is not None:
            for e in range(H, len(self.next_srcs)):
                self.fill_hT(e, self.next_srcs[e], self.next_v)
            self.hT_prefilled = True
        self.kv_ready = nkb if INTER else None

    def out_proj(self, nblk, v, GAb, x_res, y_dst):
        fw = self.fw
        self._wo = getattr(self, "_wo", 0)
        chunks = [("a", h) for h in range(H)] + [("b", 0), ("b", 1), ("c", 0), ("c", 1)]
        for p0 in range(0, nblk, 2):
            blks = list(range(p0, min(p0 + 2, nblk)))
            if p0 == 0:
                bankset = {blks[0]: (self.pS[0], self.pS[1])}
                if len(blks) > 1:
                    bankset[blks[1]] = (self.pQ[0], self.pQ[1])
            else:
                bankset = {blks[0]: (self.pO[0], self.pO[1])}
                if len(blks) > 1:
                    bankset[blks[1]] = (self.pA, self.pS[0])
            for ci, (kind, idx) in enumerate(chunks):
                wo = self.woch[self._wo % 3]
                self._wo += 1
                if kind == "a":
                    K = 64
                    r0 = idx * 64
                    ybuf, yv = GAb[idx], (lambda b: self.GA.t[0:64, idx, b * 128:(b + 1) * 128])
                elif kind == "b":
                    K = 128
                    r0 = 512 + idx * 128
                    ybuf, yv = self.YB, (lambda b: self.YB.t[:, idx, b * 128:(b + 1) * 128])
                else:
                    K = 128
                    r0 = 768 + idx * 128
                    ybuf, yv = self.YC, (lambda b: self.YC.t[:, idx, b * 128:(b + 1) * 128])
                fw.dma("sp", wo.t[0:K, :], self.w_out_bf.t[r0:r0 + K, :], [self.w_out_bf], [wo])
                for b in blks:
                    for ct in range(2):
                        bk = bankset[b][ct]
                        self.mm(bk.t[:, :], yv(b), wo.t[0:K, ct * 512:(ct + 1) * 512], ci == 0, ci == len(chunks) - 1, [ybuf, wo], [bk],
                                signal=(b == blks[-1] and ct == 1))
            for b in blks:
                i = self.xi % 2
                self.xi += 1
                xt = self.xt[i]
                rap, rbuf = x_res(b)
                fw.dma("sp", xt.t[:], rap, [rbuf], [xt])
                for ct in range(2):
                    bk = bankset[b][ct]
                    tmp = self.sqn.t[:, 0:512]
                    self.tt("dve", tmp, bk.t[:, :], self.GATE[v].t[:, ct * 512:(ct + 1) * 512], MUL, [bk, self.GATE[v]], [self.sqn])
                    self.tt("pool", xt.t[:, ct * 512:(ct + 1) * 512], xt.t[:, ct * 512:(ct + 1) * 512], tmp, ADD, [xt, self.sqn], [xt])
                yap, ybuf_ = y_dst(b)
                fw.dma("sp", yap, xt.t[:], [xt], [ybuf_])

    def _emit_main(self, l, last, xfull, xown, cx, yd, ycd):
        fw = self.fw
        S, OWN, NKB = self.S, self.OWN, self.NKB
        fw.op("pool", lambda e: e.memset(self.ONR.t[:], 0.0), [], [self.ONR])
        for p in range(2):
            fw.op("pool", lambda e: e.memset(self.k_tok[p].t[:, 0:64], 0.0), [], [self.k_tok[p]])
        self.kv_ready = None
        self.hT_prefilled = False
        def ksrc(j):
            return cx(j * 128) + (1,) if j < 2 else xfull((j - 2) * 128) + (0,)
        for j in range(NKB + 1):
            if j < NKB:
                ap, bf, v = ksrc(j)
                self.phase_k_A(j, ap, bf, v)
            if j >= 1:
                self.phase_k_B(j - 1)
                self.phase_k_C(j - 1)
        nb = TM // 128
        lat_srcs = []
        for m in range(OWN // TM):
            srcs = []
            for e in range(nb + 2):
                ob = m * nb - 1 + e
                if ob < 0:
                    srcs.append(xfull(S // 2 - 128) + (self.hm.t[:, 0:1],))
                elif ob >= OWN // 128:
                    srcs.append(xfull(S // 2) + (self.hm.t[:, 1:2],))
                else:
                    srcs.append(xown(ob * 128) + (None,))
            lat_srcs.append(srcs)
        if not last:
            srcs = [None] + [cx(b * 128) + (None,) for b in range(2)] + [None]
            self.macro(srcs, 0, [0, 1], 1, lambda b: cx(b * 128), lambda b: ycd(b * 128), next_srcs=lat_srcs[0], next_v=0)
        nm = OWN // TM
        for m in range(nm):
            self.macro(lat_srcs[m], CTX + m * TM, list(range(NKB)), 0,
                       (lambda m_: (lambda b: xown(m_ * TM + b * 128)))(m), (lambda m_: (lambda b: yd(m_ * TM + b * 128)))(m),
                       next_srcs=(lat_srcs[m + 1] if m + 1 < nm else None), next_v=0)


_PROG_CACHE = {}


def _get_prog(S, layers, fused=False):
    key = (S, tuple(layers), fused)
    if key not in _PROG_CACHE:
        _PROG_CACHE[key] = Prog(S, list(layers), fused=fused)
    return _PROG_CACHE[key]


def _rope_tables(S):
    rows = S // 64
    pos_r = np.repeat(np.arange(rows, dtype=np.float32), 64)
    pos_c = np.tile(np.arange(64, dtype=np.float32), rows)
    inv = (np.float32(10000.0) ** (-np.arange(8, dtype=np.float32) / np.float32(8))).astype(np.float32)
    ang = np.concatenate([pos_r[:, None] * inv, pos_c[:, None] * inv], axis=-1).astype(np.float32)
    lat = np.concatenate([np.cos(ang), np.sin(ang)], axis=-1).astype(np.float32)
    ctxr = np.concatenate([np.ones((CTX, 16), np.float32), np.zeros((CTX, 16), np.float32)], axis=-1)
    return lat, ctxr


def _static_inputs(inp):
    f = lambda a: np.ascontiguousarray(np.asarray(a, dtype=np.float32))
    L = inp["norm_w"].shape[0]
    col = lambda a, c: np.ascontiguousarray(np.asarray(a, np.float32).reshape(L, c, 128).transpose(0, 2, 1))
    sel = np.zeros((2, 256), np.float32)
    sel[0, 0:128] = 1.0
    sel[1, 128:256] = 1.0
    scw = np.asarray(inp["sc_conv_w"], np.float32).reshape(L, 3, 2, 128).transpose(0, 3, 2, 1)
    cfw = np.asarray(inp["cf_conv_w"], np.float32).reshape(L, 31, 2, 128).transpose(0, 3, 2, 1)
    cfv = np.stack([col(inp["cf_conv_b"], 2), col(inp["cf_ln_w"], 2), col(inp["cf_ln_b"], 2), col(inp["cf_pw_b"], 2)], axis=-1)
    return dict(
        ident=np.eye(128, dtype=np.float32), sel=sel,
        norm_w=f(inp["norm_w"]), w_mod=f(inp["w_mod"]), b_mod=f(inp["b_mod"]), w_in=f(inp["w_in"]),
        qnw_col=col(inp["q_norm_w"], 2), w_uq=f(inp["w_uq"]), kvnw_col=col(inp["kv_norm_w"], 1), w_ukv=f(inp["w_ukv"]),
        qhw=f(inp["q_head_norm_w"]), khw=f(inp["k_head_norm_w"]),
        scw_col=np.ascontiguousarray(scw), cfw_col=np.ascontiguousarray(cfw), cfv_col=np.ascontiguousarray(cfv),
        cf_pw=f(inp["cf_pw_w"]), w_out=f(inp["w_out"]),
    )


def _core_inputs(static, x, ctx, c, c_ctx, S):
    lat, ctxr = _rope_tables(S)
    OWN = S // 2
    maps = []
    for core in range(8):
        b, half = core // 2, core % 2
        cv = np.stack([np.asarray(c[b], np.float32), np.asarray(c_ctx, np.float32)], axis=0)
        csil = np.ascontiguousarray(cv.reshape(2, 8, 128).transpose(2, 1, 0))
        hm = np.zeros((128, 2), np.float32)
        hm[:, 0] = 1.0 if half == 1 else 0.0
        hm[:, 1] = 1.0 if half == 0 else 0.0
        m = dict(static)
        m.update(
            x_full=np.ascontiguousarray(x[b]), x_own=np.ascontiguousarray(x[b, half * OWN:(half + 1) * OWN]),
            ctx_in=np.ascontiguousarray(ctx[b]), csil=csil,
            rope_k=np.ascontiguousarray(np.concatenate([ctxr, lat], axis=0)),
            rope_q=np.ascontiguousarray(np.concatenate([ctxr, lat[half * OWN:(half + 1) * OWN]], axis=0)),
            hmask=hm,
        )
        maps.append(m)
    return maps


def kernel(x, c, ctx, c_ctx, norm_w, w_mod, b_mod, w_in, q_norm_w, w_uq, kv_norm_w, w_ukv, q_head_norm_w, k_head_norm_w,
           sc_conv_w, cf_conv_w, cf_conv_b, cf_ln_w, cf_ln_b, cf_pw_w, cf_pw_b, w_out):
    inp = dict(norm_w=norm_w, w_mod=w_mod, b_mod=b_mod, w_in=w_in, q_norm_w=q_norm_w, w_uq=w_uq, kv_norm_w=kv_norm_w, w_ukv=w_ukv,
               q_head_norm_w=q_head_norm_w, k_head_norm_w=k_head_norm_w, sc_conv_w=sc_conv_w, cf_conv_w=cf_conv_w,
               cf_conv_b=cf_conv_b, cf_ln_w=cf_ln_w, cf_ln_b=cf_ln_b, cf_pw_w=cf_pw_w, cf_pw_b=cf_pw_b, w_out=w_out)
    x = np.asarray(x, np.float32)
    ctx = np.asarray(ctx, np.float32)
    c = np.asarray(c, np.float32)
    c_ctx = np.asarray(c_ctx, np.float32)
    B, S, _ = x.shape
    OWN = S // 2
    static = _static_inputs(inp)
    L = static["norm_w"].shape[0]
    prog = _get_prog(S, list(range(L)), fused=True)
    maps = _core_inputs(static, x, ctx, c, c_ctx, S)
    res = run_bass_kernel_spmd(prog.nc, maps, core_ids=list(range(8)))
    out = np.empty_like(x)
    for core in range(8):
        b, half = core // 2, core % 2
        out[b, half * OWN:(half + 1) * OWN] = res.results[core]["y"]
    return out
```
